# Optimizing a Trainium2 kernel written in Bass

```python
import jax, jax.numpy as jnp
from jax import lax
import numpy as np

D_MODEL = 1024
BATCH = 2
SEQ = 8192
DEPTH = 1
DEC_BATCH = 128
DEC_SEQ = 8
PAST_LEN = 16384
PAGE_SIZE = 128

ATT_HEADS = 8
ATT_KV_HEADS = 2
ATT_GROUP = ATT_HEADS // ATT_KV_HEADS
ATT_HEAD_DIM = 64
ATT_WIDTH = ATT_HEADS * ATT_HEAD_DIM
KV_WIDTH = ATT_KV_HEADS * ATT_HEAD_DIM
WINDOW = 128
ATT_BLOCK = 128
ROPE_THETA = 10000.0
GLA_HEADS = 4
GLA_WIDTH = D_MODEL // 2
GLA_DV = GLA_WIDTH // GLA_HEADS
GLA_KEY_WIDTH = GLA_WIDTH // 2
GLA_DK = GLA_KEY_WIDTH // GLA_HEADS
GLA_GATE_RANK = 16
GLA_GATE_TAU = 16.0
GLA_CHUNK = 16
EPS = 1e-6
NEG = -1e30
IN_COLS = 2 * ATT_WIDTH + 2 * KV_WIDTH + 2 * GLA_KEY_WIDTH + 2 * GLA_WIDTH + GLA_GATE_RANK + 2 * D_MODEL

kernel_name = "hybrid_swa_sink_gla_parallel_adaln_step"


def rms_norm(x, g):
    xf = x.astype(jnp.float32)
    y = xf * lax.rsqrt(jnp.mean(xf * xf, axis=-1, keepdims=True) + EPS)
    return (y * g.astype(jnp.float32)).astype(x.dtype)


def rope(x, pos):
    half = ATT_HEAD_DIM // 2
    inv = 1.0 / (ROPE_THETA ** (jnp.arange(half, dtype=jnp.float32) / half))
    ang = pos.astype(jnp.float32)[:, None] * inv[None, :]
    cos = jnp.cos(ang)[:, None, :]
    sin = jnp.sin(ang)[:, None, :]
    xf = x.astype(jnp.float32)
    x1, x2 = xf[..., :half], xf[..., half:]
    return jnp.concatenate([x1 * cos - x2 * sin, x2 * cos + x1 * sin], axis=-1).astype(x.dtype)


def split_proj(p):
    sizes = (ATT_WIDTH, KV_WIDTH, KV_WIDTH, ATT_WIDTH, GLA_KEY_WIDTH, GLA_KEY_WIDTH,
             GLA_WIDTH, GLA_GATE_RANK, GLA_WIDTH, D_MODEL, D_MODEL)
    idx = np.cumsum(sizes)[:-1].tolist()
    return jnp.split(p, idx, axis=-1)


def sink_attention(q, k, v, mask, sinks):
    s = jnp.einsum('...qhgd,...khd->...hgqk', q, k).astype(jnp.float32) * (ATT_HEAD_DIM ** -0.5)
    s = jnp.where(mask, s, NEG)
    sink = jnp.broadcast_to(sinks.astype(jnp.float32).reshape(ATT_KV_HEADS, ATT_GROUP, 1, 1),
                            s.shape[:-1] + (1,))
    p = jax.nn.softmax(jnp.concatenate([s, sink], axis=-1), axis=-1)[..., :-1]
    return jnp.einsum('...hgqk,...khd->...qhgd', p.astype(v.dtype), v)


def swa_prompt(q, k, v, sinks):
    B, S = q.shape[:2]
    nb = S // ATT_BLOCK
    qb = q.reshape(B, nb, ATT_BLOCK, ATT_KV_HEADS, ATT_GROUP, ATT_HEAD_DIM)
    kb = k.reshape(B, nb, ATT_BLOCK, ATT_KV_HEADS, ATT_HEAD_DIM)
    vb = v.reshape(B, nb, ATT_BLOCK, ATT_KV_HEADS, ATT_HEAD_DIM)
    pad = ((0, 0), (1, 0), (0, 0), (0, 0), (0, 0))
    kk = jnp.concatenate([jnp.pad(kb, pad)[:, :-1], kb], axis=2)
    vv = jnp.concatenate([jnp.pad(vb, pad)[:, :-1], vb], axis=2)
    qpos = jnp.arange(ATT_BLOCK)[:, None] + ATT_BLOCK
    kpos = jnp.arange(2 * ATT_BLOCK)[None, :]
    d = qpos - kpos
    band = (d >= 0) & (d <= WINDOW)
    blk = jnp.arange(nb)[:, None, None]
    mask = band[None] & ((kpos[None] >= ATT_BLOCK) | (blk > 0))
    o = sink_attention(qb, kk, vv, mask[None, :, None, None], sinks)
    return o.reshape(B, S, ATT_HEADS, ATT_HEAD_DIM)


def swa_sample(q, k_new, v_new, k_cache, v_cache, sinks):
    B, L = q.shape[:2]
    W = k_cache.shape[1]
    kk = jnp.concatenate([k_cache.astype(k_new.dtype), k_new], axis=1)
    vv = jnp.concatenate([v_cache.astype(v_new.dtype), v_new], axis=1)
    d = (W + jnp.arange(L))[:, None] - jnp.arange(W + L)[None, :]
    mask = (d >= 0) & (d <= WINDOW)
    qg = q.reshape(B, L, ATT_KV_HEADS, ATT_GROUP, ATT_HEAD_DIM)
    o = sink_attention(qg, kk, vv, mask, sinks).reshape(B, L, ATT_HEADS, ATT_HEAD_DIM)
    return o, kk[:, -W:], vv[:, -W:]


def gla_recurrence(q, k, v, log_a, S0):
    B, L = q.shape[:2]
    C = GLA_CHUNK
    pad = (-L) % C
    nc = (L + pad) // C

    def chunks(t):
        t = jnp.pad(t.astype(jnp.float32), ((0, 0), (0, pad), (0, 0), (0, 0)))
        return t.reshape(B, nc, C, t.shape[2], t.shape[3]).transpose(1, 0, 2, 3, 4)

    tri = (jnp.arange(C)[:, None] >= jnp.arange(C)[None, :])[None, :, :, None, None]

    def step(S, inp):
        qc, kc, vc, ac = inp
        b = jnp.cumsum(ac, axis=1)
        o_inter = jnp.einsum('bthk,bhkv->bthv', qc * jnp.exp(b), S)
        diff = jnp.where(tri, b[:, :, None] - b[:, None, :], NEG)
        A = jnp.einsum('bthk,bshk,btshk->bhts', qc, kc, jnp.exp(diff))
        o_intra = jnp.einsum('bhts,bshv->bthv', A, vc)
        b_last = b[:, -1]
        k_dec = kc * jnp.exp(b_last[:, None] - b)
        S_new = jnp.exp(b_last)[..., None] * S + jnp.einsum('bshk,bshv->bhkv', k_dec, vc)
        return S_new, o_inter + o_intra

    S_fin, o = lax.scan(step, S0.astype(jnp.float32), (chunks(q), chunks(k), chunks(v), chunks(log_a)))
    o = o.transpose(1, 0, 2, 3, 4).reshape(B, nc * C, GLA_HEADS, GLA_DV)[:, :L]
    return o, S_fin


def trunk_layer(x, c, pos0, win_k, win_v, gla_s, norm_g, w_ada, b_ada, w_in, q_norm_g, k_norm_g,
                attn_sinks, w_gla_gate, b_gla_gate, gla_norm_g, w_branch_att, w_branch_gla, w_out):
    B, L, _ = x.shape
    mod = jnp.einsum('bd,de->be', jax.nn.silu(c), w_ada) + b_ada
    shift, scale, gate = jnp.split(mod, 3, axis=-1)
    h = rms_norm(x, norm_g) * (1.0 + scale[:, None]) + shift[:, None]
    proj = jnp.einsum('bld,de->ble', h, w_in)
    q_a, k_a, v_a, z_a, q_g, k_g, v_g, lr_g, z_g, m_a, m_g = split_proj(proj)

    pos = pos0 + jnp.arange(L, dtype=jnp.int32)
    q_a = rope(rms_norm(q_a.reshape(B, L, ATT_HEADS, ATT_HEAD_DIM), q_norm_g), pos)
    k_a = rope(rms_norm(k_a.reshape(B, L, ATT_KV_HEADS, ATT_HEAD_DIM), k_norm_g), pos)
    v_a = v_a.reshape(B, L, ATT_KV_HEADS, ATT_HEAD_DIM)
    if win_k is None:
        o_a = swa_prompt(q_a, k_a, v_a, attn_sinks)
        new_k, new_v = k_a[:, -WINDOW:], v_a[:, -WINDOW:]
    else:
        o_a, new_k, new_v = swa_sample(q_a, k_a, v_a, win_k, win_v, attn_sinks)
    o_a = o_a.reshape(B, L, ATT_WIDTH) * jax.nn.silu(z_a)

    log_a = jax.nn.log_sigmoid((jnp.einsum('blr,rk->blk', lr_g, w_gla_gate) + b_gla_gate).astype(jnp.float32)) / GLA_GATE_TAU
    q_g = q_g.reshape(B, L, GLA_HEADS, GLA_DK) * (GLA_DK ** -0.5)
    k_g = k_g.reshape(B, L, GLA_HEADS, GLA_DK)
    v_g = v_g.reshape(B, L, GLA_HEADS, GLA_DV)
    o_g, new_s = gla_recurrence(q_g, k_g, v_g, log_a.reshape(B, L, GLA_HEADS, GLA_DK), gla_s)
    o_g = rms_norm(o_g, gla_norm_g).astype(x.dtype).reshape(B, L, GLA_WIDTH) * jax.nn.silu(z_g)

    merged = (jax.nn.sigmoid(m_a) * jnp.einsum('ble,ed->bld', o_a, w_branch_att)
              + jax.nn.sigmoid(m_g) * jnp.einsum('ble,ed->bld', o_g, w_branch_gla))
    y = x + gate[:, None] * jnp.einsum('bld,de->ble', merged, w_out)
    return y, new_k, new_v, new_s


def setup_inputs(seed: int = 0) -> dict:
    key = jax.random.key(seed)
    ks = jax.random.split(key, 24)
    f32 = jnp.float32
    win = min(WINDOW, PAST_LEN)

    def nrm(k, shape, s=1.0):
        return jax.random.normal(k, shape, f32) * s

    return {
        'x_prompt': nrm(ks[0], (BATCH, SEQ, D_MODEL)),
        'x_sample': nrm(ks[1], (DEC_BATCH, DEC_SEQ, D_MODEL)),
        'cache_win_k': nrm(ks[2], (DEPTH, DEC_BATCH, win, ATT_KV_HEADS, ATT_HEAD_DIM)),
        'cache_win_v': nrm(ks[3], (DEPTH, DEC_BATCH, win, ATT_KV_HEADS, ATT_HEAD_DIM)),
        'state_gla': nrm(ks[4], (DEPTH, DEC_BATCH, GLA_HEADS, GLA_DK, GLA_DV)),
        'c_prompt': nrm(ks[5], (BATCH, D_MODEL)),
        'c_sample': nrm(ks[6], (DEC_BATCH, D_MODEL)),
        'norm_g': 1.0 + nrm(ks[7], (DEPTH, D_MODEL), 0.02),
        'w_ada': nrm(ks[8], (DEPTH, D_MODEL, 3 * D_MODEL), D_MODEL ** -0.5),
        'b_ada': nrm(ks[9], (DEPTH, 3 * D_MODEL), 0.02),
        'w_in': nrm(ks[10], (DEPTH, D_MODEL, IN_COLS), D_MODEL ** -0.5),
        'q_norm_g': 1.0 + nrm(ks[11], (DEPTH, ATT_HEAD_DIM), 0.02),
        'k_norm_g': 1.0 + nrm(ks[12], (DEPTH, ATT_HEAD_DIM), 0.02),
        'attn_sinks': nrm(ks[13], (DEPTH, ATT_HEADS), 0.5),
        'w_gla_gate': nrm(ks[14], (DEPTH, GLA_GATE_RANK, GLA_KEY_WIDTH), GLA_GATE_RANK ** -0.5),
        'b_gla_gate': nrm(ks[15], (DEPTH, GLA_KEY_WIDTH), 0.02),
        'gla_norm_g': 1.0 + nrm(ks[16], (DEPTH, GLA_DV), 0.02),
        'w_branch_att': nrm(ks[17], (DEPTH, ATT_WIDTH, D_MODEL), ATT_WIDTH ** -0.5),
        'w_branch_gla': nrm(ks[18], (DEPTH, GLA_WIDTH, D_MODEL), GLA_WIDTH ** -0.5),
        'w_out': nrm(ks[19], (DEPTH, D_MODEL, D_MODEL), D_MODEL ** -0.5),
    }


def reference(x_prompt, x_sample, cache_win_k, cache_win_v, state_gla, c_prompt, c_sample,
              norm_g, w_ada, b_ada, w_in, q_norm_g, k_norm_g, attn_sinks, w_gla_gate, b_gla_gate,
              gla_norm_g, w_branch_att, w_branch_gla, w_out):
    y_prompt, y_sample = x_prompt, x_sample
    pk, pv, ps, sk, sv, ss = [], [], [], [], [], []
    for l in range(DEPTH):
        weights = (norm_g[l], w_ada[l], b_ada[l], w_in[l], q_norm_g[l], k_norm_g[l], attn_sinks[l],
                   w_gla_gate[l], b_gla_gate[l], gla_norm_g[l], w_branch_att[l], w_branch_gla[l], w_out[l])
        s0 = jnp.zeros((BATCH, GLA_HEADS, GLA_DK, GLA_DV), jnp.float32)
        y_prompt, k1, v1, s1 = trunk_layer(y_prompt, c_prompt, 0, None, None, s0, *weights)
        y_sample, k2, v2, s2 = trunk_layer(y_sample, c_sample, PAST_LEN, cache_win_k[l], cache_win_v[l],
                                           state_gla[l], *weights)
        pk.append(k1); pv.append(v1); ps.append(s1)
        sk.append(k2); sv.append(v2); ss.append(s2)
    prompt_win_k = jnp.stack(pk)
    prompt_win_v = jnp.stack(pv)
    prompt_gla_state = jnp.stack(ps)
    sample_win_k = jnp.stack(sk)
    sample_win_v = jnp.stack(sv)
    sample_gla_state = jnp.stack(ss)
    return (y_prompt, y_sample, prompt_win_k, prompt_win_v, prompt_gla_state, sample_win_k, sample_win_v, sample_gla_state)
```

```python
import contextlib
import numpy as np
import concourse.bass as bass
import concourse.mybir as mybir
from concourse.bass_utils import run_bass_kernel_spmd

F32 = mybir.dt.float32
BF16 = mybir.dt.bfloat16
AF = mybir.ActivationFunctionType
ALU = mybir.AluOpType
AX = mybir.AxisListType

ENGS = ("pe", "act", "dve", "pool", "sp")
NCORES = 8
D = 1024
KC = 8
NMAIN = 16
NPRE = 48
INC = 4880
QA, KA, VA, ZA, QG, KG, VG, LR, ZG, MA, MG = 0, 512, 640, 768, 1280, 1536, 1792, 2304, 2320, 2832, 3856
EPS = 1e-6


import types


def _freeze(fn):
    if fn is None or fn.__closure__ is None:
        return fn
    cells = []
    for c in fn.__closure__:
        try:
            cells.append(types.CellType(c.cell_contents))
        except ValueError:
            cells.append(c)
    return types.FunctionType(fn.__code__, fn.__globals__, fn.__name__, fn.__defaults__, tuple(cells))


class Buf:
    def __init__(self, name, excl=False):
        self.name = name
        self.excl = excl
        self.w = None
        self.r = {}


class Sched:
    def __init__(self, nc):
        self.nc = nc
        self.ops = {e: [] for e in ENGS}
        self.cnt = {}
        self.seen = {e: {} for e in ENGS}
        self.sems = {}
        for e in ENGS:
            self.cnt["e_" + e] = 0

    def dma_sem(self, name):
        k = "d_" + name
        self.cnt[k] = 0
        return k

    def _need(self, eng, ev, waits):
        if ev is None:
            return
        k, v = ev[0], ev[1]
        if k.startswith("d_"):
            v = self.cnt[k]
        if self.seen[eng].get(k, 0) >= v:
            return
        waits[k] = max(waits.get(k, 0), v)

    def op(self, eng, fn, reads=(), writes=(), dsem=None):
        waits = {}
        is_dma = dsem is not None
        for b in reads:
            if b.w is not None:
                self._need(eng, b.w, waits)
            if b.excl:
                for e2, ev in b.r.items():
                    if e2 != eng or is_dma:
                        self._need(eng, ev, waits)
        for b in writes:
            if b.w is not None and (b.w[2] != eng or is_dma or b.w[0].startswith("d_") or eng != "pe"):
                self._need(eng, b.w, waits)
            for e2, ev in b.r.items():
                if e2 != eng or is_dma or ev[0].startswith("d_") or eng != "pe":
                    self._need(eng, ev, waits)
        for k, v in waits.items():
            self.seen[eng][k] = v
        if is_dma:
            self.cnt[dsem] += 16
            ev = (dsem, self.cnt[dsem], eng)
            inc = (dsem, 16)
        else:
            k = "e_" + eng
            self.cnt[k] += 1
            ev = (k, self.cnt[k], eng)
            inc = (k, 1)
        for b in reads:
            key = eng if not is_dma else "dma_" + dsem
            b.r[key] = (ev[0], ev[1])
        for b in writes:
            b.w = ev
            b.r = {}
        self.ops[eng].append((list(waits.items()), _freeze(fn), inc))
        return ev

    def wait_all(self, eng, exclude=()):
        waits = []
        for k, v in self.cnt.items():
            if k in exclude:
                continue
            if v > 0 and k != "e_" + eng and self.seen[eng].get(k, 0) < v:
                waits.append((k, v))
                self.seen[eng][k] = v
        self.ops[eng].append((waits, None, None))

    def emit(self):
        nc = self.nc
        with contextlib.ExitStack() as st:
            for k in self.cnt:
                self.sems[k] = st.enter_context(nc.semaphore(k))
            block = st.enter_context(nc.Block())
            handles = {"pe": block.tensor, "act": block.scalar, "dve": block.vector,
                       "pool": block.gpsimd, "sp": block.sync}
            sems = self.sems
            for e in ENGS:
                ops = self.ops[e]

                def body(eh, ops=ops):
                    for waits, fn, inc in ops:
                        for k, v in waits:
                            eh.wait_ge(sems[k], v)
                        if fn is not None:
                            ins = fn(eh)
                            ins.then_inc(sems[inc[0]], inc[1])
                handles[e](body)


class T:
    def __init__(self, t, name, excl=False):
        self.t = t
        self.b = Buf(name, excl)

    def __getitem__(self, k):
        return self.t[k]


def bc_ap(ap, dims):
    return bass.AP(ap.tensor, ap.offset, [list(ap.ap[0])] + [list(d) for d in dims])


class _Stop(Exception):
    pass


def build_nc(n_pre=NPRE, n_main=NMAIN, do_sample=True, stage=None, debug=False):
    def chk(n):
        if stage == n:
            raise _Stop()
    nc = bass.Bass("TRN2", target_bir_lowering=False)
    S = Sched(nc)
    es = contextlib.ExitStack()

    def din(name, shape):
        return nc.dram_tensor(name, list(shape), F32, kind="ExternalInput").ap()

    def dout(name, shape):
        return nc.dram_tensor(name, list(shape), F32, kind="ExternalOutput").ap()

    xm = din("xm", [NMAIN, 128, D]); xp = din("xp", [NPRE, 128, D]); xs_d = din("xs", [128, D])
    cvec = din("cvec", [17, D]); flags_d = din("flags", [128, 64]); rope_d = din("rope", [18, 128, 64])
    ck_d = din("ck", [16, 128, 128]); cv_d = din("cv", [16, 128, 128]); st0_d = din("st0", [16, 4, 64, 128])
    normg_d = din("normg", [128, 8]); normrow_d = din("normrow", [1, D])
    wada_d = din("w_ada", [D, 3072]); bada_d = din("b_ada", [1, 3072])
    win_d = din("w_in", [D, INC]); gqk_d = din("gqk", [128, 640]); sinks_d = din("sinks", [1, 1024])
    wgate_d = din("w_gla_gate", [16, 256]); bgate_d = din("b_gla_gate", [1, 256]); gdv_d = din("gdv", [128, 1])
    wba_d = din("w_branch_att", [512, D]); wbg_d = din("w_branch_gla", [512, D]); wout_d = din("w_out", [D, D])
    cU = din("cU", [128, 128]); cL = din("cL", [128, 128]); cUb = din("cUb", [128, 128])
    cUm1 = din("cUm1", [128, 128]); cUbm1 = din("cUbm1", [128, 128]); cI = din("cI", [128, 128])
    cSelS = din("cSelS", [17, 128]); cSelP = din("cSelP", [17, 128]); cOH = din("cOH", [128, 16])
    cC = din("cC", [128, 8]); cOnes = din("cOnes", [128, 128])

    ym = dout("ym", [NMAIN, 128, D]); ys = dout("ys", [128, D])
    pk_o = dout("pk", [128, 128]); pv_o = dout("pv", [128, 128]); pst_o = dout("pst", [4, 64, 128])
    sk_o = dout("sk", [16, 128, 128]); sv_o = dout("sv", [16, 128, 128]); sst_o = dout("sst", [16, 4, 64, 128])

    ptr = [16640]
    LIMIT = nc.SBUF_PARTITION_SIZE_BYTES

    def sb(name, shape, dt=F32, at=None):
        size = int(np.prod(shape[1:])) * (4 if dt == F32 else 2)
        size = (size + 31) // 32 * 32
        if at is None:
            off = ptr[0]
            ptr[0] += size
            assert ptr[0] <= LIMIT, (name, ptr[0], LIMIT)
        else:
            off = at[0]
            at[0] += size
        t = nc.alloc_sbuf_tensor_at(name, list(shape), dt, offset=off)
        return T(t, name)

    pf = [T(es.enter_context(nc.psum_tensor("pf%d" % i, [128, 512], F32)), "pf%d" % i, True) for i in range(6)]
    pbk = [T(es.enter_context(nc.psum_tensor("pb%d" % i, [128, 1024], BF16)), "pb%d" % i, True) for i in range(2)]
    dq = {"rec": None, "queue": [], "every": 5, "cnt": 0, "banks": []}

    rot = {"f": 0, "b": 0}

    pinned = set()

    pinned_b = set()

    def nf():
        while True:
            rot["f"] = (rot["f"] + 1) % 6
            if rot["f"] not in pinned:
                if dq["rec"] is not None:
                    pinned.add(rot["f"]); dq["banks"].append(pf[rot["f"]])
                return pf[rot["f"]]

    def pin(*banks):
        for b_ in banks:
            pinned.add(pf.index(b_))

    def unpin(*banks):
        for b_ in banks:
            pinned.discard(pf.index(b_))

    def nb():
        while True:
            rot["b"] = (rot["b"] + 1) % 2
            if rot["b"] not in pinned_b:
                if dq["rec"] is not None:
                    pinned_b.add(rot["b"]); dq["banks"].append(pbk[rot["b"]])
                return pbk[rot["b"]]

    def _reg(eng, fn, reads, writes, dsem):
        return S.op(eng, fn, reads=reads, writes=writes, dsem=dsem)

    def _emit(eng, fn, reads, writes, dsem):
        if dq["rec"] is not None:
            dq["rec"].append((eng, _freeze(fn), reads, writes, dsem))
            return None
        ev = _reg(eng, fn, reads, writes, dsem)
        if dq["queue"]:
            dq["cnt"] += 1
            if dq["cnt"] % dq["every"] == 0:
                _reg(*dq["queue"].pop(0))
                if not dq["queue"]:
                    _release_deferred()
        return ev

    def _release_deferred():
        for b_ in dq["banks"]:
            if b_ in pf:
                pinned.discard(pf.index(b_))
            else:
                pinned_b.discard(pbk.index(b_))
        dq["banks"] = []

    def defer_begin():
        assert dq["rec"] is None and not dq["queue"]
        dq["rec"] = []

    def defer_end():
        dq["queue"] = dq["rec"]
        dq["rec"] = None
        dq["cnt"] = 0
        if not dq["queue"]:
            _release_deferred()

    def defer_flush():
        while dq["queue"]:
            _reg(*dq["queue"].pop(0))
        _release_deferred()

    def OP(eng, fn, r=(), w=()):
        return _emit(eng, fn, [x.b for x in r], [x.b for x in w], None)

    def DMA(eng, out_ap, in_ap, sem, r=(), w=()):
        return _emit(eng, lambda e: e.dma_start(out=out_ap, in_=in_ap), [x.b for x in r], [x.b for x in w], sem)

    dram_sink = T(None, "dram_out")
    dbg_sem = S.dma_sem("dbg")
    dbg_seen = set()

    def DBG(name, tobj, ap):
        if not debug or name in dbg_seen:
            return
        dbg_seen.add(name)
        shp = list(ap.shape)
        d = nc.dram_tensor("dbg_" + name, shp, F32, kind="ExternalOutput").ap()
        if tobj.b.excl:
            raise ValueError("dump SBUF only")
        DMA("pool", d, ap, dbg_sem, r=[tobj])


    Win = sb("Win", [128, KC, INC], BF16)
    WB = [T(Win.t, "WB%d" % i) for i in range(3)]
    Wba = sb("Wba", [128, 4, D], BF16); Wbg = sb("Wbg", [128, 4, D], BF16); Wout = sb("Wout", [128, KC, D], BF16)

    def wbuf(c):
        if 512 <= c < 768 or 1536 <= c < 2320:
            return WB[0]
        if c < 1536:
            return WB[1]
        return WB[2]

    U4b = sb("U4b", [128, 4, 128], BF16); L4b = sb("L4b", [128, 4, 128], BF16); L4fb = sb("L4fb", [128, 4, 128], BF16)
    Ub4b = sb("Ub4b", [128, 4, 128], BF16); C4b = sb("C4b", [128, 8, 8], BF16)
    Uf = sb("Uf", [128, 128]); Um1f = sb("Um1f", [128, 128]); Ubf = sb("Ubf", [128, 128]); Ubm1f = sb("Ubm1f", [128, 128])
    onesf = sb("onesf", [128, 128]); onesb = sb("onesb", [128, 128], BF16); identb = sb("identb", [128, 128], BF16)
    OH = sb("OH", [128, 16]); selS = sb("selS", [17, 128]); selP = sb("selP", [17, 128])
    Gqk = sb("Gqk", [128, 640]); esink = sb("esink", [1, 1024]); gdv = sb("gdv", [128, 1])
    acs = sb("acs", [128, 16]); Gbc = sb("Gbc", [128, D], BF16)
    flg = sb("flg", [128, 64]); WgF = sb("WgF", [17, 256]); WgA = sb("WgA", [17, 256], BF16)
    Sst = sb("Sst", [128, 2, 128])
    VgH = None
    SstQ = [[T(Sst.t, "Sst_%d_%d" % (_j, _hl)) for _hl in range(2)] for _j in range(2)]
    SstAll = [SstQ[0][0], SstQ[0][1], SstQ[1][0], SstQ[1][1]]
    LRa = sb("LRa", [17, 128], BF16)
    ov0 = ptr[0]
    ov = [ov0]
    WadaC = [sb("WadaC%d" % i, [128, KC, 512], BF16, at=ov) for i in range(2)]
    badaC = [sb("badaC%d" % i, [1, 512], F32, at=ov) for i in range(2)]
    modtm = sb("modtm", [17, 3072], F32, at=ov)
    CV = sb("CV", [17, D], F32, at=ov); CE = sb("CE", [17, D], F32, at=ov); CSb = sb("CSb", [17, D], BF16, at=ov)
    sTc = sb("sTc", [128, KC, 32], BF16, at=ov); Grow = sb("Grow", [17, D], F32, at=ov)
    sinkrow = sb("sinkrow", [1, 1024], F32, at=ov)
    idf_s = sb("idf_s", [128, 128], F32, at=ov); cLf_s = sb("cLf_s", [128, 128], F32, at=ov); cCf_s = sb("cCf_s", [128, 8], F32, at=ov)
    Atm = sb("Atm", [128, D], BF16, at=ov); Stm = sb("Stm", [128, D], BF16, at=ov); Gtm = sb("Gtm", [128, D], BF16, at=ov)
    modx = nc.dram_tensor("modx", [3, 128, D], BF16).ap()
    assert ov[0] <= LIMIT
    X = [sb("X%d" % i, [128, D]) for i in range(2)]
    XN = sb("XN", [128, D], BF16)
    junk = XN; XH = XN; ss = sb("ss", [128, 1]); rstd = sb("rstd", [128, 1])
    hT = [sb("hT%d" % i, [128, KC, 128], BF16) for i in range(2)]
    for _h in hT:
        _h.k = [T(_h.t, _h.b.name + "_k%d" % _k) for _k in range(KC)]
    ropeT = [sb("rope%d" % i, [128, 64]) for i in range(2)]
    t1 = sb("t1", [128, 640]); tu = sb("tu", [128, 640]); tw = sb("tw", [128, 640]); QKb = sb("QKb", [128, 640], BF16)
    sq640 = sb("sq640", [128, 640], BF16); ssq = sb("ssq", [128, 10]); rq = sb("rq", [128, 10])
    qT = sb("qT", [128, 4, 128], BF16); kT = [sb("kT%d" % i, [128, 128], BF16) for i in range(2)]
    Vd = [sb("Vd%d" % i, [128, 2, 2, 64], BF16) for i in range(2)]; Vf = sb("Vf", [128, 128])
    PT = [sb("PT%d" % i, [128, 512], BF16) for i in range(2)]
    Zs = sb("Zs", [128, 4, 128], BF16); Rr = sb("Rr", [128, 512]); tt = sb("tt", [128, 512], BF16)
    OZ = sb("OZ", [128, 4, 128], BF16)
    Lg = sb("Lg", [128, 256]); Dk = sb("Dk", [128, 256]); kd = sb("kd", [128, 256], BF16)
    Eq = sb("Eq", [128, 2, 128]); Ek = sb("Ek", [128, 2, 128]); Elast = sb("Elast", [128, 2])
    qe = sb("qe", [128, 2, 128], BF16); ke = sb("ke", [128, 2, 2, 128], BF16); Sbm = sb("Sbm", [128, 2, 2, 128], BF16)
    Vg = sb("Vg", [128, 512], BF16); Am = sb("Am", [128, 4, 128], BF16)
    sqg = sb("sqg", [128, 512], BF16); rsg = Rr
    Zg = sb("Zg", [128, 4, 128], BF16); OGZ = sb("OGZ", [128, 4, 128], BF16)
    Et = sb("Et", [128, 512]); Et2 = sb("Et2", [128, 512])
    Mg = sb("Mg", [128, 16, 128], BF16); mT = sb("mT", [128, KC, 128], BF16); brt = sb("brt", [128, 512]); og = brt
    Y = sb("Y", [128, D])
    YH = [T(Y.t, "Y_h%d" % _i) for _i in range(2)]
    t1P = [T(t1.t, "t1_p%d" % _i) for _i in range(2)]
    sqP = [T(sq640.t, "sq_p%d" % _i) for _i in range(2)]
    QKbP = [T(QKb.t, "QKb_p%d" % _i) for _i in range(2)]
    tuH = [T(tu.t, "tu_h%d" % _i) for _i in range(2)]
    twH = [T(tw.t, "tw_h%d" % _i) for _i in range(2)]
    keH = [T(ke.t, "ke_h%d" % _i) for _i in range(2)]
    SbmH = [T(Sbm.t, "Sbm_h%d" % _i) for _i in range(2)]
    VgH = [T(Vg.t, "Vg_h%d" % _i) for _i in range(2)]
    OZc = [T(OZ.t, "OZ_c%d" % _i) for _i in range(4)]
    MgQ = [T(Mg.t, "Mg_q%d" % _i) for _i in range(4)]
    _xn_off = XN.t.manual_sbuf_range[0]
    MTM = [T(nc.alloc_sbuf_tensor_at("MTM%d" % _i, [128, 512], BF16, offset=_xn_off + 1024 * _i), "MTM%d" % _i) for _i in range(2)]
    _mt_off = mT.t.manual_sbuf_range[0]
    for _i in range(2):
        _t = T(nc.alloc_sbuf_tensor_at("PTx%d" % _i, [128, 512], BF16, offset=_mt_off + 1024 * _i), "PTx%d" % _i)
        _t.b = mT.b
        PT.append(_t)
    CKs = [sb("CKs%d" % i, [128, 128]) for i in range(4)]; CVs = [sb("CVs%d" % i, [128, 128]) for i in range(4)]
    CKb = [sb("CKb%d" % i, [128, 128], BF16) for i in range(2)]
    KcT = [sb("KcT%d" % i, [128, 128], BF16) for i in range(2)]; Vcd = [sb("Vcd%d" % i, [128, 2, 2, 64], BF16) for i in range(2)]
    PTc = [sb("PTc%d" % i, [128, 2, 4, 8], BF16) for i in range(2)]
    S0s = [sb("S0s%d" % i, [128, 2, 128]) for i in range(4)]; S0b = [sb("S0b%d" % i, [128, 2, 2, 128], BF16) for i in range(4)]
    kdm = [sb("kdm%d" % i, [128, 256], BF16) for i in range(2)]; SN = S0s
    ElS = sb("ElS", [128, 2, 16]); qTs = sb("qTs", [128, 16, 4, 8], BF16)

    sem_c = S.dma_sem("c"); sem_mx = S.dma_sem("mx"); sem_c0 = S.dma_sem("c0"); sem_cp = S.dma_sem("cp"); sem_c0p = S.dma_sem("c0p"); sem_w = [S.dma_sem("w%d" % i) for i in range(3)]; sem_wb = S.dma_sem("wb")
    sem_ada = [S.dma_sem("ada%d" % i) for i in range(2)]; sem_bada = [S.dma_sem("bada%d" % i) for i in range(2)]
    sem_x = [S.dma_sem("x%d" % i) for i in range(2)]; sem_y = S.dma_sem("y"); sem_r = [S.dma_sem("r%d" % i) for i in range(2)]
    sem_o = S.dma_sem("o"); sem_ck = [S.dma_sem("ck%d" % i) for i in range(4)]; sem_s0 = [S.dma_sem("s0%d" % i) for i in range(4)]
    sem_sn = [S.dma_sem("sn%d" % i) for i in range(4)]

    try:
        DMA("sp", CV.t[:], cvec, sem_c0, w=[CV])
        DMA("sp", onesf.t[:], cOnes, sem_c0, w=[onesf])
        DMA("sp", idf_s.t[:], cI, sem_c0, w=[idf_s])
        OP("dve", lambda e: e.tensor_copy(identb.t[:], idf_s.t[:]), r=[idf_s], w=[identb])
        DMA("sp", cLf_s.t[:], cL, sem_c, w=[cLf_s])
        DMA("sp", cCf_s.t[:], cC, sem_c, w=[cCf_s])
        for (dst, src) in [(Uf, cU), (Um1f, cUm1), (Ubf, cUb), (Ubm1f, cUbm1), (OH, cOH),
                           (selS, cSelS), (selP, cSelP), (Gqk, gqk_d), (gdv, gdv_d), (flg, flags_d),
                           (sinkrow, sinks_d)]:
            DMA("sp", dst.t[:], src, sem_c, w=[dst])
        chk(10)
        DMA("sp", Grow.t[:], bass.AP(normrow_d.tensor, normrow_d.offset, [[0, 17], [1, D]]), sem_c, w=[Grow])
        chk(11)
        chk(14)
        DMA("sp", WgF.t[0:16, :], wgate_d, sem_c, w=[WgF])
        DMA("sp", WgF.t[16:17, :], bgate_d, sem_c, w=[WgF])
        OP("dve", lambda e: e.tensor_copy(WgA.t[:], WgF.t[:]), r=[WgF], w=[WgA])

        wada_v = wada_d.rearrange("(kc p) c -> p kc c", p=128)
        win_v = win_d.rearrange("(kc p) c -> p kc c", p=128)

        def load_ada(n):
            DMA("pool", WadaC[n % 2].t[:], wada_v[:, :, n * 512:(n + 1) * 512], sem_ada[n % 2], w=[WadaC[n % 2]])
            DMA("sp", badaC[n % 2].t[:], bada_d[:, n * 512:(n + 1) * 512], sem_bada[n % 2], w=[badaC[n % 2]])

        chk(1)
        load_ada(0); load_ada(1)
        DMA("pool", Win.t[:, :, 512:768], win_v[:, :, 512:768], sem_w[0], w=[WB[0]])
        DMA("pool", Win.t[:, :, 1536:2320], win_v[:, :, 1536:2320], sem_w[0], w=[WB[0]])

        OP("dve", lambda e: e.tensor_copy(onesb.t[:], onesf.t[:]), r=[onesf], w=[onesb])
        OP("dve", lambda e: e.tensor_copy(U4b.t[:], bc_ap(Uf.t[:, 0:1], [(0, 4), (1, 128)])), r=[Uf], w=[U4b])
        OP("dve", lambda e: e.tensor_copy(L4b.t[:], bc_ap(cLf_s.t[:, 0:1], [(0, 4), (1, 128)])), r=[cLf_s], w=[L4b])
        OP("dve", lambda e: e.tensor_copy(Ub4b.t[:], bc_ap(Ubf.t[:, 0:1], [(0, 4), (1, 128)])), r=[Ubf], w=[Ub4b])
        OP("dve", lambda e: e.tensor_copy(C4b.t[:], bc_ap(cCf_s.t[:, 0:1], [(0, 8), (1, 8)])), r=[cCf_s], w=[C4b])
        OP("pool", lambda e: e.memset(LRa.t[:], 1.0), w=[LRa])
        OP("pool", lambda e: e.memset(Sst.t[:], 0.0), w=SstAll)

        OP("dve", lambda e: e.tensor_scalar_mul(L4fb.t[:], L4b.t[:], flg.t[:, 48:49]), r=[L4b, flg], w=[L4fb])
        OP("dve", lambda e: e.tensor_scalar_mul(Gqk.t[:, 0:512], Gqk.t[:, 0:512], 0.125), r=[Gqk], w=[Gqk])
        OP("act", lambda e: e.activation(out=esink.t[:], in_=sinkrow.t[:], func=AF.Exp), r=[sinkrow], w=[esink])

        chk(2)
        OP("act", lambda e: e.activation(out=CE.t[:], in_=CV.t[:], func=AF.Exp, scale=-1.0), r=[CV], w=[CE])
        OP("dve", lambda e: e.tensor_scalar_add(CE.t[:], CE.t[:], 1.0), r=[CE], w=[CE])
        OP("dve", lambda e: e.reciprocal(CE.t[:], CE.t[:]), r=[CE], w=[CE])
        OP("dve", lambda e: e.tensor_tensor(CSb.t[:], CV.t[:], CE.t[:], ALU.mult), r=[CV, CE], w=[CSb])
        pb0 = nb()
        for kc in range(KC):
            OP("pe", lambda e, kc=kc: e.transpose(pb0.t[:, kc * 32:kc * 32 + 17], CSb.t[0:17, kc * 128:(kc + 1) * 128],
                                                   identb.t[0:17, 0:17]), r=[CSb, identb], w=[pb0])
        OP("dve", lambda e: e.tensor_copy(sTc.t[:, :, 0:17], pb0.t[:, 0:256].rearrange("p (k c) -> p k c", c=32)[:, :, 0:17]),
           r=[pb0], w=[sTc])
        for n in range(6):
            bk = nf()
            for kc in range(KC):
                OP("pe", lambda e, kc=kc, n=n, bk=bk: e.matmul(bk.t[0:17, :], sTc.t[:, kc, 0:17], WadaC[n % 2].t[:, kc, :],
                                                               start=(kc == 0), stop=False), r=[sTc, WadaC[n % 2]], w=[bk])
            OP("pe", lambda e, n=n, bk=bk: e.matmul(bk.t[0:17, :], onesf.t[0:1, 0:17], badaC[n % 2].t[0:1, :],
                                                    start=False, stop=True), r=[onesf, badaC[n % 2]], w=[bk])
            OP("act", lambda e, n=n, bk=bk: e.copy(modtm.t[0:17, n * 512:(n + 1) * 512], bk.t[0:17, :]), r=[bk], w=[modtm])
            if n + 2 < 6:
                load_ada(n + 2)
        chk(3)
        DMA("pool", Win.t[:, :, 0:512], win_v[:, :, 0:512], sem_w[1], w=[WB[1]])
        DMA("pool", Win.t[:, :, 768:1536], win_v[:, :, 768:1536], sem_w[1], w=[WB[1]])
        for c0 in range(2320, INC, 640):
            DMA("pool", Win.t[:, :, c0:c0 + 640], win_v[:, :, c0:c0 + 640], sem_w[2], w=[WB[2]])
        DMA("pool", Wba.t[:], wba_d.rearrange("(kc p) c -> p kc c", p=128), sem_wb, w=[Wba])
        DMA("pool", Wbg.t[:], wbg_d.rearrange("(kc p) c -> p kc c", p=128), sem_wb, w=[Wbg])
        DMA("pool", Wout.t[:], wout_d.rearrange("(kc p) c -> p kc c", p=128), sem_wb, w=[Wout])

        chk(4)
        OP("dve", lambda e: e.tensor_scalar_add(modtm.t[:, 1024:2048], modtm.t[:, 1024:2048], 1.0), r=[modtm], w=[modtm])
        OP("dve", lambda e: e.tensor_tensor(modtm.t[:, 1024:2048], modtm.t[:, 1024:2048], Grow.t[:], ALU.mult),
           r=[modtm, Grow], w=[modtm])
        for (dst, sel, c0) in [(Atm, selS, 1024), (Stm, selS, 0), (Gtm, selS, 2048), (Gbc, selP, 2048)]:
            for n in range(2):
                bk = nf()
                OP("pe", lambda e, bk=bk, sel=sel, c0=c0, n=n: e.matmul(bk.t[:, :], sel.t[0:17, :],
                                                                        modtm.t[0:17, c0 + n * 512:c0 + (n + 1) * 512],
                                                                        start=True, stop=True), r=[sel, modtm], w=[bk])
                OP("act", lambda e, bk=bk, dst=dst, n=n: e.copy(dst.t[:, n * 512:(n + 1) * 512], bk.t[:, :]), r=[bk], w=[dst])
        for k_, t_ in enumerate((Atm, Stm, Gtm)):
            DMA("sp", modx[k_], t_.t[:], sem_mx, r=[t_])
        bk = nf()
        for kc in range(KC):
            OP("pe", lambda e, kc=kc, bk=bk: e.matmul(bk.t[:, kc:kc + 1], modtm.t[0:1, 1024 + kc * 128:1024 + (kc + 1) * 128],
                                                      onesf.t[0:1, 0:1], start=True, stop=True), r=[modtm, onesf], w=[bk])
            OP("pe", lambda e, kc=kc, bk=bk: e.matmul(bk.t[:, 8 + kc:9 + kc], modtm.t[0:1, kc * 128:(kc + 1) * 128],
                                                      onesf.t[0:1, 0:1], start=True, stop=True), r=[modtm, onesf], w=[bk])
        OP("dve", lambda e, bk=bk: e.tensor_copy(acs.t[:], bk.t[:, 0:16]), r=[bk], w=[acs])

        chk(5)
        for e_ in ENGS:
            S.wait_all(e_, exclude=set(sem_w) | {sem_wb})
        OP("pool", lambda e: e.memset(ke.t[:], 0.0), w=keH)
        OP("pool", lambda e: e.memset(Sbm.t[:], 0.0), w=SbmH)
        for i_ in range(4):
            OP("pool", lambda e, i_=i_: e.memset(S0b[i_].t[:], 0.0), w=[S0b[i_]])

        st = {"x": 0, "h": 0, "r": 0, "kv": 0}
        smod = {"A": None, "S": None}

        def front_dma(x_ap):
            xi = st["x"]; st["x"] ^= 1
            xs = X[xi]
            DMA("sp", xs.t[:], x_ap, sem_x[xi], w=[xs])
            return xs

        def front_a(x_ap, sample, xs=None):
            if xs is None:
                xs = front_dma(x_ap)
            OP("act", lambda e: e.activation(out=junk.t[:], in_=xs.t[:], func=AF.Square, accum_out=ss.t[:, 0:1]),
               r=[xs], w=[junk, ss])
            OP("act", lambda e: e.activation(out=rstd.t[:], in_=ss.t[:], func=AF.Ln, scale=1.0 / D, bias=EPS), r=[ss], w=[rstd])
            OP("act", lambda e: e.activation(out=rstd.t[:], in_=rstd.t[:], func=AF.Exp, scale=-0.5), r=[rstd], w=[rstd])
            chk(16)
            OP("pool", lambda e: e.tensor_tensor(XN.t[:], xs.t[:], bc_ap(rstd.t[:, 0:1], [(0, D)]), ALU.mult), r=[xs, rstd], w=[XN])
            DBG("rstd", rstd, rstd.t[:]); DBG("XN", XN, XN.t[:]); DBG("acs", acs, acs.t[:]); DBG("Gbc", Gbc, Gbc.t[:])
            chk(17)
            src = XN
            if sample:
                OP("dve", lambda e: e.tensor_tensor(XH.t[:], XN.t[:], smod["A"].t[:], ALU.mult), r=[XN, smod["A"]], w=[XH])
                OP("pool", lambda e: e.tensor_tensor(XH.t[:], XH.t[:], smod["S"].t[:], ALU.add), r=[XH, smod["S"]], w=[XH])
                src = XH
            return xs, src

        def front_b(src, sample):
            EV = "dve"
            bk = nb()
            bk2 = nb() if EV == "twobank" else bk
            bks = [bk, bk2]
            for kc in range(KC):
                OP("pe", lambda e, kc=kc: e.transpose(bks[kc % 2].t[:, kc * 128:(kc + 1) * 128], src.t[:, kc * 128:(kc + 1) * 128],
                                                       identb.t[:]), r=[src, identb], w=[bks[kc % 2]])
            chk(18)
            hi = st["h"]; st["h"] ^= 1
            h = hT[hi]
            if sample:
                assert bk2 is bk
                OP("dve", lambda e: e.tensor_copy(h.t[:], bk.t[:].rearrange("p (k c) -> p k c", c=128)), r=[bk], w=h.k)
            else:
                for kc in range(KC):
                    use_dve = {"mix": kc % 2 == 0, "twobank": kc % 2 == 0, "half": kc >= 4, "half2": kc < 4, "dve": True, "act": False}[EV]
                    bkc = bks[kc % 2]
                    OP("dve" if use_dve else "act",
                       (lambda e, kc=kc, bkc=bkc: e.tensor_scalar(h.t[:, kc, :], bkc.t[:, kc * 128:(kc + 1) * 128], acs.t[:, kc:kc + 1],
                                                         acs.t[:, 8 + kc:9 + kc], ALU.mult, ALU.add)) if use_dve else
                       (lambda e, kc=kc, bkc=bkc: e.activation(out=h.t[:, kc, :], in_=bkc.t[:, kc * 128:(kc + 1) * 128], func=AF.Identity,
                                                      scale=acs.t[:, kc:kc + 1], bias=acs.t[:, 8 + kc:9 + kc])),
                       r=[bkc, acs], w=[h.k[kc]])
            pass
            return h

        def proj_tm(h, c0, n, bk, off=0):
            for kc in range(KC):
                OP("pe", lambda e, kc=kc: e.matmul(bk.t[:, off:off + n], h.t[:, kc, :], Win.t[:, kc, c0:c0 + n],
                                                   start=(kc == 0), stop=(kc == KC - 1)), r=[h.k[kc], wbuf(c0)], w=[bk])

        def proj_fm(h, c0, m, bk, off):
            for kc in range(KC):
                OP("pe", lambda e, kc=kc: e.matmul(bk.t[0:m, off:off + 128], Win.t[:, kc, c0:c0 + m], h.t[:, kc, :],
                                                   start=(kc == 0), stop=(kc == KC - 1)), r=[h.k[kc], wbuf(c0)], w=[bk])

        def attn_kv(h, rope_idx, need_q, bq=None):
            ri = st["r"]; st["r"] ^= 1
            rp = ropeT[ri]
            DMA("sp", rp.t[:], rope_d[rope_idx], sem_r[ri], w=[rp])
            bkv = nf()
            proj_tm(h, KA, 256, bkv)
            nh = 10 if need_q else 2
            c0 = 0 if need_q else 512
            w = nh * 64
            if need_q:
                OP("act", lambda e: e.activation(out=sq640.t[:, 0:512], in_=bq.t[:, :], func=AF.Square), r=[bq], w=[sqP[0]])
            OP("act", lambda e: e.activation(out=sq640.t[:, 512:640], in_=bkv.t[:, 0:128], func=AF.Square), r=[bkv], w=[sqP[1]])
            OP("dve", lambda e: e.tensor_reduce(ssq.t[:, 10 - nh:10], sq640.t[:, c0:640].rearrange("p (h d) -> p h d", d=64),
                                                AX.X, ALU.add), r=(sqP if need_q else sqP[1:]), w=[ssq])
            OP("act", lambda e: e.activation(out=rq.t[:, 10 - nh:10], in_=ssq.t[:, 10 - nh:10], func=AF.Ln, scale=1.0 / 64, bias=EPS),
               r=[ssq], w=[rq])
            OP("act", lambda e: e.activation(out=rq.t[:, 10 - nh:10], in_=rq.t[:, 10 - nh:10], func=AF.Exp, scale=-0.5), r=[rq], w=[rq])
            if need_q:
                OP("dve", lambda e: e.tensor_tensor(t1.t[:, 0:512].rearrange("p (h d) -> p h d", d=64),
                                                    bq.t[:, :].rearrange("p (h d) -> p h d", d=64),
                                                    bc_ap(rq.t[:, 0:8], [(1, 8), (0, 64)]), ALU.mult), r=[bq, rq], w=[t1P[0]])
            OP("dve", lambda e: e.tensor_tensor(t1.t[:, 512:640].rearrange("p (h d) -> p h d", d=64),
                                                bkv.t[:, 0:128].rearrange("p (h d) -> p h d", d=64),
                                                bc_ap(rq.t[:, 8:10], [(1, 2), (0, 64)]), ALU.mult), r=[bkv, rq], w=[t1P[1]])
            OP("pool", lambda e: e.tensor_tensor(t1.t[:, c0:640], t1.t[:, c0:640], Gqk.t[:, c0:640], ALU.mult), r=t1P + [Gqk], w=t1P)
            t1v = t1.t[:, c0:640].rearrange("p (h t d) -> p h t d", t=2, d=32)
            tuv = tu.t[:, c0:640].rearrange("p (h t d) -> p h t d", t=2, d=32)
            twv = tw.t[:, c0:640].rearrange("p (h t d) -> p h t d", t=2, d=32)
            cosb = bc_ap(rp.t[:, 0:32], [(0, nh), (0, 2), (1, 32)])
            sinb = bc_ap(rp.t[:, 32:64], [(0, nh), (1, 32)])
            OP("dve", lambda e: e.tensor_tensor(tuv, t1v, cosb, ALU.mult), r=t1P + [rp], w=tuH)
            OP("pool", lambda e: e.tensor_tensor(twv[:, :, 0, :], t1v[:, :, 1, :], sinb, ALU.mult), r=t1P + [rp], w=[twH[0]])
            OP("pool", lambda e: e.tensor_tensor(twv[:, :, 1, :], t1v[:, :, 0, :], sinb, ALU.mult), r=t1P + [rp], w=[twH[1]])
            OP("dve", lambda e: e.tensor_tensor(tuv[:, :, 0, :], tuv[:, :, 0, :], twv[:, :, 0, :], ALU.subtract), r=[tuH[0], twH[0]], w=[tuH[0]])
            OP("dve", lambda e: e.tensor_tensor(tuv[:, :, 1, :], tuv[:, :, 1, :], twv[:, :, 1, :], ALU.add), r=[tuH[1], twH[1]], w=[tuH[1]])
            OP("pool", lambda e: e.tensor_copy(QKb.t[:, 512:640], tu.t[:, 512:640]), r=tuH, w=[QKbP[1]])
            if need_q:
                OP("pool", lambda e: e.tensor_copy(QKb.t[:, 0:512].rearrange("p (a two d) -> p two a d", two=2, a=4),
                                                   tu.t[:, 0:512].rearrange("p (two a d) -> p two a d", two=2, a=4)), r=tuH, w=[QKbP[0]])
            kvi = st["kv"]; st["kv"] ^= 1
            bt = nb()
            OP("pe", lambda e: e.transpose(bt.t[:, 512:640], QKb.t[:, 512:640], identb.t[:]), r=[QKbP[1], identb], w=[bt])
            if need_q:
                for a in range(4):
                    OP("pe", lambda e, a=a: e.transpose(bt.t[:, a * 128:(a + 1) * 128], QKb.t[:, a * 128:(a + 1) * 128], identb.t[:]),
                       r=[QKbP[0], identb], w=[bt])
                OP("act", lambda e: e.copy(qT.t[:], bt.t[:, 0:512].rearrange("p (a q) -> p a q", a=4)), r=[bt], w=[qT])
            OP("act", lambda e: e.copy(kT[kvi].t[:], bt.t[:, 512:640]), r=[bt], w=[kT[kvi]])
            if need_q:
                pass
            vsrc = bass.AP(bkv.t[:, 128:256].tensor, bkv.t[:, 128:256].offset,
                           [list(bkv.t[:, 128:256].ap[0]), [64, 2], [0, 2], [1, 64]])
            OP("dve", lambda e: e.tensor_copy(Vd[kvi].t[:], vsrc), r=[bkv], w=[Vd[kvi]])
            OP("act", lambda e: e.copy(Vf.t[:], bkv.t[:, 128:256]), r=[bkv], w=[Vf])
            if need_q:
                DBG("Vf", Vf, Vf.t[:]); DBG("Vd", Vd[kvi], Vd[kvi].t[:])
            return kvi

        def gla_gates(h, bkg, bvg, flag_col, sample):
            bl = nf()
            proj_fm(h, LR, 16, bl, 0)
            OP("act", lambda e: e.copy(LRa.t[0:16, :], bl.t[0:16, 0:128]), r=[bl], w=[LRa])
            bg_ = nf()
            OP("pe", lambda e: e.matmul(bg_.t[:, 0:256], LRa.t[0:17, :], WgA.t[0:17, :], start=True, stop=True),
               r=[LRa, WgA], w=[bg_])
            OP("act", lambda e: e.activation(out=Lg.t[:], in_=bg_.t[:, 0:256], func=AF.Exp, scale=-1.0), r=[bg_], w=[Lg])
            OP("act", lambda e: e.activation(out=Lg.t[:], in_=Lg.t[:], func=AF.Ln, scale=1.0, bias=1.0), r=[Lg], w=[Lg])
            um1 = Ubm1f if sample else Um1f
            be = nf()
            OP("pe", lambda e: e.matmul(be.t[:, 0:256], um1.t[:], Lg.t[:], start=True, stop=True), r=[um1, Lg], w=[be])
            OP("act", lambda e: e.activation(out=Dk.t[:], in_=be.t[:, 0:256], func=AF.Exp, scale=1.0 / 16), r=[be], w=[Dk])
            OP("dve", lambda e: e.tensor_tensor(kd.t[:], bkg.t[:, 0:256], Dk.t[:], ALU.mult), r=[bkg, Dk], w=[kd])
            if flag_col is None:
                OP("act", lambda e: e.copy(Vg.t[:, 0:256], bkg.t[:, 256:512]), r=[bkg], w=[VgH[0]])
                OP("act", lambda e: e.copy(Vg.t[:, 256:512], bvg.t[:, 0:256]), r=[bvg], w=[VgH[1]])
            else:
                OP("act", lambda e: e.activation(out=Vg.t[:, 0:256], in_=bkg.t[:, 256:512], func=AF.Identity,
                                                 scale=flg.t[:, flag_col:flag_col + 1]), r=[bkg, flg], w=[VgH[0]])
                OP("act", lambda e: e.activation(out=Vg.t[:, 256:512], in_=bvg.t[:, 0:256], func=AF.Identity,
                                                 scale=flg.t[:, flag_col:flag_col + 1]), r=[bvg, flg], w=[VgH[1]])

        def gla_state_update():
            bt_ = nf()
            for j in range(2):
                OP("pe", lambda e, j=j: e.matmul(bt_.t[:, j:j + 1], Lg.t[:, 128 * j:128 * (j + 1)], onesf.t[:, 0:1],
                                                 start=True, stop=True), r=[Lg, onesf], w=[bt_])
            OP("act", lambda e: e.activation(out=Elast.t[:], in_=bt_.t[:, 0:2], func=AF.Exp, scale=-1.0 / 16), r=[bt_], w=[Elast])
            bs = nf()
            for j in range(2):
                OP("pe", lambda e, j=j: e.matmul(bs.t[:, 256 * j:256 * (j + 1)], kd.t[:, 128 * j:128 * (j + 1)],
                                                 Vg.t[:, 256 * j:256 * (j + 1)], start=True, stop=True), r=[kd, VgH[j]], w=[bs])
            for j in range(2):
                for hl in range(2):
                    OP("dve", lambda e, j=j, hl=hl: e.scalar_tensor_tensor(
                        out=Sst.t[64 * hl:64 * (hl + 1), j, :], in0=Sst.t[64 * hl:64 * (hl + 1), j, :],
                        scalar=Elast.t[64 * hl:64 * (hl + 1), j:j + 1],
                        in1=bs.t[64 * hl:64 * (hl + 1), 256 * j + 128 * hl:256 * j + 128 * (hl + 1)],
                        op0=ALU.mult, op1=ALU.add), r=[SstQ[j][hl], Elast, bs], w=[SstQ[j][hl]])
            for hl in range(2):
                OP("pool", lambda e, hl=hl: e.tensor_copy(Sbm.t[64 * hl:64 * (hl + 1), :, hl, :], Sst.t[64 * hl:64 * (hl + 1), :, :]),
                   r=[SstQ[0][hl], SstQ[1][hl]], w=[SbmH[hl]])

        def sigmoid_from_psum(dst_ap, src_ap, src_t, dst_t, tmp):
            OP("act", lambda e: e.activation(out=tmp.t[:], in_=src_ap, func=AF.Exp, scale=-1.0), r=[src_t], w=[tmp])
            OP("act", lambda e: e.activation(out=tmp.t[:], in_=tmp.t[:], func=AF.Ln, scale=1.0, bias=1.0), r=[tmp], w=[tmp])
            OP("act", lambda e: e.activation(out=dst_ap, in_=tmp.t[:], func=AF.Exp, scale=-1.0), r=[tmp], w=[dst_t])

        def attn_prep(h, rope_idx, sample, last):
            bq = nf()
            proj_tm(h, QA, 512, bq)
            cur = attn_kv(h, rope_idx, True, bq)
            if last:
                DMA("sp", pk_o, tu.t[:, 512:640], sem_o, r=tuH)
                DMA("sp", pv_o, Vf.t[:], sem_o, r=[Vf])
            if sample:
                for i in range(16):
                    DMA("sp", sk_o[i, 120:128, :], tu.t[8 * i:8 * (i + 1), 512:640], sem_o, r=tuH)
                    DMA("sp", sv_o[i, 120:128, :], Vf.t[8 * i:8 * (i + 1), :], sem_o, r=[Vf])
            return cur

        def full_tile(xs, h, y_ap, rope_idx, sample, first, last, hook_a=None, hook_b=None, cur=None):
            chk(20)
            if cur is None:
                cur = attn_prep(h, rope_idx, sample, last)
            prev = cur ^ 1
            chk(21)
            def z_a_part():
                bz = nf()
                for c in range(4):
                    proj_fm(h, ZA + 128 * c, 128, bz, 128 * c)
                sigmoid_from_psum(Et2.t[:], bz.t[:, :], bz, Et2, Et)
                OP("dve", lambda e: e.tensor_tensor(Zs.t[:].rearrange("p a q -> p (a q)"), bz.t[:, :], Et2.t[:], ALU.mult),
                   r=[bz, Et2], w=[Zs])
                DBG("Zs", Zs, Zs.t[:])
            if sample:
                z_a_part()
            chk(22)
            mcur = Ub4b if sample else U4b
            mprev = L4fb if first else L4b
            def scores_part(kvh):
                rows = slice(64 * kvh, 64 * (kvh + 1))
                blocks = [(cur, mcur)] if sample else [(prev, mprev), (cur, mcur)]
                pts = []
                for bi, (slot, msk) in enumerate(blocks):
                    bsc = nf()
                    OP("pe", lambda e, slot=slot, bsc=bsc: e.matmul(bsc.t[:, :], kT[slot].t[rows, :],
                                                                    qT.t[rows, :, :].rearrange("p a q -> p (a q)"),
                                                                    start=True, stop=True), r=[kT[slot], qT], w=[bsc])
                    pt = PT[bi] if sample else PT[2 * kvh + bi]
                    OP("act", lambda e, pt=pt, bsc=bsc: e.activation(out=pt.t[:], in_=bsc.t[:, :], func=AF.Exp), r=[bsc], w=[pt])
                    OP("pool", lambda e, pt=pt, msk=msk: e.tensor_tensor(pt.t[:], pt.t[:], msk.t[:].rearrange("p a q -> p (a q)"),
                                                                         ALU.mult), r=[pt, msk], w=[pt])
                    pts.append((pt, slot))
                return pts

            all_pts = {}
            if not sample:
                for kvh in range(2):
                    all_pts[kvh] = scores_part(kvh)
                if hook_a is not None:
                    hook_a()
                    hook_a = None
                z_a_part()
            for kvh in range(2):
                rows = slice(64 * kvh, 64 * (kvh + 1))
                pts = scores_part(kvh) if sample else all_pts[kvh]
                bo = nf(); bd = nf()
                for bi, (pt, slot) in enumerate(pts):
                    OP("pe", lambda e, pt=pt, slot=slot, bi=bi: e.matmul(bo.t[:, :], Vd[slot].t[:, kvh, :, :].rearrange("p u d -> p (u d)"),
                                                                         pt.t[:], start=(bi == 0), stop=(bi == len(pts) - 1)),
                       r=[Vd[slot], pt], w=[bo])
                    OP("pe", lambda e, pt=pt, bi=bi: e.matmul(bd.t[:, :], onesb.t[:], pt.t[:], start=(bi == 0), stop=False),
                       r=[onesb, pt], w=[bd])
                OP("pe", lambda e: e.matmul(bd.t[:, :], onesf.t[0:1, :], esink.t[0:1, 512 * kvh:512 * (kvh + 1)],
                                            start=False, stop=True), r=[onesf, esink], w=[bd])
                if sample:
                    pin(bo, bd)
                    boc, bdc = sample_cache_attn(kvh)
                    unpin(bo, bd, boc, bdc)
                    OP("act", lambda e, bdc=bdc: e.copy(Et.t[:], bdc.t[:, :]), r=[bdc], w=[Et])
                    OP("dve", lambda e: e.tensor_tensor(Et2.t[:].rearrange("p (a i q) -> p i a q", a=4, q=8),
                                                        bd.t[:, :].rearrange("p (a i q) -> p i a q", a=4, q=8),
                                                        Et.t[:].rearrange("p (i a q) -> p i a q", a=4, q=8), ALU.add),
                       r=[bd, Et], w=[Et2])
                    OP("act", lambda e: e.activation(out=Rr.t[:], in_=Et2.t[:], func=AF.Ln), r=[Et2], w=[Rr])
                    OP("act", lambda e: e.activation(out=Rr.t[:], in_=Rr.t[:], func=AF.Exp, scale=-1.0), r=[Rr], w=[Rr])
                    OP("act", lambda e, boc=boc: e.copy(Et.t[:], boc.t[:, :]), r=[boc], w=[Et])
                    OP("dve", lambda e: e.tensor_tensor(Et2.t[:].rearrange("p (a i q) -> p i a q", a=4, q=8),
                                                        bo.t[:, :].rearrange("p (a i q) -> p i a q", a=4, q=8),
                                                        Et.t[:].rearrange("p (i a q) -> p i a q", a=4, q=8), ALU.add),
                       r=[bo, Et], w=[Et2])
                    OP("dve", lambda e: e.tensor_tensor(tt.t[:], Et2.t[:], Rr.t[:], ALU.mult), r=[Et2, Rr], w=[tt])
                else:
                    Rk = Rr if kvh == 0 else Et2
                    tk = tt if kvh == 0 else sqg
                    OP("act", lambda e: e.activation(out=Rk.t[:], in_=bd.t[:, :], func=AF.Ln), r=[bd], w=[Rk])
                    OP("act", lambda e: e.activation(out=Rk.t[:], in_=Rk.t[:], func=AF.Exp, scale=-1.0), r=[Rk], w=[Rk])
                    OP("dve", lambda e: e.tensor_tensor(tk.t[:], bo.t[:, :], Rk.t[:], ALU.mult), r=[bo, Rk], w=[tk])
                tk_ = tt if (sample or kvh == 0) else sqg
                for a2 in range(2):
                    c = 2 * kvh + a2
                    for par in range(2):
                        pr = slice(64 * par, 64 * (par + 1))
                        a = 2 * a2 + par
                        OP("pool", lambda e, c=c, pr=pr, a=a: e.tensor_tensor(OZ.t[pr, c, :], tk_.t[pr, 128 * a:128 * (a + 1)],
                                                                              Zs.t[pr, c, :], ALU.mult), r=[tk_, Zs], w=[OZc[c]])
            chk(23)
            if hook_a is not None:
                hook_a()
            if hook_b is not None:
                hook_b()
            DBG("OZ", OZ, OZ.t[:]); DBG("tt", tt, tt.t[:]); DBG("Rr", Rr, Rr.t[:]); DBG("PT1", PT[1], PT[1].t[:])
            bkg = nf(); pin(bkg)
            bvg = nf(); pin(bvg)
            proj_tm(h, KG, 512, bkg)
            proj_tm(h, VG + 256, 256, bvg)
            gla_gates(h, bkg, bvg, None, sample)
            unpin(bkg, bvg)
            chk(24)
            DBG("Lg", Lg, Lg.t[:]); DBG("Dk", Dk, Dk.t[:]); DBG("kd", kd, kd.t[:]); DBG("Vg", Vg, Vg.t[:])
            um = Ubf if sample else Uf
            bb = nf()
            for j in range(2):
                OP("pe", lambda e, j=j: e.matmul(bb.t[:, 128 * j:128 * (j + 1)], Lg.t[:, 128 * j:128 * (j + 1)], um.t[:],
                                                 start=True, stop=True), r=[Lg, um], w=[bb])
            OP("act", lambda e: e.activation(out=Eq.t[:].rearrange("p j t -> p (j t)"), in_=bb.t[:, 0:256], func=AF.Exp,
                                             scale=-1.0 / 16), r=[bb], w=[Eq])
            OP("act", lambda e: e.activation(out=Ek.t[:].rearrange("p j t -> p (j t)"), in_=bb.t[:, 0:256], func=AF.Exp,
                                             scale=1.0 / 16), r=[bb], w=[Ek])
            bqk = nf()
            for j in range(2):
                proj_fm(h, QG + 128 * j, 128, bqk, 128 * j)
                proj_fm(h, KG + 128 * j, 128, bqk, 256 + 128 * j)
            OP("dve", lambda e: e.scalar_tensor_tensor(out=qe.t[:].rearrange("p j t -> p (j t)"), in0=bqk.t[:, 0:256], scalar=0.125,
                                                       in1=Eq.t[:].rearrange("p j t -> p (j t)"), op0=ALU.mult, op1=ALU.mult),
               r=[bqk, Eq], w=[qe])
            for hl in range(2):
                pr = slice(64 * hl, 64 * (hl + 1))
                OP("dve", lambda e, hl=hl, pr=pr: e.tensor_tensor(ke.t[pr, hl, :, :], bqk.t[pr, 256:512].rearrange("p (j t) -> p j t", j=2),
                                                                  Ek.t[pr, :, :], ALU.mult), r=[bqk, Ek], w=[keH[hl]])
            ba = nf()
            for hh in range(4):
                j, hl = hh // 2, hh % 2
                pr = slice(64 * hl, 64 * (hl + 1))
                OP("pe", lambda e, hh=hh, j=j, hl=hl: e.matmul(ba.t[:, 128 * hh:128 * (hh + 1)], ke.t[:, hl, j, :], qe.t[:, j, :],
                                                               start=True, stop=True), r=[keH[hl], qe], w=[ba])
            OP("dve", lambda e: e.tensor_tensor(Am.t[:].rearrange("p a q -> p (a q)"), ba.t[:, :],
                                                mcur.t[:].rearrange("p a q -> p (a q)"), ALU.mult), r=[ba, mcur], w=[Am])
            bog = nf()
            for hh in range(4):
                j, hl = hh // 2, hh % 2
                pr = slice(64 * hl, 64 * (hl + 1))
                OP("pe", lambda e, hh=hh: e.matmul(bog.t[:, 128 * hh:128 * (hh + 1)], Vg.t[:, 128 * hh:128 * (hh + 1)],
                                                   Am.t[:, hh, :], start=True, stop=sample), r=[VgH[hh // 2], Am], w=[bog])
                if not sample:
                    OP("pe", lambda e, hh=hh, j=j, hl=hl: e.matmul(bog.t[:, 128 * hh:128 * (hh + 1)], Sbm.t[:, j, hl, :], qe.t[:, j, :],
                                                                   start=False, stop=True), r=[SbmH[hl], qe], w=[bog])
            chk(25)
            pass
            if sample:
                pin(bog)
                bint = sample_gla_state()
                unpin(bog, bint)
                OP("act", lambda e: e.copy(Et.t[:], bint.t[:, :]), r=[bint], w=[Et])
                OP("dve", lambda e: e.tensor_tensor(Et2.t[:], bog.t[:, :], Et.t[:], ALU.add), r=[bog, Et], w=[Et2])
                osrc, osrc_t = Et2.t[:], Et2
            else:
                gla_state_update()
                osrc, osrc_t = bog.t[:, :], bog
            chk(26)
            OP("act", lambda e: e.activation(out=sqg.t[:], in_=osrc, func=AF.Square), r=[osrc_t], w=[sqg])
            bn = nf()
            OP("pe", lambda e: e.matmul(bn.t[:, :], onesb.t[:], sqg.t[:], start=True, stop=True), r=[onesb, sqg], w=[bn])
            OP("act", lambda e: e.activation(out=rsg.t[:], in_=bn.t[:, :], func=AF.Ln, scale=1.0 / 128, bias=EPS), r=[bn], w=[rsg])
            OP("act", lambda e: e.activation(out=rsg.t[:], in_=rsg.t[:], func=AF.Exp, scale=-0.5), r=[rsg], w=[rsg])
            OP("dve", lambda e: e.tensor_tensor(og.t[:], osrc, rsg.t[:], ALU.mult), r=[osrc_t, rsg], w=[og])
            bzg = nf()
            for c in range(4):
                proj_fm(h, ZG + 128 * c, 128, bzg, 128 * c)
            sigmoid_from_psum(Et2.t[:], bzg.t[:, :], bzg, Et2, Et)
            OP("dve", lambda e: e.tensor_tensor(Zg.t[:].rearrange("p a q -> p (a q)"), bzg.t[:, :], Et2.t[:], ALU.mult),
               r=[bzg, Et2], w=[Zg])
            OP("dve", lambda e: e.scalar_tensor_tensor(out=OGZ.t[:].rearrange("p a q -> p (a q)"), in0=og.t[:], scalar=gdv.t[:, 0:1],
                                                       in1=Zg.t[:].rearrange("p a q -> p (a q)"), op0=ALU.mult, op1=ALU.mult),
               r=[og, gdv, Zg], w=[OGZ])
            chk(27)
            pass
            Mgt = Mg.t[:].rearrange("p a q -> p (a q)")
            for g4 in range(4):
                bm = nf()
                proj_tm(h, MA + 512 * g4, 512, bm)
                sigmoid_from_psum(Mgt[:, 512 * g4:512 * (g4 + 1)], bm.t[:, :], bm, MgQ[g4], Et if g4 % 2 == 0 else Et2)
            chk(28)
            for n in range(2):
                bra = nf(); brg = nf()
                for kc in range(4):
                    OP("pe", lambda e, kc=kc, n=n: e.matmul(bra.t[:, :], OZ.t[:, kc, :], Wba.t[:, kc, 512 * n:512 * (n + 1)],
                                                            start=(kc == 0), stop=(kc == 3)), r=[Wba, OZc[kc]], w=[bra])
                for kc in range(4):
                    OP("pe", lambda e, kc=kc, n=n: e.matmul(brg.t[:, :], OGZ.t[:, kc, :], Wbg.t[:, kc, 512 * n:512 * (n + 1)],
                                                            start=(kc == 0), stop=(kc == 3)), r=[Wbg, OGZ], w=[brg])
                ba_, bg2_ = (brt, Et) if n == 0 else (Rr, Et2)
                OP("dve", lambda e, n=n: e.tensor_tensor(ba_.t[:], bra.t[:, :], Mgt[:, 512 * n:512 * (n + 1)], ALU.mult),
                   r=[bra, MgQ[n]], w=[ba_])
                OP("dve", lambda e, n=n: e.tensor_tensor(bg2_.t[:], brg.t[:, :], Mgt[:, 1024 + 512 * n:1024 + 512 * (n + 1)], ALU.mult),
                   r=[brg, MgQ[2 + n]], w=[bg2_])
                OP("pool", lambda e, n=n: e.tensor_tensor(MTM[n].t[:], ba_.t[:], bg2_.t[:], ALU.add),
                   r=[ba_, bg2_, XN], w=[MTM[n]])
            btm = nb()
            for kc in range(KC):
                OP("pe", lambda e, kc=kc: e.transpose(btm.t[:, kc * 128:(kc + 1) * 128],
                                                       MTM[kc // 4].t[:, (kc % 4) * 128:(kc % 4 + 1) * 128], identb.t[:]),
                   r=[MTM[kc // 4], identb], w=[btm])
            OP("act", lambda e: e.copy(mT.t[:].rearrange("p a q -> p (a q)"), btm.t[:, :]), r=[btm, XN], w=[mT])
            DBG("mT", mT, mT.t[:])
            chk(29)
            gt = Gbc
            for n in range(2):
                by = nf()
                for kc in range(KC):
                    OP("pe", lambda e, kc=kc, n=n: e.matmul(by.t[:, :], mT.t[:, kc, :], Wout.t[:, kc, 512 * n:512 * (n + 1)],
                                                            start=(kc == 0), stop=(kc == KC - 1)), r=[mT, Wout], w=[by])
                chk(33)
                OP("dve", lambda e, n=n: e.tensor_tensor(Y.t[:, 512 * n:512 * (n + 1)], by.t[:, :], gt.t[:, 512 * n:512 * (n + 1)],
                                                         ALU.mult), r=[by, gt], w=[YH[n]])
                chk(32)
                OP("pool", lambda e, n=n: e.tensor_tensor(Y.t[:, 512 * n:512 * (n + 1)], Y.t[:, 512 * n:512 * (n + 1)],
                                                          xs.t[:, 512 * n:512 * (n + 1)], ALU.add), r=[YH[n], xs], w=[YH[n]])
                DMA("sp", y_ap[:, 512 * n:512 * (n + 1)], Y.t[:, 512 * n:512 * (n + 1)], sem_y, r=[YH[n]])
            chk(30)
            chk(31)

        smp = {"i": 0}

        def sample_cache_attn(kvh):
            rows = slice(64 * kvh, 64 * (kvh + 1))
            if kvh == 0:
                OP("dve", lambda e: e.tensor_copy(qTs.t[:], qT.t[:].rearrange("p a (i q) -> p i a q", q=8)), r=[qT], w=[qTs])
            boc = nf(); bdc = nf()
            pin(boc, bdc)
            base = smp["i"]; smp["i"] += 16

            def stage1(i):
                sl = (base + i) % 2; s4 = (base + i) % 4
                DMA("sp", CKs[s4].t[:], ck_d[i], sem_ck[s4], w=[CKs[s4]])
                DMA("sp", CVs[s4].t[:], cv_d[i], sem_ck[s4], w=[CVs[s4]])
                OP("dve", lambda e: e.tensor_copy(CKb[sl].t[:], CKs[s4].t[:]), r=[CKs[s4]], w=[CKb[sl]])
                bt = nb()
                OP("pe", lambda e: e.transpose(bt.t[:, 0:128], CKb[sl].t[:], identb.t[:]), r=[CKb[sl], identb], w=[bt])
                OP("act", lambda e: e.copy(KcT[sl].t[:], bt.t[:, 0:128]), r=[bt], w=[KcT[sl]])
                vsrc = bass.AP(CVs[s4].t[:].tensor, CVs[s4].t[:].offset, [list(CVs[s4].t[:].ap[0]), [64, 2], [0, 2], [1, 64]])
                OP("dve", lambda e: e.tensor_copy(Vcd[sl].t[:], vsrc), r=[CVs[s4]], w=[Vcd[sl]])

            def stage2(i):
                sl = (base + i) % 2
                bsc = nf()
                OP("pe", lambda e: e.matmul(bsc.t[:, 0:32], KcT[sl].t[rows, :], qTs.t[rows, i, :, :].rearrange("p a q -> p (a q)"),
                                            start=True, stop=True), r=[KcT[sl], qTs], w=[bsc])
                OP("act", lambda e: e.activation(out=PTc[sl].t[:, 0, :, :].rearrange("p a q -> p (a q)"),
                                                 in_=bsc.t[:, 0:32], func=AF.Exp), r=[bsc], w=[PTc[sl]])
                OP("pool", lambda e: e.tensor_tensor(PTc[sl].t[:, 0, :, :], PTc[sl].t[:, 0, :, :], C4b.t[:, 0:4, :], ALU.mult),
                   r=[PTc[sl], C4b], w=[PTc[sl]])

            def stage3(i):
                sl = (base + i) % 2
                OP("pe", lambda e: e.matmul(boc.t[:, 32 * i:32 * (i + 1)], Vcd[sl].t[:, kvh, :, :].rearrange("p u d -> p (u d)"),
                                            PTc[sl].t[:, 0, :, :].rearrange("p a q -> p (a q)"), start=True, stop=True),
                   r=[Vcd[sl], PTc[sl]], w=[boc])
                OP("pe", lambda e: e.matmul(bdc.t[:, 32 * i:32 * (i + 1)], onesb.t[:],
                                            PTc[sl].t[:, 0, :, :].rearrange("p a q -> p (a q)"), start=True, stop=True),
                   r=[onesb, PTc[sl]], w=[bdc])

            for t_ in range(18):
                if 0 <= t_ - 2 < 16:
                    stage3(t_ - 2)
                if 0 <= t_ - 1 < 16:
                    stage2(t_ - 1)
                if t_ < 16:
                    stage1(t_)
            return boc, bdc

        def sample_gla_state():
            bog = nf()
            pin(bog)
            OP("dve", lambda e: e.tensor_copy(ElS.t[:], Eq.t[:].rearrange("p j (i r) -> p j i r", r=8)[:, :, :, 7]), r=[Eq], w=[ElS])

            def stage1(i):
                sl = i % 2; s4 = i % 4
                DMA("sp", S0s[s4].t[:], st0_d[i].rearrange("(j hl) k v -> (hl k) j v", hl=2), sem_s0[s4], w=[S0s[s4]])
                OP("dve", lambda e: e.tensor_copy(S0b[s4].t[0:64, :, 0, :], S0s[s4].t[0:64, :, :]), r=[S0s[s4]], w=[S0b[s4]])
                OP("act", lambda e: e.copy(S0b[s4].t[64:128, :, 1, :], S0s[s4].t[64:128, :, :]), r=[S0s[s4]], w=[S0b[s4]])
                OP("dve", lambda e: e.tensor_scalar_mul(kdm[sl].t[:], kd.t[:], OH.t[:, i:i + 1]), r=[kd, OH], w=[kdm[sl]])

            def stage2(i):
                sl = i % 2; s4 = i % 4
                for hh in range(4):
                    j, hl = hh // 2, hh % 2
                    OP("pe", lambda e, hh=hh, j=j, hl=hl: e.matmul(bog.t[:, 128 * hh + 8 * i:128 * hh + 8 * (i + 1)],
                                                                   S0b[s4].t[:, j, hl, :], qe.t[:, j, 8 * i:8 * (i + 1)],
                                                                   start=True, stop=True), r=[S0b[s4], qe], w=[bog])
                bs = nf()
                for j in range(2):
                    OP("pe", lambda e, j=j: e.matmul(bs.t[:, 256 * j:256 * (j + 1)], kdm[sl].t[:, 128 * j:128 * (j + 1)],
                                                     Vg.t[:, 256 * j:256 * (j + 1)], start=True, stop=True),
                       r=[kdm[sl], VgH[j]], w=[bs])
                for j in range(2):
                    for hl in range(2):
                        OP("dve", lambda e, j=j, hl=hl: e.scalar_tensor_tensor(
                            out=SN[s4].t[64 * hl:64 * (hl + 1), j, :], in0=S0s[s4].t[64 * hl:64 * (hl + 1), j, :],
                            scalar=ElS.t[64 * hl:64 * (hl + 1), j, i:i + 1],
                            in1=bs.t[64 * hl:64 * (hl + 1), 256 * j + 128 * hl:256 * j + 128 * (hl + 1)],
                            op0=ALU.mult, op1=ALU.add), r=[S0s[s4], ElS, bs], w=[SN[s4]])
                DMA("act", sst_o[i].rearrange("(j hl) k v -> (hl k) j v", hl=2), SN[s4].t[:], sem_sn[s4], r=[SN[s4]])

            for t_ in range(17):
                if 0 <= t_ - 1 < 16:
                    stage2(t_ - 1)
                if t_ < 16:
                    stage1(t_)
            return bog

        def pre_tile(i, xs, h, hook_a=None, hook_b=None):
            if hook_a is not None:
                hook_a()
            bkg = nf(); bvg = nf()
            proj_tm(h, KG, 512, bkg)
            proj_tm(h, VG + 256, 256, bvg)
            gla_gates(h, bkg, bvg, i, False)
            if hook_b is not None:
                hook_b()
            gla_state_update()
            if i == NPRE - 1:
                attn_kv(h, 16, False)

        chk(15)
        DMA("act", sk_o[:, 0:120, :], ck_d[:, 8:128, :], sem_o)
        DMA("act", sv_o[:, 0:120, :], cv_d[:, 8:128, :], sem_o)
        tiles = [("main", i) for i in range(n_main)]
        if do_sample:
            tiles.append(("smp", 0))

        def pre_P1(h):
            bkg = nf(); pin(bkg)
            proj_tm(h, KG, 512, bkg)
            return bkg

        def pre_P2(h):
            bvg = nf(); pin(bvg)
            proj_tm(h, VG + 256, 256, bvg)
            proj_fm(h, LR, 16, bvg, 256)
            return bvg

        def pre_G1(bvg):
            OP("act", lambda e: e.copy(LRa.t[0:16, :], bvg.t[0:16, 256:384]), r=[bvg], w=[LRa])
            bg_ = nf()
            OP("pe", lambda e: e.matmul(bg_.t[:, 0:256], LRa.t[0:17, :], WgA.t[0:17, :], start=True, stop=True),
               r=[LRa, WgA], w=[bg_])
            return bg_

        def pre_G2a(bg_):
            OP("act", lambda e: e.activation(out=Lg.t[:], in_=bg_.t[:, 0:256], func=AF.Exp, scale=-1.0), r=[bg_], w=[Lg])
            OP("act", lambda e: e.activation(out=Lg.t[:], in_=Lg.t[:], func=AF.Ln, scale=1.0, bias=1.0), r=[Lg], w=[Lg])
            OP("pe", lambda e: e.matmul(bg_.t[:, 256:512], Um1f.t[:], Lg.t[:], start=True, stop=True), r=[Um1f, Lg], w=[bg_])
            for j in range(2):
                OP("pe", lambda e, j=j: e.matmul(bg_.t[:, j:j + 1], Lg.t[:, 128 * j:128 * (j + 1)], onesf.t[:, 0:1],
                                                 start=True, stop=True), r=[Lg, onesf], w=[bg_])

        def pre_G2b(bkg, bvg, bg_, flag_col):
            OP("act", lambda e: e.activation(out=Dk.t[:], in_=bg_.t[:, 256:512], func=AF.Exp, scale=1.0 / 16), r=[bg_], w=[Dk])
            OP("act", lambda e: e.activation(out=Elast.t[:], in_=bg_.t[:, 0:2], func=AF.Exp, scale=-1.0 / 16), r=[bg_], w=[Elast])
            OP("dve", lambda e: e.tensor_scalar_mul(Vg.t[:, 0:256], bkg.t[:, 256:512], flg.t[:, flag_col:flag_col + 1]),
               r=[bkg, flg], w=[VgH[0]])
            OP("dve", lambda e: e.tensor_scalar_mul(Vg.t[:, 256:512], bvg.t[:, 0:256], flg.t[:, flag_col:flag_col + 1]),
               r=[bvg, flg], w=[VgH[1]])
            OP("dve", lambda e: e.tensor_tensor(kd.t[:], bkg.t[:, 0:256], Dk.t[:], ALU.mult), r=[bkg, Dk], w=[kd])
            unpin(bkg, bvg)

        def pre_U():
            bs = nf()
            for j in range(2):
                OP("pe", lambda e, j=j: e.matmul(bs.t[:, 256 * j:256 * (j + 1)], kd.t[:, 128 * j:128 * (j + 1)],
                                                 Vg.t[:, 256 * j:256 * (j + 1)], start=True, stop=True), r=[kd, VgH[j]], w=[bs])
            for j in range(2):
                for hl in range(2):
                    OP("dve", lambda e, j=j, hl=hl: e.scalar_tensor_tensor(
                        out=Sst.t[64 * hl:64 * (hl + 1), j, :], in0=Sst.t[64 * hl:64 * (hl + 1), j, :],
                        scalar=Elast.t[64 * hl:64 * (hl + 1), j:j + 1],
                        in1=bs.t[64 * hl:64 * (hl + 1), 256 * j + 128 * hl:256 * j + 128 * (hl + 1)],
                        op0=ALU.mult, op1=ALU.add), r=[SstQ[j][hl], Elast, bs], w=[SstQ[j][hl]])

        pre_idx = list(range(NPRE - n_pre, NPRE))
        items = [("pre", i) for i in pre_idx] + tiles[:1]
        fr = {}

        def fa(k):
            it = items[k]
            xs_, src_ = front_a(x_of(it), it[0] == "smp")
            fr[k] = {"xs": xs_, "src": src_, "smp": it[0] == "smp"}

        def fb(k):
            fr[k]["h"] = front_b(fr[k]["src"], fr[k]["smp"])

        def x_of(t):
            return {"pre": lambda: xp[t[1]], "main": lambda: xm[t[1]], "smp": lambda: xs_d}[t[0]]()

        cur_fr = {}
        if n_pre > 0:
            fa(0); fb(0)
            if len(items) > 1:
                fa(1); fb(1)
            PB = {0: (pre_P1(fr[0]["h"]), pre_P2(fr[0]["h"]))}
            for k in range(n_pre):
                bkg, bvg = PB[k]
                bg_ = pre_G1(bvg)
                if k + 2 < len(items):
                    fa(k + 2)
                nb1 = pre_P1(fr[k + 1]["h"]) if k + 1 < n_pre else None
                pre_G2a(bg_)
                nb2 = pre_P2(fr[k + 1]["h"]) if k + 1 < n_pre else None
                PB[k + 1] = (nb1, nb2)
                pre_G2b(bkg, bvg, bg_, pre_idx[k])
                if k + 2 < len(items):
                    fb(k + 2)
                pre_U()
                if k == n_pre - 1:
                    for hl in range(2):
                        OP("pool", lambda e, hl=hl: e.tensor_copy(Sbm.t[64 * hl:64 * (hl + 1), :, hl, :], Sst.t[64 * hl:64 * (hl + 1), :, :]),
                           r=[SstQ[0][hl], SstQ[1][hl]], w=[SbmH[hl]])
                    attn_kv(fr[k]["h"], 16, False)
            if len(items) > n_pre:
                cur_fr = {"xs": fr[n_pre]["xs"], "h": fr[n_pre]["h"]}

        if tiles and not cur_fr:
            xs0, src0 = front_a(x_of(tiles[0]), tiles[0][0] == "smp")
            cur_fr = {"xs": xs0, "h": front_b(src0, tiles[0][0] == "smp")}
        for ti, t in enumerate(tiles):
            nxt = tiles[ti + 1] if ti + 1 < len(tiles) else None
            if nxt is not None and nxt[0] == "smp":
                nxt = None
            if t[0] == "smp":
                o_ = st["x"] ^ 1
                xo_off = X[o_].t.manual_sbuf_range[0]
                for k_, nm in enumerate(("A", "S")):
                    tt_ = T(nc.alloc_sbuf_tensor_at("smod" + nm, [128, D], BF16, offset=xo_off + 2048 * k_), "smod" + nm)
                    tt_.b = X[o_].b
                    smod[nm] = tt_
                    DMA("sp", tt_.t[:], modx[k_], sem_mx, w=[tt_])
                DMA("sp", Gbc.t[:], modx[2], sem_mx, w=[Gbc])
                xs0, src0 = front_a(xs_d, True)
                cur_fr = {"xs": xs0, "h": front_b(src0, True)}
            nx = {}

            if nxt is not None:
                nx["xs_pre"] = front_dma(x_of(nxt))

            def hook_a(nxt=nxt, nx=nx):
                if nxt is not None:
                    nx["xs"], nx["src"] = front_a(x_of(nxt), nxt[0] == "smp", nx["xs_pre"])

            def tinfo(tt_):
                if tt_[0] == "main":
                    return tt_[1], False, tt_[1] == NMAIN - 1
                return 17, True, False

            def hook_b(nxt=nxt, nx=nx):
                if nxt is not None:
                    nx["h"] = front_b(nx["src"], nxt[0] == "smp")
                    ri, sm, la = tinfo(nxt)
                    defer_begin()
                    nx["cur"] = attn_prep(nx["h"], ri, sm, la)
                    defer_end()

            if t[0] == "main":
                i = t[1]
                full_tile(cur_fr["xs"], cur_fr["h"], ym[i], i, False, i == 0, i == NMAIN - 1, hook_a, hook_b, cur_fr.get("cur"))
            else:
                full_tile(cur_fr["xs"], cur_fr["h"], ys, 17, True, False, False, hook_a, hook_b, cur_fr.get("cur"))
            defer_flush()
            cur_fr = nx
        for hl in range(2):
            dst = pst_o.rearrange("(j hl) k v -> hl k j v", hl=2)[hl]
            DMA("sp", dst, Sst.t[64 * hl:64 * (hl + 1), :, :], sem_o, r=[SstQ[0][hl], SstQ[1][hl]])

    except _Stop:
        pass
    S.wait_all("sp")
    with nc.allow_low_precision("bf16 matmul operands, fp32 accumulation"):
        S.emit()
    es.close()
    return nc


_NC_CACHE = {}


def _consts():
    idx = np.arange(128)
    U = (idx[:, None] <= idx[None, :]).astype(np.float32)
    L = (idx[:, None] >= idx[None, :]).astype(np.float32)
    same = (idx[:, None] // 8 == idx[None, :] // 8).astype(np.float32)
    Ub = U * same
    Um1 = U - 1.0
    Ubm1 = (Ub - same).astype(np.float32)
    I = np.eye(128, dtype=np.float32)
    SelS = np.zeros((17, 128), np.float32)
    for t in range(128):
        SelS[1 + t // 8, t] = 1.0
    SelP = np.zeros((17, 128), np.float32)
    SelP[0, :] = 1.0
    OH = (idx[:, None] // 8 == np.arange(16)[None, :]).astype(np.float32)
    C = (idx[:, None] >= np.arange(8)[None, :]).astype(np.float32)
    ones = np.ones((128, 128), np.float32)
    return dict(cU=U, cL=L, cUb=Ub, cUm1=Um1, cUbm1=Ubm1, cI=I, cSelS=SelS, cSelP=SelP, cOH=OH, cC=C, cOnes=ones)


def _rope_table(pos):
    half = 32
    inv = (1.0 / (np.float32(10000.0) ** (np.arange(half, dtype=np.float32) / np.float32(half)))).astype(np.float32)
    ang = pos.astype(np.float32)[:, None] * inv[None, :]
    return np.concatenate([np.cos(ang), np.sin(ang)], axis=-1).astype(np.float32)


def kernel(x_prompt, x_sample, cache_win_k, cache_win_v, state_gla, c_prompt, c_sample,
           norm_g, w_ada, b_ada, w_in, q_norm_g, k_norm_g, attn_sinks, w_gla_gate, b_gla_gate,
           gla_norm_g, w_branch_att, w_branch_gla, w_out):
    f = lambda a: np.ascontiguousarray(np.asarray(a, dtype=np.float32))
    x_prompt, x_sample = f(x_prompt), f(x_sample)
    ck, cv, st0 = f(cache_win_k)[0], f(cache_win_v)[0], f(state_gla)[0]
    c_prompt, c_sample = f(c_prompt), f(c_sample)
    if "nc" not in _NC_CACHE:
        _NC_CACHE["nc"] = build_nc()
    nc = _NC_CACHE["nc"]
    consts = _consts()
    shared = dict(
        normg=f(np.asarray(norm_g)[0].reshape(8, 128).T), normrow=f(np.asarray(norm_g)[0].reshape(1, D)),
        w_ada=f(w_ada)[0], b_ada=f(b_ada)[0].reshape(1, 3072), w_in=f(w_in)[0],
        gqk=f(np.broadcast_to(np.concatenate([np.tile(np.asarray(q_norm_g)[0], 8), np.tile(np.asarray(k_norm_g)[0], 2)])[None, :], (128, 640))),
        sinks=f(np.repeat(np.asarray(attn_sinks)[0], 128).reshape(1, 1024)),
        w_gla_gate=f(w_gla_gate)[0], b_gla_gate=f(b_gla_gate)[0].reshape(1, 256),
        gdv=f(np.asarray(gla_norm_g)[0].reshape(128, 1)),
        w_branch_att=f(w_branch_att)[0], w_branch_gla=f(w_branch_gla)[0], w_out=f(w_out)[0], **consts)
    in_maps = []
    for c in range(NCORES):
        b, p = c // 4, c % 4
        seq = x_prompt[b].reshape(64, 128, D)
        xm = seq[16 * p:16 * (p + 1)]
        xp = np.zeros((NPRE, 128, D), np.float32)
        npv = 16 * p
        if npv:
            xp[NPRE - npv:] = seq[0:npv]
        flags = np.zeros((128, 64), np.float32)
        flags[:, NPRE - npv:NPRE] = 1.0
        flags[:, 48] = 1.0 if p > 0 else 0.0
        rope = np.zeros((18, 128, 64), np.float32)
        for i in range(16):
            rope[i] = _rope_table(2048 * p + 128 * i + np.arange(128))
        rope[16] = _rope_table(np.maximum(2048 * p - 128 + np.arange(128), 0))
        rope[17] = _rope_table(16384 + (np.arange(128) % 8))
        sl = slice(16 * c, 16 * (c + 1))
        m = dict(xm=np.ascontiguousarray(xm), xp=xp, xs=np.ascontiguousarray(x_sample[sl].reshape(128, D)),
                 cvec=np.ascontiguousarray(np.concatenate([c_prompt[b:b + 1], c_sample[sl]], 0)),
                 flags=flags, rope=rope,
                 ck=np.ascontiguousarray(ck[sl].reshape(16, 128, 128)), cv=np.ascontiguousarray(cv[sl].reshape(16, 128, 128)),
                 st0=np.ascontiguousarray(st0[sl]))
        m.update(shared)
        in_maps.append(m)
    res = run_bass_kernel_spmd(nc, in_maps, core_ids=list(range(NCORES)))
    R = res.results
    _NC_CACHE['last'] = R
    y_prompt = np.zeros((2, 8192, D), np.float32)
    for c in range(NCORES):
        b, p = c // 4, c % 4
        y_prompt[b, 2048 * p:2048 * (p + 1)] = R[c]["ym"].reshape(2048, D)
    y_sample = np.concatenate([R[c]["ys"].reshape(16, 8, D) for c in range(NCORES)], 0)
    pk = np.stack([R[3]["pk"], R[7]["pk"]], 0).reshape(1, 2, 128, 2, 64)
    pv = np.stack([R[3]["pv"], R[7]["pv"]], 0).reshape(1, 2, 128, 2, 64)
    pst = np.stack([R[3]["pst"], R[7]["pst"]], 0).reshape(1, 2, 4, 64, 128)
    sk = np.concatenate([R[c]["sk"] for c in range(NCORES)], 0).reshape(1, 128, 128, 2, 64)
    sv = np.concatenate([R[c]["sv"] for c in range(NCORES)], 0).reshape(1, 128, 128, 2, 64)
    sst = np.concatenate([R[c]["sst"] for c in range(NCORES)], 0).reshape(1, 128, 4, 64, 128)
    return (y_prompt, y_sample, pk.astype(np.float32), pv.astype(np.float32), pst.astype(np.float32),
            sk.astype(np.float32), sv.astype(np.float32), sst.astype(np.float32))
```

```python
import contextlib
import numpy as np
import concourse.bass as bass
import concourse.mybir as mybir
from concourse.bass_utils import run_bass_kernel_spmd

F32 = mybir.dt.float32
BF16 = mybir.dt.bfloat16
AF = mybir.ActivationFunctionType
ALU = mybir.AluOpType
AX = mybir.AxisListType

ENGS = ("pe", "act", "dve", "pool", "sp")
NCORES = 8
D = 1024
KC = 8
NMAIN = 16
NPRE = 48
INC = 4880
QA, KA, VA, ZA, QG, KG, VG, LR, ZG, MA, MG = 0, 512, 640, 768, 1280, 1536, 1792, 2304, 2320, 2832, 3856
EPS = 1e-6


import types


def _freeze(fn):
    if fn is None or fn.__closure__ is None:
        return fn
    cells = []
    for c in fn.__closure__:
        try:
            cells.append(types.CellType(c.cell_contents))
        except ValueError:
            cells.append(c)
    return types.FunctionType(fn.__code__, fn.__globals__, fn.__name__, fn.__defaults__, tuple(cells))


class Buf:
    def __init__(self, name, excl=False):
        self.name = name
        self.excl = excl
        self.w = None
        self.r = {}


class Sched:
    def __init__(self, nc):
        self.nc = nc
        self.ops = {e: [] for e in ENGS}
        self.cnt = {}
        self.seen = {e: {} for e in ENGS}
        self.sems = {}
        for e in ENGS:
            self.cnt["e_" + e] = 0

    def dma_sem(self, name):
        k = "d_" + name
        self.cnt[k] = 0
        return k

    def _need(self, eng, ev, waits):
        if ev is None:
            return
        k, v = ev[0], ev[1]
        if k.startswith("d_"):
            v = self.cnt[k]
        if self.seen[eng].get(k, 0) >= v:
            return
        waits[k] = max(waits.get(k, 0), v)

    def op(self, eng, fn, reads=(), writes=(), dsem=None):
        waits = {}
        is_dma = dsem is not None
        for b in reads:
            if b.w is not None:
                self._need(eng, b.w, waits)
            if b.excl:
                for e2, ev in b.r.items():
                    if e2 != eng or is_dma:
                        self._need(eng, ev, waits)
        for b in writes:
            if b.w is not None and (b.w[2] != eng or is_dma or b.w[0].startswith("d_") or eng != "pe"):
                self._need(eng, b.w, waits)
            for e2, ev in b.r.items():
                if e2 != eng or is_dma or ev[0].startswith("d_") or eng != "pe":
                    self._need(eng, ev, waits)
        for k, v in waits.items():
            self.seen[eng][k] = v
        if is_dma:
            self.cnt[dsem] += 16
            ev = (dsem, self.cnt[dsem], eng)
            inc = (dsem, 16)
        else:
            k = "e_" + eng
            self.cnt[k] += 1
            ev = (k, self.cnt[k], eng)
            inc = (k, 1)
        for b in reads:
            key = eng if not is_dma else "dma_" + dsem
            b.r[key] = (ev[0], ev[1])
        for b in writes:
            b.w = ev
            b.r = {}
        self.ops[eng].append((list(waits.items()), _freeze(fn), inc))
        return ev

    def wait_all(self, eng, exclude=()):
        waits = []
        for k, v in self.cnt.items():
            if k in exclude:
                continue
            if v > 0 and k != "e_" + eng and self.seen[eng].get(k, 0) < v:
                waits.append((k, v))
                self.seen[eng][k] = v
        self.ops[eng].append((waits, None, None))

    def emit(self):
        nc = self.nc
        with contextlib.ExitStack() as st:
            for k in self.cnt:
                self.sems[k] = st.enter_context(nc.semaphore(k))
            block = st.enter_context(nc.Block())
            handles = {"pe": block.tensor, "act": block.scalar, "dve": block.vector,
                       "pool": block.gpsimd, "sp": block.sync}
            sems = self.sems
            for e in ENGS:
                ops = self.ops[e]

                def body(eh, ops=ops):
                    for waits, fn, inc in ops:
                        for k, v in waits:
                            eh.wait_ge(sems[k], v)
                        if fn is not None:
                            ins = fn(eh)
                            ins.then_inc(sems[inc[0]], inc[1])
                handles[e](body)


class T:
    def __init__(self, t, name, excl=False):
        self.t = t
        self.b = Buf(name, excl)

    def __getitem__(self, k):
        return self.t[k]


def bc_ap(ap, dims):
    return bass.AP(ap.tensor, ap.offset, [list(ap.ap[0])] + [list(d) for d in dims])


class _Stop(Exception):
    pass


def build_nc(n_pre=NPRE, n_main=NMAIN, do_sample=True, stage=None, debug=False):
    def chk(n):
        if stage == n:
            raise _Stop()
    nc = bass.Bass("TRN2", target_bir_lowering=False)
    S = Sched(nc)
    es = contextlib.ExitStack()

    def din(name, shape):
        return nc.dram_tensor(name, list(shape), F32, kind="ExternalInput").ap()

    def dout(name, shape):
        return nc.dram_tensor(name, list(shape), F32, kind="ExternalOutput").ap()

    xm = din("xm", [NMAIN, 128, D]); xp = din("xp", [NPRE, 128, D]); xs_d = din("xs", [128, D])
    cvec = din("cvec", [17, D]); flags_d = din("flags", [128, 64]); rope_d = din("rope", [18, 128, 64])
    ck_d = din("ck", [16, 128, 128]); cv_d = din("cv", [16, 128, 128]); st0_d = din("st0", [16, 4, 64, 128])
    normg_d = din("normg", [128, 8]); normrow_d = din("normrow", [1, D])
    wada_d = din("w_ada", [D, 3072]); bada_d = din("b_ada", [1, 3072])
    win_d = din("w_in", [D, INC]); gqk_d = din("gqk", [128, 640]); sinks_d = din("sinks", [1, 1024])
    wgate_d = din("w_gla_gate", [16, 256]); bgate_d = din("b_gla_gate", [1, 256]); gdv_d = din("gdv", [128, 1])
    wba_d = din("w_branch_att", [512, D]); wbg_d = din("w_branch_gla", [512, D]); wout_d = din("w_out", [D, D])
    cU = din("cU", [128, 128]); cL = din("cL", [128, 128]); cUb = din("cUb", [128, 128])
    cUm1 = din("cUm1", [128, 128]); cUbm1 = din("cUbm1", [128, 128]); cI = din("cI", [128, 128])
    cSelS = din("cSelS", [17, 128]); cSelP = din("cSelP", [17, 128]); cOH = din("cOH", [128, 16])
    cC = din("cC", [128, 8]); cOnes = din("cOnes", [128, 128])

    ym = dout("ym", [NMAIN, 128, D]); ys = dout("ys", [128, D])
    pk_o = dout("pk", [128, 128]); pv_o = dout("pv", [128, 128]); pst_o = dout("pst", [4, 64, 128])
    sk_o = dout("sk", [16, 128, 128]); sv_o = dout("sv", [16, 128, 128]); sst_o = dout("sst", [16, 4, 64, 128])

    ptr = [16640]
    LIMIT = nc.SBUF_PARTITION_SIZE_BYTES

    def sb(name, shape, dt=F32, at=None):
        size = int(np.prod(shape[1:])) * (4 if dt == F32 else 2)
        size = (size + 31) // 32 * 32
        if at is None:
            off = ptr[0]
            ptr[0] += size
            assert ptr[0] <= LIMIT, (name, ptr[0], LIMIT)
        else:
            off = at[0]
            at[0] += size
        t = nc.alloc_sbuf_tensor_at(name, list(shape), dt, offset=off)
        return T(t, name)

    pf = [T(es.enter_context(nc.psum_tensor("pf%d" % i, [128, 512], F32)), "pf%d" % i, True) for i in range(6)]
    pbk = [T(es.enter_context(nc.psum_tensor("pb%d" % i, [128, 1024], BF16)), "pb%d" % i, True) for i in range(2)]
    dq = {"rec": None, "queue": [], "every": 5, "cnt": 0, "banks": []}

    rot = {"f": 0, "b": 0}

    pinned = set()

    pinned_b = set()

    def nf():
        while True:
            rot["f"] = (rot["f"] + 1) % 6
            if rot["f"] not in pinned:
                if dq["rec"] is not None:
                    pinned.add(rot["f"]); dq["banks"].append(pf[rot["f"]])
                return pf[rot["f"]]

    def pin(*banks):
        for b_ in banks:
            pinned.add(pf.index(b_))

    def unpin(*banks):
        for b_ in banks:
            pinned.discard(pf.index(b_))

    def nb():
        while True:
            rot["b"] = (rot["b"] + 1) % 2
            if rot["b"] not in pinned_b:
                if dq["rec"] is not None:
                    pinned_b.add(rot["b"]); dq["banks"].append(pbk[rot["b"]])
                return pbk[rot["b"]]

    def _reg(eng, fn, reads, writes, dsem):
        return S.op(eng, fn, reads=reads, writes=writes, dsem=dsem)

    def _emit(eng, fn, reads, writes, dsem):
        if dq["rec"] is not None:
            dq["rec"].append((eng, _freeze(fn), reads, writes, dsem))
            return None
        ev = _reg(eng, fn, reads, writes, dsem)
        if dq["queue"]:
            dq["cnt"] += 1
            if dq["cnt"] % dq["every"] == 0:
                _reg(*dq["queue"].pop(0))
                if not dq["queue"]:
                    _release_deferred()
        return ev

    def _release_deferred():
        for b_ in dq["banks"]:
            if b_ in pf:
                pinned.discard(pf.index(b_))
            else:
                pinned_b.discard(pbk.index(b_))
        dq["banks"] = []

    def defer_begin():
        assert dq["rec"] is None and not dq["queue"]
        dq["rec"] = []

    def defer_end():
        dq["queue"] = dq["rec"]
        dq["rec"] = None
        dq["cnt"] = 0
        if not dq["queue"]:
            _release_deferred()

    def defer_flush():
        while dq["queue"]:
            _reg(*dq["queue"].pop(0))
        _release_deferred()

    def OP(eng, fn, r=(), w=()):
        return _emit(eng, fn, [x.b for x in r], [x.b for x in w], None)

    def DMA(eng, out_ap, in_ap, sem, r=(), w=()):
        return _emit(eng, lambda e: e.dma_start(out=out_ap, in_=in_ap), [x.b for x in r], [x.b for x in w], sem)

    dram_sink = T(None, "dram_out")
    dbg_sem = S.dma_sem("dbg")
    dbg_seen = set()

    def DBG(name, tobj, ap):
        if not debug or name in dbg_seen:
            return
        dbg_seen.add(name)
        shp = list(ap.shape)
        d = nc.dram_tensor("dbg_" + name, shp, F32, kind="ExternalOutput").ap()
        if tobj.b.excl:
            raise ValueError("dump SBUF only")
        DMA("pool", d, ap, dbg_sem, r=[tobj])


    Win = sb("Win", [128, KC, INC], BF16)
    WB = [T(Win.t, "WB%d" % i) for i in range(3)]
    Wba = sb("Wba", [128, 4, D], BF16); Wbg = sb("Wbg", [128, 4, D], BF16); Wout = sb("Wout", [128, KC, D], BF16)

    def wbuf(c):
        if 512 <= c < 768 or 1536 <= c < 2320:
            return WB[0]
        if c < 1536:
            return WB[1]
        return WB[2]

    U4b = sb("U4b", [128, 4, 128], BF16); L4b = sb("L4b", [128, 4, 128], BF16); L4fb = sb("L4fb", [128, 4, 128], BF16)
    Ub4b = sb("Ub4b", [128, 4, 128], BF16); C4b = sb("C4b", [128, 8, 8], BF16)
    Uf = sb("Uf", [128, 128]); Um1f = sb("Um1f", [128, 128]); Ubf = sb("Ubf", [128, 128]); Ubm1f = sb("Ubm1f", [128, 128])
    onesf = sb("onesf", [128, 128]); onesb = sb("onesb", [128, 128], BF16); identb = sb("identb", [128, 128], BF16)
    OH = sb("OH", [128, 16]); selS = sb("selS", [17, 128]); selP = sb("selP", [17, 128])
    Gqk = sb("Gqk", [128, 640]); esink = sb("esink", [1, 1024]); gdv = sb("gdv", [128, 1])
    acs = sb("acs", [128, 16]); Gbc = sb("Gbc", [128, D], BF16)
    flg = sb("flg", [128, 64]); WgF = sb("WgF", [17, 256]); WgA = sb("WgA", [17, 256], BF16)
    Sst = sb("Sst", [128, 2, 128])
    VgH = None
    SstQ = [[T(Sst.t, "Sst_%d_%d" % (_j, _hl)) for _hl in range(2)] for _j in range(2)]
    SstAll = [SstQ[0][0], SstQ[0][1], SstQ[1][0], SstQ[1][1]]
    LRa = sb("LRa", [17, 128], BF16)
    ov0 = ptr[0]
    ov = [ov0]
    WadaC = [sb("WadaC%d" % i, [128, KC, 512], BF16, at=ov) for i in range(2)]
    badaC = [sb("badaC%d" % i, [1, 512], F32, at=ov) for i in range(2)]
    modtm = sb("modtm", [17, 3072], F32, at=ov)
    CV = sb("CV", [17, D], F32, at=ov); CE = sb("CE", [17, D], F32, at=ov); CSb = sb("CSb", [17, D], BF16, at=ov)
    sTc = sb("sTc", [128, KC, 32], BF16, at=ov); Grow = sb("Grow", [17, D], F32, at=ov)
    sinkrow = sb("sinkrow", [1, 1024], F32, at=ov)
    idf_s = sb("idf_s", [128, 128], F32, at=ov); cLf_s = sb("cLf_s", [128, 128], F32, at=ov); cCf_s = sb("cCf_s", [128, 8], F32, at=ov)
    Atm = sb("Atm", [128, D], BF16, at=ov); Stm = sb("Stm", [128, D], BF16, at=ov); Gtm = sb("Gtm", [128, D], BF16, at=ov)
    modx = nc.dram_tensor("modx", [3, 128, D], BF16).ap()
    assert ov[0] <= LIMIT
    X = [sb("X%d" % i, [128, D]) for i in range(2)]
    XN = sb("XN", [128, D], BF16)
    junk = XN; XH = XN; ss = sb("ss", [128, 1]); rstd = sb("rstd", [128, 1])
    hT = [sb("hT%d" % i, [128, KC, 128], BF16) for i in range(2)]
    for _h in hT:
        _h.k = [T(_h.t, _h.b.name + "_k%d" % _k) for _k in range(KC)]
    ropeT = [sb("rope%d" % i, [128, 64]) for i in range(2)]
    t1 = sb("t1", [128, 640]); tu = sb("tu", [128, 640]); tw = sb("tw", [128, 640]); QKb = sb("QKb", [128, 640], BF16)
    sq640 = sb("sq640", [128, 640], BF16); ssq = sb("ssq", [128, 10]); rq = sb("rq", [128, 10])
    qT = sb("qT", [128, 4, 128], BF16); kT = [sb("kT%d" % i, [128, 128], BF16) for i in range(2)]
    Vd = [sb("Vd%d" % i, [128, 2, 2, 64], BF16) for i in range(2)]; Vf = sb("Vf", [128, 128])
    PT = [sb("PT%d" % i, [128, 512], BF16) for i in range(2)]
    Zs = sb("Zs", [128, 4, 128], BF16); Rr = sb("Rr", [128, 512]); tt = sb("tt", [128, 512], BF16)
    OZ = sb("OZ", [128, 4, 128], BF16)
    Lg = sb("Lg", [128, 256]); Dk = sb("Dk", [128, 256]); kd = sb("kd", [128, 256], BF16)
    Eq = sb("Eq", [128, 2, 128]); Ek = sb("Ek", [128, 2, 128]); Elast = sb("Elast", [128, 2])
    qe = sb("qe", [128, 2, 128], BF16); ke = sb("ke", [128, 2, 2, 128], BF16); Sbm = sb("Sbm", [128, 2, 2, 128], BF16)
    Vg = sb("Vg", [128, 512], BF16); Am = sb("Am", [128, 4, 128], BF16)
    sqg = sb("sqg", [128, 512], BF16); rsg = Rr
    Zg = sb("Zg", [128, 4, 128], BF16); OGZ = sb("OGZ", [128, 4, 128], BF16)
    Et = sb("Et", [128, 512]); Et2 = sb("Et2", [128, 512])
    Mg = sb("Mg", [128, 16, 128], BF16); mT = sb("mT", [128, KC, 128], BF16); brt = sb("brt", [128, 512]); og = brt
    Y = sb("Y", [128, D])
    YH = [T(Y.t, "Y_h%d" % _i) for _i in range(2)]
    t1P = [T(t1.t, "t1_p%d" % _i) for _i in range(2)]
    sqP = [T(sq640.t, "sq_p%d" % _i) for _i in range(2)]
    QKbP = [T(QKb.t, "QKb_p%d" % _i) for _i in range(2)]
    tuH = [T(tu.t, "tu_h%d" % _i) for _i in range(2)]
    twH = [T(tw.t, "tw_h%d" % _i) for _i in range(2)]
    keH = [T(ke.t, "ke_h%d" % _i) for _i in range(2)]
    SbmH = [T(Sbm.t, "Sbm_h%d" % _i) for _i in range(2)]
    VgH = [T(Vg.t, "Vg_h%d" % _i) for _i in range(2)]
    OZc = [T(OZ.t, "OZ_c%d" % _i) for _i in range(4)]
    MgQ = [T(Mg.t, "Mg_q%d" % _i) for _i in range(4)]
    _xn_off = XN.t.manual_sbuf_range[0]
    MTM = [T(nc.alloc_sbuf_tensor_at("MTM%d" % _i, [128, 512], BF16, offset=_xn_off + 1024 * _i), "MTM%d" % _i) for _i in range(2)]
    _mt_off = mT.t.manual_sbuf_range[0]
    for _i in range(2):
        _t = T(nc.alloc_sbuf_tensor_at("PTx%d" % _i, [128, 512], BF16, offset=_mt_off + 1024 * _i), "PTx%d" % _i)
        _t.b = mT.b
        PT.append(_t)
    CKs = [sb("CKs%d" % i, [128, 128]) for i in range(4)]; CVs = [sb("CVs%d" % i, [128, 128]) for i in range(4)]
    CKb = [sb("CKb%d" % i, [128, 128], BF16) for i in range(2)]
    KcT = [sb("KcT%d" % i, [128, 128], BF16) for i in range(2)]; Vcd = [sb("Vcd%d" % i, [128, 2, 2, 64], BF16) for i in range(2)]
    PTc = [sb("PTc%d" % i, [128, 2, 4, 8], BF16) for i in range(2)]
    S0s = [sb("S0s%d" % i, [128, 2, 128]) for i in range(4)]; S0b = [sb("S0b%d" % i, [128, 2, 2, 128], BF16) for i in range(4)]
    kdm = [sb("kdm%d" % i, [128, 256], BF16) for i in range(2)]; SN = S0s
    ElS = sb("ElS", [128, 2, 16]); qTs = sb("qTs", [128, 16, 4, 8], BF16)

    sem_c = S.dma_sem("c"); sem_mx = S.dma_sem("mx"); sem_c0 = S.dma_sem("c0"); sem_cp = S.dma_sem("cp"); sem_c0p = S.dma_sem("c0p"); sem_w = [S.dma_sem("w%d" % i) for i in range(3)]; sem_wb = S.dma_sem("wb")
    sem_ada = [S.dma_sem("ada%d" % i) for i in range(2)]; sem_bada = [S.dma_sem("bada%d" % i) for i in range(2)]
    sem_x = [S.dma_sem("x%d" % i) for i in range(2)]; sem_y = S.dma_sem("y"); sem_r = [S.dma_sem("r%d" % i) for i in range(2)]
    sem_o = S.dma_sem("o"); sem_ck = [S.dma_sem("ck%d" % i) for i in range(4)]; sem_s0 = [S.dma_sem("s0%d" % i) for i in range(4)]
    sem_sn = [S.dma_sem("sn%d" % i) for i in range(4)]

    try:
        DMA("sp", CV.t[:], cvec, sem_c0, w=[CV])
        DMA("sp", onesf.t[:], cOnes, sem_c0, w=[onesf])
        DMA("sp", idf_s.t[:], cI, sem_c0, w=[idf_s])
        OP("dve", lambda e: e.tensor_copy(identb.t[:], idf_s.t[:]), r=[idf_s], w=[identb])
        DMA("sp", cLf_s.t[:], cL, sem_c, w=[cLf_s])
        DMA("sp", cCf_s.t[:], cC, sem_c, w=[cCf_s])
        for (dst, src) in [(Uf, cU), (Um1f, cUm1), (Ubf, cUb), (Ubm1f, cUbm1), (OH, cOH),
                           (selS, cSelS), (selP, cSelP), (Gqk, gqk_d), (gdv, gdv_d), (flg, flags_d),
                           (sinkrow, sinks_d)]:
            DMA("sp", dst.t[:], src, sem_c, w=[dst])
        chk(10)
        DMA("sp", Grow.t[:], bass.AP(normrow_d.tensor, normrow_d.offset, [[0, 17], [1, D]]), sem_c, w=[Grow])
        chk(11)
        chk(14)
        DMA("sp", WgF.t[0:16, :], wgate_d, sem_c, w=[WgF])
        DMA("sp", WgF.t[16:17, :], bgate_d, sem_c, w=[WgF])
        OP("dve", lambda e: e.tensor_copy(WgA.t[:], WgF.t[:]), r=[WgF], w=[WgA])

        wada_v = wada_d.rearrange("(kc p) c -> p kc c", p=128)
        win_v = win_d.rearrange("(kc p) c -> p kc c", p=128)

        def load_ada(n):
            DMA("pool", WadaC[n % 2].t[:], wada_v[:, :, n * 512:(n + 1) * 512], sem_ada[n % 2], w=[WadaC[n % 2]])
            DMA("sp", badaC[n % 2].t[:], bada_d[:, n * 512:(n + 1) * 512], sem_bada[n % 2], w=[badaC[n % 2]])

        chk(1)
        load_ada(0); load_ada(1)
        DMA("pool", Win.t[:, :, 512:768], win_v[:, :, 512:768], sem_w[0], w=[WB[0]])
        DMA("pool", Win.t[:, :, 1536:2320], win_v[:, :, 1536:2320], sem_w[0], w=[WB[0]])

        OP("dve", lambda e: e.tensor_copy(onesb.t[:], onesf.t[:]), r=[onesf], w=[onesb])
        OP("dve", lambda e: e.tensor_copy(U4b.t[:], bc_ap(Uf.t[:, 0:1], [(0, 4), (1, 128)])), r=[Uf], w=[U4b])
        OP("dve", lambda e: e.tensor_copy(L4b.t[:], bc_ap(cLf_s.t[:, 0:1], [(0, 4), (1, 128)])), r=[cLf_s], w=[L4b])
        OP("dve", lambda e: e.tensor_copy(Ub4b.t[:], bc_ap(Ubf.t[:, 0:1], [(0, 4), (1, 128)])), r=[Ubf], w=[Ub4b])
        OP("dve", lambda e: e.tensor_copy(C4b.t[:], bc_ap(cCf_s.t[:, 0:1], [(0, 8), (1, 8)])), r=[cCf_s], w=[C4b])
        OP("pool", lambda e: e.memset(LRa.t[:], 1.0), w=[LRa])
        OP("pool", lambda e: e.memset(Sst.t[:], 0.0), w=SstAll)

        OP("dve", lambda e: e.tensor_scalar_mul(L4fb.t[:], L4b.t[:], flg.t[:, 48:49]), r=[L4b, flg], w=[L4fb])
        OP("dve", lambda e: e.tensor_scalar_mul(Gqk.t[:, 0:512], Gqk.t[:, 0:512], 0.125), r=[Gqk], w=[Gqk])
        OP("act", lambda e: e.activation(out=esink.t[:], in_=sinkrow.t[:], func=AF.Exp), r=[sinkrow], w=[esink])

        chk(2)
        OP("act", lambda e: e.activation(out=CE.t[:], in_=CV.t[:], func=AF.Exp, scale=-1.0), r=[CV], w=[CE])
        OP("dve", lambda e: e.tensor_scalar_add(CE.t[:], CE.t[:], 1.0), r=[CE], w=[CE])
        OP("dve", lambda e: e.reciprocal(CE.t[:], CE.t[:]), r=[CE], w=[CE])
        OP("dve", lambda e: e.tensor_tensor(CSb.t[:], CV.t[:], CE.t[:], ALU.mult), r=[CV, CE], w=[CSb])
        pb0 = nb()
        for kc in range(KC):
            OP("pe", lambda e, kc=kc: e.transpose(pb0.t[:, kc * 32:kc * 32 + 17], CSb.t[0:17, kc * 128:(kc + 1) * 128],
                                                   identb.t[0:17, 0:17]), r=[CSb, identb], w=[pb0])
        OP("dve", lambda e: e.tensor_copy(sTc.t[:, :, 0:17], pb0.t[:, 0:256].rearrange("p (k c) -> p k c", c=32)[:, :, 0:17]),
           r=[pb0], w=[sTc])
        for n in range(6):
            bk = nf()
            for kc in range(KC):
                OP("pe", lambda e, kc=kc, n=n, bk=bk: e.matmul(bk.t[0:17, :], sTc.t[:, kc, 0:17], WadaC[n % 2].t[:, kc, :],
                                                               start=(kc == 0), stop=False), r=[sTc, WadaC[n % 2]], w=[bk])
            OP("pe", lambda e, n=n, bk=bk: e.matmul(bk.t[0:17, :], onesf.t[0:1, 0:17], badaC[n % 2].t[0:1, :],
                                                    start=False, stop=True), r=[onesf, badaC[n % 2]], w=[bk])
            OP("act", lambda e, n=n, bk=bk: e.copy(modtm.t[0:17, n * 512:(n + 1) * 512], bk.t[0:17, :]), r=[bk], w=[modtm])
            if n + 2 < 6:
                load_ada(n + 2)
        chk(3)
        DMA("pool", Win.t[:, :, 0:512], win_v[:, :, 0:512], sem_w[1], w=[WB[1]])
        DMA("pool", Win.t[:, :, 768:1536], win_v[:, :, 768:1536], sem_w[1], w=[WB[1]])
        for c0 in range(2320, INC, 640):
            DMA("pool", Win.t[:, :, c0:c0 + 640], win_v[:, :, c0:c0 + 640], sem_w[2], w=[WB[2]])
        DMA("pool", Wba.t[:], wba_d.rearrange("(kc p) c -> p kc c", p=128), sem_wb, w=[Wba])
        DMA("pool", Wbg.t[:], wbg_d.rearrange("(kc p) c -> p kc c", p=128), sem_wb, w=[Wbg])
        DMA("pool", Wout.t[:], wout_d.rearrange("(kc p) c -> p kc c", p=128), sem_wb, w=[Wout])

        chk(4)
        OP("dve", lambda e: e.tensor_scalar_add(modtm.t[:, 1024:2048], modtm.t[:, 1024:2048], 1.0), r=[modtm], w=[modtm])
        OP("dve", lambda e: e.tensor_tensor(modtm.t[:, 1024:2048], modtm.t[:, 1024:2048], Grow.t[:], ALU.mult),
           r=[modtm, Grow], w=[modtm])
        for (dst, sel, c0) in [(Atm, selS, 1024), (Stm, selS, 0), (Gtm, selS, 2048), (Gbc, selP, 2048)]:
            for n in range(2):
                bk = nf()
                OP("pe", lambda e, bk=bk, sel=sel, c0=c0, n=n: e.matmul(bk.t[:, :], sel.t[0:17, :],
                                                                        modtm.t[0:17, c0 + n * 512:c0 + (n + 1) * 512],
                                                                        start=True, stop=True), r=[sel, modtm], w=[bk])
                OP("act", lambda e, bk=bk, dst=dst, n=n: e.copy(dst.t[:, n * 512:(n + 1) * 512], bk.t[:, :]), r=[bk], w=[dst])
        for k_, t_ in enumerate((Atm, Stm, Gtm)):
            DMA("sp", modx[k_], t_.t[:], sem_mx, r=[t_])
        bk = nf()
        for kc in range(KC):
            OP("pe", lambda e, kc=kc, bk=bk: e.matmul(bk.t[:, kc:kc + 1], modtm.t[0:1, 1024 + kc * 128:1024 + (kc + 1) * 128],
                                                      onesf.t[0:1, 0:1], start=True, stop=True), r=[modtm, onesf], w=[bk])
            OP("pe", lambda e, kc=kc, bk=bk: e.matmul(bk.t[:, 8 + kc:9 + kc], modtm.t[0:1, kc * 128:(kc + 1) * 128],
                                                      onesf.t[0:1, 0:1], start=True, stop=True), r=[modtm, onesf], w=[bk])
        OP("dve", lambda e, bk=bk: e.tensor_copy(acs.t[:], bk.t[:, 0:16]), r=[bk], w=[acs])

        chk(5)
        for e_ in ENGS:
            S.wait_all(e_, exclude=set(sem_w) | {sem_wb})
        OP("pool", lambda e: e.memset(ke.t[:], 0.0), w=keH)
        OP("pool", lambda e: e.memset(Sbm.t[:], 0.0), w=SbmH)
        for i_ in range(4):
            OP("pool", lambda e, i_=i_: e.memset(S0b[i_].t[:], 0.0), w=[S0b[i_]])

        st = {"x": 0, "h": 0, "r": 0, "kv": 0}
        smod = {"A": None, "S": None}

        def front_dma(x_ap):
            xi = st["x"]; st["x"] ^= 1
            xs = X[xi]
            DMA("sp", xs.t[:], x_ap, sem_x[xi], w=[xs])
            return xs

        def front_a(x_ap, sample, xs=None):
            if xs is None:
                xs = front_dma(x_ap)
            OP("act", lambda e: e.activation(out=junk.t[:], in_=xs.t[:], func=AF.Square, accum_out=ss.t[:, 0:1]),
               r=[xs], w=[junk, ss])
            OP("act", lambda e: e.activation(out=rstd.t[:], in_=ss.t[:], func=AF.Ln, scale=1.0 / D, bias=EPS), r=[ss], w=[rstd])
            OP("act", lambda e: e.activation(out=rstd.t[:], in_=rstd.t[:], func=AF.Exp, scale=-0.5), r=[rstd], w=[rstd])
            chk(16)
            OP("pool", lambda e: e.tensor_tensor(XN.t[:], xs.t[:], bc_ap(rstd.t[:, 0:1], [(0, D)]), ALU.mult), r=[xs, rstd], w=[XN])
            DBG("rstd", rstd, rstd.t[:]); DBG("XN", XN, XN.t[:]); DBG("acs", acs, acs.t[:]); DBG("Gbc", Gbc, Gbc.t[:])
            chk(17)
            src = XN
            if sample:
                OP("dve", lambda e: e.tensor_tensor(XH.t[:], XN.t[:], smod["A"].t[:], ALU.mult), r=[XN, smod["A"]], w=[XH])
                OP("pool", lambda e: e.tensor_tensor(XH.t[:], XH.t[:], smod["S"].t[:], ALU.add), r=[XH, smod["S"]], w=[XH])
                src = XH
            return xs, src

        def front_b(src, sample):
            EV = "dve"
            bk = nb()
            bk2 = nb() if EV == "twobank" else bk
            bks = [bk, bk2]
            for kc in range(KC):
                OP("pe", lambda e, kc=kc: e.transpose(bks[kc % 2].t[:, kc * 128:(kc + 1) * 128], src.t[:, kc * 128:(kc + 1) * 128],
                                                       identb.t[:]), r=[src, identb], w=[bks[kc % 2]])
            chk(18)
            hi = st["h"]; st["h"] ^= 1
            h = hT[hi]
            if sample:
                assert bk2 is bk
                OP("dve", lambda e: e.tensor_copy(h.t[:], bk.t[:].rearrange("p (k c) -> p k c", c=128)), r=[bk], w=h.k)
            else:
                for kc in range(KC):
                    use_dve = {"mix": kc % 2 == 0, "twobank": kc % 2 == 0, "half": kc >= 4, "half2": kc < 4, "dve": True, "act": False}[EV]
                    bkc = bks[kc % 2]
                    OP("dve" if use_dve else "act",
                       (lambda e, kc=kc, bkc=bkc: e.tensor_scalar(h.t[:, kc, :], bkc.t[:, kc * 128:(kc + 1) * 128], acs.t[:, kc:kc + 1],
                                                         acs.t[:, 8 + kc:9 + kc], ALU.mult, ALU.add)) if use_dve else
                       (lambda e, kc=kc, bkc=bkc: e.activation(out=h.t[:, kc, :], in_=bkc.t[:, kc * 128:(kc + 1) * 128], func=AF.Identity,
                                                      scale=acs.t[:, kc:kc + 1], bias=acs.t[:, 8 + kc:9 + kc])),
                       r=[bkc, acs], w=[h.k[kc]])
            pass
            return h

        def proj_tm(h, c0, n, bk, off=0):
            for kc in range(KC):
                OP("pe", lambda e, kc=kc: e.matmul(bk.t[:, off:off + n], h.t[:, kc, :], Win.t[:, kc, c0:c0 + n],
                                                   start=(kc == 0), stop=(kc == KC - 1)), r=[h.k[kc], wbuf(c0)], w=[bk])

        def proj_fm(h, c0, m, bk, off):
            for kc in range(KC):
                OP("pe", lambda e, kc=kc: e.matmul(bk.t[0:m, off:off + 128], Win.t[:, kc, c0:c0 + m], h.t[:, kc, :],
                                                   start=(kc == 0), stop=(kc == KC - 1)), r=[h.k[kc], wbuf(c0)], w=[bk])

        def attn_kv(h, rope_idx, need_q, bq=None):
            ri = st["r"]; st["r"] ^= 1
            rp = ropeT[ri]
            DMA("sp", rp.t[:], rope_d[rope_idx], sem_r[ri], w=[rp])
            bkv = nf()
            proj_tm(h, KA, 256, bkv)
            nh = 10 if need_q else 2
            c0 = 0 if need_q else 512
            w = nh * 64
            if need_q:
                OP("act", lambda e: e.activation(out=sq640.t[:, 0:512], in_=bq.t[:, :], func=AF.Square), r=[bq], w=[sqP[0]])
            OP("act", lambda e: e.activation(out=sq640.t[:, 512:640], in_=bkv.t[:, 0:128], func=AF.Square), r=[bkv], w=[sqP[1]])
            OP("dve", lambda e: e.tensor_reduce(ssq.t[:, 10 - nh:10], sq640.t[:, c0:640].rearrange("p (h d) -> p h d", d=64),
                                                AX.X, ALU.add), r=(sqP if need_q else sqP[1:]), w=[ssq])
            OP("act", lambda e: e.activation(out=rq.t[:, 10 - nh:10], in_=ssq.t[:, 10 - nh:10], func=AF.Ln, scale=1.0 / 64, bias=EPS),
               r=[ssq], w=[rq])
            OP("act", lambda e: e.activation(out=rq.t[:, 10 - nh:10], in_=rq.t[:, 10 - nh:10], func=AF.Exp, scale=-0.5), r=[rq], w=[rq])
            if need_q:
                OP("dve", lambda e: e.tensor_tensor(t1.t[:, 0:512].rearrange("p (h d) -> p h d", d=64),
                                                    bq.t[:, :].rearrange("p (h d) -> p h d", d=64),
                                                    bc_ap(rq.t[:, 0:8], [(1, 8), (0, 64)]), ALU.mult), r=[bq, rq], w=[t1P[0]])
            OP("dve", lambda e: e.tensor_tensor(t1.t[:, 512:640].rearrange("p (h d) -> p h d", d=64),
                                                bkv.t[:, 0:128].rearrange("p (h d) -> p h d", d=64),
                                                bc_ap(rq.t[:, 8:10], [(1, 2), (0, 64)]), ALU.mult), r=[bkv, rq], w=[t1P[1]])
            OP("pool", lambda e: e.tensor_tensor(t1.t[:, c0:640], t1.t[:, c0:640], Gqk.t[:, c0:640], ALU.mult), r=t1P + [Gqk], w=t1P)
            t1v = t1.t[:, c0:640].rearrange("p (h t d) -> p h t d", t=2, d=32)
            tuv = tu.t[:, c0:640].rearrange("p (h t d) -> p h t d", t=2, d=32)
            twv = tw.t[:, c0:640].rearrange("p (h t d) -> p h t d", t=2, d=32)
            cosb = bc_ap(rp.t[:, 0:32], [(0, nh), (0, 2), (1, 32)])
            sinb = bc_ap(rp.t[:, 32:64], [(0, nh), (1, 32)])
            OP("dve", lambda e: e.tensor_tensor(tuv, t1v, cosb, ALU.mult), r=t1P + [rp], w=tuH)
            OP("pool", lambda e: e.tensor_tensor(twv[:, :, 0, :], t1v[:, :, 1, :], sinb, ALU.mult), r=t1P + [rp], w=[twH[0]])
            OP("pool", lambda e: e.tensor_tensor(twv[:, :, 1, :], t1v[:, :, 0, :], sinb, ALU.mult), r=t1P + [rp], w=[twH[1]])
            OP("dve", lambda e: e.tensor_tensor(tuv[:, :, 0, :], tuv[:, :, 0, :], twv[:, :, 0, :], ALU.subtract), r=[tuH[0], twH[0]], w=[tuH[0]])
            OP("dve", lambda e: e.tensor_tensor(tuv[:, :, 1, :], tuv[:, :, 1, :], twv[:, :, 1, :], ALU.add), r=[tuH[1], twH[1]], w=[tuH[1]])
            OP("pool", lambda e: e.tensor_copy(QKb.t[:, 512:640], tu.t[:, 512:640]), r=tuH, w=[QKbP[1]])
            if need_q:
                OP("pool", lambda e: e.tensor_copy(QKb.t[:, 0:512].rearrange("p (a two d) -> p two a d", two=2, a=4),
                                                   tu.t[:, 0:512].rearrange("p (two a d) -> p two a d", two=2, a=4)), r=tuH, w=[QKbP[0]])
            kvi = st["kv"]; st["kv"] ^= 1
            bt = nb()
            OP("pe", lambda e: e.transpose(bt.t[:, 512:640], QKb.t[:, 512:640], identb.t[:]), r=[QKbP[1], identb], w=[bt])
            if need_q:
                for a in range(4):
                    OP("pe", lambda e, a=a: e.transpose(bt.t[:, a * 128:(a + 1) * 128], QKb.t[:, a * 128:(a + 1) * 128], identb.t[:]),
                       r=[QKbP[0], identb], w=[bt])
                OP("act", lambda e: e.copy(qT.t[:], bt.t[:, 0:512].rearrange("p (a q) -> p a q", a=4)), r=[bt], w=[qT])
            OP("act", lambda e: e.copy(kT[kvi].t[:], bt.t[:, 512:640]), r=[bt], w=[kT[kvi]])
            if need_q:
                pass
            vsrc = bass.AP(bkv.t[:, 128:256].tensor, bkv.t[:, 128:256].offset,
                           [list(bkv.t[:, 128:256].ap[0]), [64, 2], [0, 2], [1, 64]])
            OP("dve", lambda e: e.tensor_copy(Vd[kvi].t[:], vsrc), r=[bkv], w=[Vd[kvi]])
            OP("act", lambda e: e.copy(Vf.t[:], bkv.t[:, 128:256]), r=[bkv], w=[Vf])
            if need_q:
                DBG("Vf", Vf, Vf.t[:]); DBG("Vd", Vd[kvi], Vd[kvi].t[:])
            return kvi

        def gla_gates(h, bkg, bvg, flag_col, sample):
            bl = nf()
            proj_fm(h, LR, 16, bl, 0)
            OP("act", lambda e: e.copy(LRa.t[0:16, :], bl.t[0:16, 0:128]), r=[bl], w=[LRa])
            bg_ = nf()
            OP("pe", lambda e: e.matmul(bg_.t[:, 0:256], LRa.t[0:17, :], WgA.t[0:17, :], start=True, stop=True),
               r=[LRa, WgA], w=[bg_])
            OP("act", lambda e: e.activation(out=Lg.t[:], in_=bg_.t[:, 0:256], func=AF.Exp, scale=-1.0), r=[bg_], w=[Lg])
            OP("act", lambda e: e.activation(out=Lg.t[:], in_=Lg.t[:], func=AF.Ln, scale=1.0, bias=1.0), r=[Lg], w=[Lg])
            um1 = Ubm1f if sample else Um1f
            be = nf()
            OP("pe", lambda e: e.matmul(be.t[:, 0:256], um1.t[:], Lg.t[:], start=True, stop=True), r=[um1, Lg], w=[be])
            OP("act", lambda e: e.activation(out=Dk.t[:], in_=be.t[:, 0:256], func=AF.Exp, scale=1.0 / 16), r=[be], w=[Dk])
            OP("dve", lambda e: e.tensor_tensor(kd.t[:], bkg.t[:, 0:256], Dk.t[:], ALU.mult), r=[bkg, Dk], w=[kd])
            if flag_col is None:
                OP("act", lambda e: e.copy(Vg.t[:, 0:256], bkg.t[:, 256:512]), r=[bkg], w=[VgH[0]])
                OP("act", lambda e: e.copy(Vg.t[:, 256:512], bvg.t[:, 0:256]), r=[bvg], w=[VgH[1]])
            else:
                OP("act", lambda e: e.activation(out=Vg.t[:, 0:256], in_=bkg.t[:, 256:512], func=AF.Identity,
                                                 scale=flg.t[:, flag_col:flag_col + 1]), r=[bkg, flg], w=[VgH[0]])
                OP("act", lambda e: e.activation(out=Vg.t[:, 256:512], in_=bvg.t[:, 0:256], func=AF.Identity,
                                                 scale=flg.t[:, flag_col:flag_col + 1]), r=[bvg, flg], w=[VgH[1]])

        def gla_state_update():
            bt_ = nf()
            for j in range(2):
                OP("pe", lambda e, j=j: e.matmul(bt_.t[:, j:j + 1], Lg.t[:, 128 * j:128 * (j + 1)], onesf.t[:, 0:1],
                                                 start=True, stop=True), r=[Lg, onesf], w=[bt_])
            OP("act", lambda e: e.activation(out=Elast.t[:], in_=bt_.t[:, 0:2], func=AF.Exp, scale=-1.0 / 16), r=[bt_], w=[Elast])
            bs = nf()
            for j in range(2):
                OP("pe", lambda e, j=j: e.matmul(bs.t[:, 256 * j:256 * (j + 1)], kd.t[:, 128 * j:128 * (j + 1)],
                                                 Vg.t[:, 256 * j:256 * (j + 1)], start=True, stop=True), r=[kd, VgH[j]], w=[bs])
            for j in range(2):
                for hl in range(2):
                    OP("dve", lambda e, j=j, hl=hl: e.scalar_tensor_tensor(
                        out=Sst.t[64 * hl:64 * (hl + 1), j, :], in0=Sst.t[64 * hl:64 * (hl + 1), j, :],
                        scalar=Elast.t[64 * hl:64 * (hl + 1), j:j + 1],
                        in1=bs.t[64 * hl:64 * (hl + 1), 256 * j + 128 * hl:256 * j + 128 * (hl + 1)],
                        op0=ALU.mult, op1=ALU.add), r=[SstQ[j][hl], Elast, bs], w=[SstQ[j][hl]])
            for hl in range(2):
                OP("pool", lambda e, hl=hl: e.tensor_copy(Sbm.t[64 * hl:64 * (hl + 1), :, hl, :], Sst.t[64 * hl:64 * (hl + 1), :, :]),
                   r=[SstQ[0][hl], SstQ[1][hl]], w=[SbmH[hl]])

        def sigmoid_from_psum(dst_ap, src_ap, src_t, dst_t, tmp):
            OP("act", lambda e: e.activation(out=tmp.t[:], in_=src_ap, func=AF.Exp, scale=-1.0), r=[src_t], w=[tmp])
            OP("act", lambda e: e.activation(out=tmp.t[:], in_=tmp.t[:], func=AF.Ln, scale=1.0, bias=1.0), r=[tmp], w=[tmp])
            OP("act", lambda e: e.activation(out=dst_ap, in_=tmp.t[:], func=AF.Exp, scale=-1.0), r=[tmp], w=[dst_t])

        def attn_prep(h, rope_idx, sample, last):
            bq = nf()
            proj_tm(h, QA, 512, bq)
            cur = attn_kv(h, rope_idx, True, bq)
            if last:
                DMA("sp", pk_o, tu.t[:, 512:640], sem_o, r=tuH)
                DMA("sp", pv_o, Vf.t[:], sem_o, r=[Vf])
            if sample:
                for i in range(16):
                    DMA("sp", sk_o[i, 120:128, :], tu.t[8 * i:8 * (i + 1), 512:640], sem_o, r=tuH)
                    DMA("sp", sv_o[i, 120:128, :], Vf.t[8 * i:8 * (i + 1), :], sem_o, r=[Vf])
            return cur

        def full_tile(xs, h, y_ap, rope_idx, sample, first, last, hook_a=None, hook_b=None, cur=None):
            chk(20)
            if cur is None:
                cur = attn_prep(h, rope_idx, sample, last)
            prev = cur ^ 1
            chk(21)
            def z_a_part():
                bz = nf()
                for c in range(4):
                    proj_fm(h, ZA + 128 * c, 128, bz, 128 * c)
                sigmoid_from_psum(Et2.t[:], bz.t[:, :], bz, Et2, Et)
                OP("dve", lambda e: e.tensor_tensor(Zs.t[:].rearrange("p a q -> p (a q)"), bz.t[:, :], Et2.t[:], ALU.mult),
                   r=[bz, Et2], w=[Zs])
                DBG("Zs", Zs, Zs.t[:])
            if sample:
                z_a_part()
            chk(22)
            mcur = Ub4b if sample else U4b
            mprev = L4fb if first else L4b
            def scores_part(kvh):
                rows = slice(64 * kvh, 64 * (kvh + 1))
                blocks = [(cur, mcur)] if sample else [(prev, mprev), (cur, mcur)]
                pts = []
                for bi, (slot, msk) in enumerate(blocks):
                    bsc = nf()
                    OP("pe", lambda e, slot=slot, bsc=bsc: e.matmul(bsc.t[:, :], kT[slot].t[rows, :],
                                                                    qT.t[rows, :, :].rearrange("p a q -> p (a q)"),
                                                                    start=True, stop=True), r=[kT[slot], qT], w=[bsc])
                    pt = PT[bi] if sample else PT[2 * kvh + bi]
                    OP("act", lambda e, pt=pt, bsc=bsc: e.activation(out=pt.t[:], in_=bsc.t[:, :], func=AF.Exp), r=[bsc], w=[pt])
                    OP("pool", lambda e, pt=pt, msk=msk: e.tensor_tensor(pt.t[:], pt.t[:], msk.t[:].rearrange("p a q -> p (a q)"),
                                                                         ALU.mult), r=[pt, msk], w=[pt])
                    pts.append((pt, slot))
                return pts

            all_pts = {}
            if not sample:
                for kvh in range(2):
                    all_pts[kvh] = scores_part(kvh)
                if hook_a is not None:
                    hook_a()
                    hook_a = None
                z_a_part()
            for kvh in range(2):
                rows = slice(64 * kvh, 64 * (kvh + 1))
                pts = scores_part(kvh) if sample else all_pts[kvh]
                bo = nf(); bd = nf()
                for bi, (pt, slot) in enumerate(pts):
                    OP("pe", lambda e, pt=pt, slot=slot, bi=bi: e.matmul(bo.t[:, :], Vd[slot].t[:, kvh, :, :].rearrange("p u d -> p (u d)"),
                                                                         pt.t[:], start=(bi == 0), stop=(bi == len(pts) - 1)),
                       r=[Vd[slot], pt], w=[bo])
                    OP("pe", lambda e, pt=pt, bi=bi: e.matmul(bd.t[:, :], onesb.t[:], pt.t[:], start=(bi == 0), stop=False),
                       r=[onesb, pt], w=[bd])
                OP("pe", lambda e: e.matmul(bd.t[:, :], onesf.t[0:1, :], esink.t[0:1, 512 * kvh:512 * (kvh + 1)],
                                            start=False, stop=True), r=[onesf, esink], w=[bd])
                if sample:
                    pin(bo, bd)
                    boc, bdc = sample_cache_attn(kvh)
                    unpin(bo, bd, boc, bdc)
                    OP("act", lambda e, bdc=bdc: e.copy(Et.t[:], bdc.t[:, :]), r=[bdc], w=[Et])
                    OP("dve", lambda e: e.tensor_tensor(Et2.t[:].rearrange("p (a i q) -> p i a q", a=4, q=8),
                                                        bd.t[:, :].rearrange("p (a i q) -> p i a q", a=4, q=8),
                                                        Et.t[:].rearrange("p (i a q) -> p i a q", a=4, q=8), ALU.add),
                       r=[bd, Et], w=[Et2])
                    OP("act", lambda e: e.activation(out=Rr.t[:], in_=Et2.t[:], func=AF.Ln), r=[Et2], w=[Rr])
                    OP("act", lambda e: e.activation(out=Rr.t[:], in_=Rr.t[:], func=AF.Exp, scale=-1.0), r=[Rr], w=[Rr])
                    OP("act", lambda e, boc=boc: e.copy(Et.t[:], boc.t[:, :]), r=[boc], w=[Et])
                    OP("dve", lambda e: e.tensor_tensor(Et2.t[:].rearrange("p (a i q) -> p i a q", a=4, q=8),
                                                        bo.t[:, :].rearrange("p (a i q) -> p i a q", a=4, q=8),
                                                        Et.t[:].rearrange("p (i a q) -> p i a q", a=4, q=8), ALU.add),
                       r=[bo, Et], w=[Et2])
                    OP("dve", lambda e: e.tensor_tensor(tt.t[:], Et2.t[:], Rr.t[:], ALU.mult), r=[Et2, Rr], w=[tt])
                else:
                    Rk = Rr if kvh == 0 else Et2
                    tk = tt if kvh == 0 else sqg
                    OP("act", lambda e: e.activation(out=Rk.t[:], in_=bd.t[:, :], func=AF.Ln), r=[bd], w=[Rk])
                    OP("act", lambda e: e.activation(out=Rk.t[:], in_=Rk.t[:], func=AF.Exp, scale=-1.0), r=[Rk], w=[Rk])
                    OP("dve", lambda e: e.tensor_tensor(tk.t[:], bo.t[:, :], Rk.t[:], ALU.mult), r=[bo, Rk], w=[tk])
                tk_ = tt if (sample or kvh == 0) else sqg
                for a2 in range(2):
                    c = 2 * kvh + a2
                    for par in range(2):
                        pr = slice(64 * par, 64 * (par + 1))
                        a = 2 * a2 + par
                        OP("pool", lambda e, c=c, pr=pr, a=a: e.tensor_tensor(OZ.t[pr, c, :], tk_.t[pr, 128 * a:128 * (a + 1)],
                                                                              Zs.t[pr, c, :], ALU.mult), r=[tk_, Zs], w=[OZc[c]])
            chk(23)
            if hook_a is not None:
                hook_a()
            if hook_b is not None:
                hook_b()
            DBG("OZ", OZ, OZ.t[:]); DBG("tt", tt, tt.t[:]); DBG("Rr", Rr, Rr.t[:]); DBG("PT1", PT[1], PT[1].t[:])
            bkg = nf(); pin(bkg)
            bvg = nf(); pin(bvg)
            proj_tm(h, KG, 512, bkg)
            proj_tm(h, VG + 256, 256, bvg)
            gla_gates(h, bkg, bvg, None, sample)
            unpin(bkg, bvg)
            chk(24)
            DBG("Lg", Lg, Lg.t[:]); DBG("Dk", Dk, Dk.t[:]); DBG("kd", kd, kd.t[:]); DBG("Vg", Vg, Vg.t[:])
            um = Ubf if sample else Uf
            bb = nf()
            for j in range(2):
                OP("pe", lambda e, j=j: e.matmul(bb.t[:, 128 * j:128 * (j + 1)], Lg.t[:, 128 * j:128 * (j + 1)], um.t[:],
                                                 start=True, stop=True), r=[Lg, um], w=[bb])
            OP("act", lambda e: e.activation(out=Eq.t[:].rearrange("p j t -> p (j t)"), in_=bb.t[:, 0:256], func=AF.Exp,
                                             scale=-1.0 / 16), r=[bb], w=[Eq])
            OP("act", lambda e: e.activation(out=Ek.t[:].rearrange("p j t -> p (j t)"), in_=bb.t[:, 0:256], func=AF.Exp,
                                             scale=1.0 / 16), r=[bb], w=[Ek])
            bqk = nf()
            for j in range(2):
                proj_fm(h, QG + 128 * j, 128, bqk, 128 * j)
                proj_fm(h, KG + 128 * j, 128, bqk, 256 + 128 * j)
            OP("dve", lambda e: e.scalar_tensor_tensor(out=qe.t[:].rearrange("p j t -> p (j t)"), in0=bqk.t[:, 0:256], scalar=0.125,
                                                       in1=Eq.t[:].rearrange("p j t -> p (j t)"), op0=ALU.mult, op1=ALU.mult),
               r=[bqk, Eq], w=[qe])
            for hl in range(2):
                pr = slice(64 * hl, 64 * (hl + 1))
                OP("dve", lambda e, hl=hl, pr=pr: e.tensor_tensor(ke.t[pr, hl, :, :], bqk.t[pr, 256:512].rearrange("p (j t) -> p j t", j=2),
                                                                  Ek.t[pr, :, :], ALU.mult), r=[bqk, Ek], w=[keH[hl]])
            ba = nf()
            for hh in range(4):
                j, hl = hh // 2, hh % 2
                pr = slice(64 * hl, 64 * (hl + 1))
                OP("pe", lambda e, hh=hh, j=j, hl=hl: e.matmul(ba.t[:, 128 * hh:128 * (hh + 1)], ke.t[:, hl, j, :], qe.t[:, j, :],
                                                               start=True, stop=True), r=[keH[hl], qe], w=[ba])
            OP("dve", lambda e: e.tensor_tensor(Am.t[:].rearrange("p a q -> p (a q)"), ba.t[:, :],
                                                mcur.t[:].rearrange("p a q -> p (a q)"), ALU.mult), r=[ba, mcur], w=[Am])
            bog = nf()
            for hh in range(4):
                j, hl = hh // 2, hh % 2
                pr = slice(64 * hl, 64 * (hl + 1))
                OP("pe", lambda e, hh=hh: e.matmul(bog.t[:, 128 * hh:128 * (hh + 1)], Vg.t[:, 128 * hh:128 * (hh + 1)],
                                                   Am.t[:, hh, :], start=True, stop=sample), r=[VgH[hh // 2], Am], w=[bog])
                if not sample:
                    OP("pe", lambda e, hh=hh, j=j, hl=hl: e.matmul(bog.t[:, 128 * hh:128 * (hh + 1)], Sbm.t[:, j, hl, :], qe.t[:, j, :],
                                                                   start=False, stop=True), r=[SbmH[hl], qe], w=[bog])
            chk(25)
            pass
            if sample:
                pin(bog)
                bint = sample_gla_state()
                unpin(bog, bint)
                OP("act", lambda e: e.copy(Et.t[:], bint.t[:, :]), r=[bint], w=[Et])
                OP("dve", lambda e: e.tensor_tensor(Et2.t[:], bog.t[:, :], Et.t[:], ALU.add), r=[bog, Et], w=[Et2])
                osrc, osrc_t = Et2.t[:], Et2
            else:
                gla_state_update()
                osrc, osrc_t = bog.t[:, :], bog
            chk(26)
            OP("act", lambda e: e.activation(out=sqg.t[:], in_=osrc, func=AF.Square), r=[osrc_t], w=[sqg])
            bn = nf()
            OP("pe", lambda e: e.matmul(bn.t[:, :], onesb.t[:], sqg.t[:], start=True, stop=True), r=[onesb, sqg], w=[bn])
            OP("act", lambda e: e.activation(out=rsg.t[:], in_=bn.t[:, :], func=AF.Ln, scale=1.0 / 128, bias=EPS), r=[bn], w=[rsg])
            OP("act", lambda e: e.activation(out=rsg.t[:], in_=rsg.t[:], func=AF.Exp, scale=-0.5), r=[rsg], w=[rsg])
            OP("dve", lambda e: e.tensor_tensor(og.t[:], osrc, rsg.t[:], ALU.mult), r=[osrc_t, rsg], w=[og])
            bzg = nf()
            for c in range(4):
                proj_fm(h, ZG + 128 * c, 128, bzg, 128 * c)
            sigmoid_from_psum(Et2.t[:], bzg.t[:, :], bzg, Et2, Et)
            OP("dve", lambda e: e.tensor_tensor(Zg.t[:].rearrange("p a q -> p (a q)"), bzg.t[:, :], Et2.t[:], ALU.mult),
               r=[bzg, Et2], w=[Zg])
            OP("dve", lambda e: e.scalar_tensor_tensor(out=OGZ.t[:].rearrange("p a q -> p (a q)"), in0=og.t[:], scalar=gdv.t[:, 0:1],
                                                       in1=Zg.t[:].rearrange("p a q -> p (a q)"), op0=ALU.mult, op1=ALU.mult),
               r=[og, gdv, Zg], w=[OGZ])
            chk(27)
            pass
            Mgt = Mg.t[:].rearrange("p a q -> p (a q)")
            for g4 in range(4):
                bm = nf()
                proj_tm(h, MA + 512 * g4, 512, bm)
                sigmoid_from_psum(Mgt[:, 512 * g4:512 * (g4 + 1)], bm.t[:, :], bm, MgQ[g4], Et2 if g4 % 2 == 0 else Et)
            chk(28)
            for n in range(2):
                bra = nf(); brg = nf()
                for kc in range(4):
                    OP("pe", lambda e, kc=kc, n=n: e.matmul(bra.t[:, :], OZ.t[:, kc, :], Wba.t[:, kc, 512 * n:512 * (n + 1)],
                                                            start=(kc == 0), stop=(kc == 3)), r=[Wba, OZc[kc]], w=[bra])
                for kc in range(4):
                    OP("pe", lambda e, kc=kc, n=n: e.matmul(brg.t[:, :], OGZ.t[:, kc, :], Wbg.t[:, kc, 512 * n:512 * (n + 1)],
                                                            start=(kc == 0), stop=(kc == 3)), r=[Wbg, OGZ], w=[brg])
                ba_, bg2_ = (brt, Et) if n == 0 else (Rr, Et2)
                OP("dve", lambda e, n=n: e.tensor_tensor(ba_.t[:], bra.t[:, :], Mgt[:, 512 * n:512 * (n + 1)], ALU.mult),
                   r=[bra, MgQ[n]], w=[ba_])
                OP("dve", lambda e, n=n: e.tensor_tensor(bg2_.t[:], brg.t[:, :], Mgt[:, 1024 + 512 * n:1024 + 512 * (n + 1)], ALU.mult),
                   r=[brg, MgQ[2 + n]], w=[bg2_])
                OP("pool", lambda e, n=n: e.tensor_tensor(MTM[n].t[:], ba_.t[:], bg2_.t[:], ALU.add),
                   r=[ba_, bg2_, XN], w=[MTM[n]])
            btm = nb()
            for kc in range(KC):
                OP("pe", lambda e, kc=kc: e.transpose(btm.t[:, kc * 128:(kc + 1) * 128],
                                                       MTM[kc // 4].t[:, (kc % 4) * 128:(kc % 4 + 1) * 128], identb.t[:]),
                   r=[MTM[kc // 4], identb], w=[btm])
            OP("act", lambda e: e.copy(mT.t[:].rearrange("p a q -> p (a q)"), btm.t[:, :]), r=[btm, XN], w=[mT])
            DBG("mT", mT, mT.t[:])
            chk(29)
            gt = Gbc
            for n in range(2):
                by = nf()
                for kc in range(KC):
                    OP("pe", lambda e, kc=kc, n=n: e.matmul(by.t[:, :], mT.t[:, kc, :], Wout.t[:, kc, 512 * n:512 * (n + 1)],
                                                            start=(kc == 0), stop=(kc == KC - 1)), r=[mT, Wout], w=[by])
                chk(33)
                OP("dve", lambda e, n=n: e.tensor_tensor(Y.t[:, 512 * n:512 * (n + 1)], by.t[:, :], gt.t[:, 512 * n:512 * (n + 1)],
                                                         ALU.mult), r=[by, gt], w=[YH[n]])
                chk(32)
                OP("pool", lambda e, n=n: e.tensor_tensor(Y.t[:, 512 * n:512 * (n + 1)], Y.t[:, 512 * n:512 * (n + 1)],
                                                          xs.t[:, 512 * n:512 * (n + 1)], ALU.add), r=[YH[n], xs], w=[YH[n]])
            chk(30)
            DMA("sp", y_ap, Y.t[:], sem_y, r=YH)
            chk(31)

        smp = {"i": 0}

        def sample_cache_attn(kvh):
            rows = slice(64 * kvh, 64 * (kvh + 1))
            if kvh == 0:
                OP("dve", lambda e: e.tensor_copy(qTs.t[:], qT.t[:].rearrange("p a (i q) -> p i a q", q=8)), r=[qT], w=[qTs])
            boc = nf(); bdc = nf()
            pin(boc, bdc)
            base = smp["i"]; smp["i"] += 16

            def stage1(i):
                sl = (base + i) % 2; s4 = (base + i) % 4
                DMA("sp", CKs[s4].t[:], ck_d[i], sem_ck[s4], w=[CKs[s4]])
                DMA("sp", CVs[s4].t[:], cv_d[i], sem_ck[s4], w=[CVs[s4]])
                OP("dve", lambda e: e.tensor_copy(CKb[sl].t[:], CKs[s4].t[:]), r=[CKs[s4]], w=[CKb[sl]])
                bt = nb()
                OP("pe", lambda e: e.transpose(bt.t[:, 0:128], CKb[sl].t[:], identb.t[:]), r=[CKb[sl], identb], w=[bt])
                OP("act", lambda e: e.copy(KcT[sl].t[:], bt.t[:, 0:128]), r=[bt], w=[KcT[sl]])
                vsrc = bass.AP(CVs[s4].t[:].tensor, CVs[s4].t[:].offset, [list(CVs[s4].t[:].ap[0]), [64, 2], [0, 2], [1, 64]])
                OP("dve", lambda e: e.tensor_copy(Vcd[sl].t[:], vsrc), r=[CVs[s4]], w=[Vcd[sl]])

            def stage2(i):
                sl = (base + i) % 2
                bsc = nf()
                OP("pe", lambda e: e.matmul(bsc.t[:, 0:32], KcT[sl].t[rows, :], qTs.t[rows, i, :, :].rearrange("p a q -> p (a q)"),
                                            start=True, stop=True), r=[KcT[sl], qTs], w=[bsc])
                OP("act", lambda e: e.activation(out=PTc[sl].t[:, 0, :, :].rearrange("p a q -> p (a q)"),
                                                 in_=bsc.t[:, 0:32], func=AF.Exp), r=[bsc], w=[PTc[sl]])
                OP("pool", lambda e: e.tensor_tensor(PTc[sl].t[:, 0, :, :], PTc[sl].t[:, 0, :, :], C4b.t[:, 0:4, :], ALU.mult),
                   r=[PTc[sl], C4b], w=[PTc[sl]])

            def stage3(i):
                sl = (base + i) % 2
                OP("pe", lambda e: e.matmul(boc.t[:, 32 * i:32 * (i + 1)], Vcd[sl].t[:, kvh, :, :].rearrange("p u d -> p (u d)"),
                                            PTc[sl].t[:, 0, :, :].rearrange("p a q -> p (a q)"), start=True, stop=True),
                   r=[Vcd[sl], PTc[sl]], w=[boc])
                OP("pe", lambda e: e.matmul(bdc.t[:, 32 * i:32 * (i + 1)], onesb.t[:],
                                            PTc[sl].t[:, 0, :, :].rearrange("p a q -> p (a q)"), start=True, stop=True),
                   r=[onesb, PTc[sl]], w=[bdc])

            for t_ in range(18):
                if 0 <= t_ - 2 < 16:
                    stage3(t_ - 2)
                if 0 <= t_ - 1 < 16:
                    stage2(t_ - 1)
                if t_ < 16:
                    stage1(t_)
            return boc, bdc

        def sample_gla_state():
            bog = nf()
            pin(bog)
            OP("dve", lambda e: e.tensor_copy(ElS.t[:], Eq.t[:].rearrange("p j (i r) -> p j i r", r=8)[:, :, :, 7]), r=[Eq], w=[ElS])

            def stage1(i):
                sl = i % 2; s4 = i % 4
                DMA("sp", S0s[s4].t[:], st0_d[i].rearrange("(j hl) k v -> (hl k) j v", hl=2), sem_s0[s4], w=[S0s[s4]])
                OP("dve", lambda e: e.tensor_copy(S0b[s4].t[0:64, :, 0, :], S0s[s4].t[0:64, :, :]), r=[S0s[s4]], w=[S0b[s4]])
                OP("act", lambda e: e.copy(S0b[s4].t[64:128, :, 1, :], S0s[s4].t[64:128, :, :]), r=[S0s[s4]], w=[S0b[s4]])
                OP("dve", lambda e: e.tensor_scalar_mul(kdm[sl].t[:], kd.t[:], OH.t[:, i:i + 1]), r=[kd, OH], w=[kdm[sl]])

            def stage2(i):
                sl = i % 2; s4 = i % 4
                for hh in range(4):
                    j, hl = hh // 2, hh % 2
                    OP("pe", lambda e, hh=hh, j=j, hl=hl: e.matmul(bog.t[:, 128 * hh + 8 * i:128 * hh + 8 * (i + 1)],
                                                                   S0b[s4].t[:, j, hl, :], qe.t[:, j, 8 * i:8 * (i + 1)],
                                                                   start=True, stop=True), r=[S0b[s4], qe], w=[bog])
                bs = nf()
                for j in range(2):
                    OP("pe", lambda e, j=j: e.matmul(bs.t[:, 256 * j:256 * (j + 1)], kdm[sl].t[:, 128 * j:128 * (j + 1)],
                                                     Vg.t[:, 256 * j:256 * (j + 1)], start=True, stop=True),
                       r=[kdm[sl], VgH[j]], w=[bs])
                for j in range(2):
                    for hl in range(2):
                        OP("dve", lambda e, j=j, hl=hl: e.scalar_tensor_tensor(
                            out=SN[s4].t[64 * hl:64 * (hl + 1), j, :], in0=S0s[s4].t[64 * hl:64 * (hl + 1), j, :],
                            scalar=ElS.t[64 * hl:64 * (hl + 1), j, i:i + 1],
                            in1=bs.t[64 * hl:64 * (hl + 1), 256 * j + 128 * hl:256 * j + 128 * (hl + 1)],
                            op0=ALU.mult, op1=ALU.add), r=[S0s[s4], ElS, bs], w=[SN[s4]])
                DMA("act", sst_o[i].rearrange("(j hl) k v -> (hl k) j v", hl=2), SN[s4].t[:], sem_sn[s4], r=[SN[s4]])

            for t_ in range(17):
                if 0 <= t_ - 1 < 16:
                    stage2(t_ - 1)
                if t_ < 16:
                    stage1(t_)
            return bog

        def pre_tile(i, xs, h, hook_a=None, hook_b=None):
            if hook_a is not None:
                hook_a()
            bkg = nf(); bvg = nf()
            proj_tm(h, KG, 512, bkg)
            proj_tm(h, VG + 256, 256, bvg)
            gla_gates(h, bkg, bvg, i, False)
            if hook_b is not None:
                hook_b()
            gla_state_update()
            if i == NPRE - 1:
                attn_kv(h, 16, False)

        chk(15)
        DMA("act", sk_o[:, 0:120, :], ck_d[:, 8:128, :], sem_o)
        DMA("act", sv_o[:, 0:120, :], cv_d[:, 8:128, :], sem_o)
        tiles = [("main", i) for i in range(n_main)]
        if do_sample:
            tiles.append(("smp", 0))

        def pre_P1(h):
            bkg = nf(); pin(bkg)
            proj_tm(h, KG, 512, bkg)
            return bkg

        def pre_P2(h):
            bvg = nf(); pin(bvg)
            proj_tm(h, VG + 256, 256, bvg)
            proj_fm(h, LR, 16, bvg, 256)
            return bvg

        def pre_G1(bvg):
            OP("act", lambda e: e.copy(LRa.t[0:16, :], bvg.t[0:16, 256:384]), r=[bvg], w=[LRa])
            bg_ = nf()
            OP("pe", lambda e: e.matmul(bg_.t[:, 0:256], LRa.t[0:17, :], WgA.t[0:17, :], start=True, stop=True),
               r=[LRa, WgA], w=[bg_])
            return bg_

        def pre_G2a(bg_):
            OP("act", lambda e: e.activation(out=Lg.t[:], in_=bg_.t[:, 0:256], func=AF.Exp, scale=-1.0), r=[bg_], w=[Lg])
            OP("act", lambda e: e.activation(out=Lg.t[:], in_=Lg.t[:], func=AF.Ln, scale=1.0, bias=1.0), r=[Lg], w=[Lg])
            OP("pe", lambda e: e.matmul(bg_.t[:, 256:512], Um1f.t[:], Lg.t[:], start=True, stop=True), r=[Um1f, Lg], w=[bg_])
            for j in range(2):
                OP("pe", lambda e, j=j: e.matmul(bg_.t[:, j:j + 1], Lg.t[:, 128 * j:128 * (j + 1)], onesf.t[:, 0:1],
                                                 start=True, stop=True), r=[Lg, onesf], w=[bg_])

        def pre_G2b(bkg, bvg, bg_, flag_col):
            OP("act", lambda e: e.activation(out=Dk.t[:], in_=bg_.t[:, 256:512], func=AF.Exp, scale=1.0 / 16), r=[bg_], w=[Dk])
            OP("act", lambda e: e.activation(out=Elast.t[:], in_=bg_.t[:, 0:2], func=AF.Exp, scale=-1.0 / 16), r=[bg_], w=[Elast])
            OP("dve", lambda e: e.tensor_scalar_mul(Vg.t[:, 0:256], bkg.t[:, 256:512], flg.t[:, flag_col:flag_col + 1]),
               r=[bkg, flg], w=[VgH[0]])
            OP("dve", lambda e: e.tensor_scalar_mul(Vg.t[:, 256:512], bvg.t[:, 0:256], flg.t[:, flag_col:flag_col + 1]),
               r=[bvg, flg], w=[VgH[1]])
            OP("dve", lambda e: e.tensor_tensor(kd.t[:], bkg.t[:, 0:256], Dk.t[:], ALU.mult), r=[bkg, Dk], w=[kd])
            unpin(bkg, bvg)

        def pre_U():
            bs = nf()
            for j in range(2):
                OP("pe", lambda e, j=j: e.matmul(bs.t[:, 256 * j:256 * (j + 1)], kd.t[:, 128 * j:128 * (j + 1)],
                                                 Vg.t[:, 256 * j:256 * (j + 1)], start=True, stop=True), r=[kd, VgH[j]], w=[bs])
            for j in range(2):
                for hl in range(2):
                    OP("dve", lambda e, j=j, hl=hl: e.scalar_tensor_tensor(
                        out=Sst.t[64 * hl:64 * (hl + 1), j, :], in0=Sst.t[64 * hl:64 * (hl + 1), j, :],
                        scalar=Elast.t[64 * hl:64 * (hl + 1), j:j + 1],
                        in1=bs.t[64 * hl:64 * (hl + 1), 256 * j + 128 * hl:256 * j + 128 * (hl + 1)],
                        op0=ALU.mult, op1=ALU.add), r=[SstQ[j][hl], Elast, bs], w=[SstQ[j][hl]])

        pre_idx = list(range(NPRE - n_pre, NPRE))
        items = [("pre", i) for i in pre_idx] + tiles[:1]
        fr = {}

        def fa(k):
            it = items[k]
            xs_, src_ = front_a(x_of(it), it[0] == "smp")
            fr[k] = {"xs": xs_, "src": src_, "smp": it[0] == "smp"}

        def fb(k):
            fr[k]["h"] = front_b(fr[k]["src"], fr[k]["smp"])

        def x_of(t):
            return {"pre": lambda: xp[t[1]], "main": lambda: xm[t[1]], "smp": lambda: xs_d}[t[0]]()

        cur_fr = {}
        if n_pre > 0:
            fa(0); fb(0)
            if len(items) > 1:
                fa(1); fb(1)
            PB = {0: (pre_P1(fr[0]["h"]), pre_P2(fr[0]["h"]))}
            for k in range(n_pre):
                bkg, bvg = PB[k]
                bg_ = pre_G1(bvg)
                if k + 2 < len(items):
                    fa(k + 2)
                nb1 = pre_P1(fr[k + 1]["h"]) if k + 1 < n_pre else None
                pre_G2a(bg_)
                nb2 = pre_P2(fr[k + 1]["h"]) if k + 1 < n_pre else None
                PB[k + 1] = (nb1, nb2)
                pre_G2b(bkg, bvg, bg_, pre_idx[k])
                if k + 2 < len(items):
                    fb(k + 2)
                pre_U()
                if k == n_pre - 1:
                    for hl in range(2):
                        OP("pool", lambda e, hl=hl: e.tensor_copy(Sbm.t[64 * hl:64 * (hl + 1), :, hl, :], Sst.t[64 * hl:64 * (hl + 1), :, :]),
                           r=[SstQ[0][hl], SstQ[1][hl]], w=[SbmH[hl]])
                    attn_kv(fr[k]["h"], 16, False)
            if len(items) > n_pre:
                cur_fr = {"xs": fr[n_pre]["xs"], "h": fr[n_pre]["h"]}

        if tiles and not cur_fr:
            xs0, src0 = front_a(x_of(tiles[0]), tiles[0][0] == "smp")
            cur_fr = {"xs": xs0, "h": front_b(src0, tiles[0][0] == "smp")}
        for ti, t in enumerate(tiles):
            nxt = tiles[ti + 1] if ti + 1 < len(tiles) else None
            if nxt is not None and nxt[0] == "smp":
                nxt = None
            if t[0] == "smp":
                o_ = st["x"] ^ 1
                xo_off = X[o_].t.manual_sbuf_range[0]
                for k_, nm in enumerate(("A", "S")):
                    tt_ = T(nc.alloc_sbuf_tensor_at("smod" + nm, [128, D], BF16, offset=xo_off + 2048 * k_), "smod" + nm)
                    tt_.b = X[o_].b
                    smod[nm] = tt_
                    DMA("sp", tt_.t[:], modx[k_], sem_mx, w=[tt_])
                DMA("sp", Gbc.t[:], modx[2], sem_mx, w=[Gbc])
                xs0, src0 = front_a(xs_d, True)
                cur_fr = {"xs": xs0, "h": front_b(src0, True)}
            nx = {}

            if nxt is not None:
                nx["xs_pre"] = front_dma(x_of(nxt))

            def hook_a(nxt=nxt, nx=nx):
                if nxt is not None:
                    nx["xs"], nx["src"] = front_a(x_of(nxt), nxt[0] == "smp", nx["xs_pre"])

            def tinfo(tt_):
                if tt_[0] == "main":
                    return tt_[1], False, tt_[1] == NMAIN - 1
                return 17, True, False

            def hook_b(nxt=nxt, nx=nx):
                if nxt is not None:
                    nx["h"] = front_b(nx["src"], nxt[0] == "smp")
                    ri, sm, la = tinfo(nxt)
                    defer_begin()
                    nx["cur"] = attn_prep(nx["h"], ri, sm, la)
                    defer_end()

            if t[0] == "main":
                i = t[1]
                full_tile(cur_fr["xs"], cur_fr["h"], ym[i], i, False, i == 0, i == NMAIN - 1, hook_a, hook_b, cur_fr.get("cur"))
            else:
                full_tile(cur_fr["xs"], cur_fr["h"], ys, 17, True, False, False, hook_a, hook_b, cur_fr.get("cur"))
            defer_flush()
            cur_fr = nx
        for hl in range(2):
            dst = pst_o.rearrange("(j hl) k v -> hl k j v", hl=2)[hl]
            DMA("sp", dst, Sst.t[64 * hl:64 * (hl + 1), :, :], sem_o, r=[SstQ[0][hl], SstQ[1][hl]])

    except _Stop:
        pass
    S.wait_all("sp")
    with nc.allow_low_precision("bf16 matmul operands, fp32 accumulation"):
        S.emit()
    es.close()
    return nc


_NC_CACHE = {}


def _consts():
    idx = np.arange(128)
    U = (idx[:, None] <= idx[None, :]).astype(np.float32)
    L = (idx[:, None] >= idx[None, :]).astype(np.float32)
    same = (idx[:, None] // 8 == idx[None, :] // 8).astype(np.float32)
    Ub = U * same
    Um1 = U - 1.0
    Ubm1 = (Ub - same).astype(np.float32)
    I = np.eye(128, dtype=np.float32)
    SelS = np.zeros((17, 128), np.float32)
    for t in range(128):
        SelS[1 + t // 8, t] = 1.0
    SelP = np.zeros((17, 128), np.float32)
    SelP[0, :] = 1.0
    OH = (idx[:, None] // 8 == np.arange(16)[None, :]).astype(np.float32)
    C = (idx[:, None] >= np.arange(8)[None, :]).astype(np.float32)
    ones = np.ones((128, 128), np.float32)
    return dict(cU=U, cL=L, cUb=Ub, cUm1=Um1, cUbm1=Ubm1, cI=I, cSelS=SelS, cSelP=SelP, cOH=OH, cC=C, cOnes=ones)


def _rope_table(pos):
    half = 32
    inv = (1.0 / (np.float32(10000.0) ** (np.arange(half, dtype=np.float32) / np.float32(half)))).astype(np.float32)
    ang = pos.astype(np.float32)[:, None] * inv[None, :]
    return np.concatenate([np.cos(ang), np.sin(ang)], axis=-1).astype(np.float32)


def kernel(x_prompt, x_sample, cache_win_k, cache_win_v, state_gla, c_prompt, c_sample,
           norm_g, w_ada, b_ada, w_in, q_norm_g, k_norm_g, attn_sinks, w_gla_gate, b_gla_gate,
           gla_norm_g, w_branch_att, w_branch_gla, w_out):
    f = lambda a: np.ascontiguousarray(np.asarray(a, dtype=np.float32))
    x_prompt, x_sample = f(x_prompt), f(x_sample)
    ck, cv, st0 = f(cache_win_k)[0], f(cache_win_v)[0], f(state_gla)[0]
    c_prompt, c_sample = f(c_prompt), f(c_sample)
    if "nc" not in _NC_CACHE:
        _NC_CACHE["nc"] = build_nc()
    nc = _NC_CACHE["nc"]
    consts = _consts()
    shared = dict(
        normg=f(np.asarray(norm_g)[0].reshape(8, 128).T), normrow=f(np.asarray(norm_g)[0].reshape(1, D)),
        w_ada=f(w_ada)[0], b_ada=f(b_ada)[0].reshape(1, 3072), w_in=f(w_in)[0],
        gqk=f(np.broadcast_to(np.concatenate([np.tile(np.asarray(q_norm_g)[0], 8), np.tile(np.asarray(k_norm_g)[0], 2)])[None, :], (128, 640))),
        sinks=f(np.repeat(np.asarray(attn_sinks)[0], 128).reshape(1, 1024)),
        w_gla_gate=f(w_gla_gate)[0], b_gla_gate=f(b_gla_gate)[0].reshape(1, 256),
        gdv=f(np.asarray(gla_norm_g)[0].reshape(128, 1)),
        w_branch_att=f(w_branch_att)[0], w_branch_gla=f(w_branch_gla)[0], w_out=f(w_out)[0], **consts)
    in_maps = []
    for c in range(NCORES):
        b, p = c // 4, c % 4
        seq = x_prompt[b].reshape(64, 128, D)
        xm = seq[16 * p:16 * (p + 1)]
        xp = np.zeros((NPRE, 128, D), np.float32)
        npv = 16 * p
        if npv:
            xp[NPRE - npv:] = seq[0:npv]
        flags = np.zeros((128, 64), np.float32)
        flags[:, NPRE - npv:NPRE] = 1.0
        flags[:, 48] = 1.0 if p > 0 else 0.0
        rope = np.zeros((18, 128, 64), np.float32)
        for i in range(16):
            rope[i] = _rope_table(2048 * p + 128 * i + np.arange(128))
        rope[16] = _rope_table(np.maximum(2048 * p - 128 + np.arange(128), 0))
        rope[17] = _rope_table(16384 + (np.arange(128) % 8))
        sl = slice(16 * c, 16 * (c + 1))
        m = dict(xm=np.ascontiguousarray(xm), xp=xp, xs=np.ascontiguousarray(x_sample[sl].reshape(128, D)),
                 cvec=np.ascontiguousarray(np.concatenate([c_prompt[b:b + 1], c_sample[sl]], 0)),
                 flags=flags, rope=rope,
                 ck=np.ascontiguousarray(ck[sl].reshape(16, 128, 128)), cv=np.ascontiguousarray(cv[sl].reshape(16, 128, 128)),
                 st0=np.ascontiguousarray(st0[sl]))
        m.update(shared)
        in_maps.append(m)
    res = run_bass_kernel_spmd(nc, in_maps, core_ids=list(range(NCORES)))
    R = res.results
    _NC_CACHE['last'] = R
    y_prompt = np.zeros((2, 8192, D), np.float32)
    for c in range(NCORES):
        b, p = c // 4, c % 4
        y_prompt[b, 2048 * p:2048 * (p + 1)] = R[c]["ym"].reshape(2048, D)
    y_sample = np.concatenate([R[c]["ys"].reshape(16, 8, D) for c in range(NCORES)], 0)
    pk = np.stack([R[3]["pk"], R[7]["pk"]], 0).reshape(1, 2, 128, 2, 64)
    pv = np.stack([R[3]["pv"], R[7]["pv"]], 0).reshape(1, 2, 128, 2, 64)
    pst = np.stack([R[3]["pst"], R[7]["pst"]], 0).reshape(1, 2, 4, 64, 128)
    sk = np.concatenate([R[c]["sk"] for c in range(NCORES)], 0).reshape(1, 128, 128, 2, 64)
    sv = np.concatenate([R[c]["sv"] for c in range(NCORES)], 0).reshape(1, 128, 128, 2, 64)
    sst = np.concatenate([R[c]["sst"] for c in range(NCORES)], 0).reshape(1, 128, 4, 64, 128)
    return (y_prompt, y_sample, pk.astype(np.float32), pv.astype(np.float32), pst.astype(np.float32),
            sk.astype(np.float32), sv.astype(np.float32), sst.astype(np.float32))
```

```python
import contextlib
import numpy as np
import concourse.bass as bass
import concourse.mybir as mybir
from concourse.bass_utils import run_bass_kernel_spmd

F32 = mybir.dt.float32
BF16 = mybir.dt.bfloat16
AF = mybir.ActivationFunctionType
ALU = mybir.AluOpType
AX = mybir.AxisListType

ENGS = ("pe", "act", "dve", "pool", "sp")
NCORES = 8
D = 1024
KC = 8
NMAIN = 16
NPRE = 48
INC = 4880
QA, KA, VA, ZA, QG, KG, VG, LR, ZG, MA, MG = 0, 512, 640, 768, 1280, 1536, 1792, 2304, 2320, 2832, 3856
EPS = 1e-6


import types


def _freeze(fn):
    if fn is None or fn.__closure__ is None:
        return fn
    cells = []
    for c in fn.__closure__:
        try:
            cells.append(types.CellType(c.cell_contents))
        except ValueError:
            cells.append(c)
    return types.FunctionType(fn.__code__, fn.__globals__, fn.__name__, fn.__defaults__, tuple(cells))


class Buf:
    def __init__(self, name, excl=False):
        self.name = name
        self.excl = excl
        self.w = None
        self.r = {}


class Sched:
    def __init__(self, nc):
        self.nc = nc
        self.ops = {e: [] for e in ENGS}
        self.cnt = {}
        self.seen = {e: {} for e in ENGS}
        self.sems = {}
        for e in ENGS:
            self.cnt["e_" + e] = 0

    def dma_sem(self, name):
        k = "d_" + name
        self.cnt[k] = 0
        return k

    def _need(self, eng, ev, waits):
        if ev is None:
            return
        k, v = ev[0], ev[1]
        if k.startswith("d_"):
            v = self.cnt[k]
        if self.seen[eng].get(k, 0) >= v:
            return
        waits[k] = max(waits.get(k, 0), v)

    def op(self, eng, fn, reads=(), writes=(), dsem=None):
        waits = {}
        is_dma = dsem is not None
        for b in reads:
            if b.w is not None:
                self._need(eng, b.w, waits)
            if b.excl:
                for e2, ev in b.r.items():
                    if e2 != eng or is_dma:
                        self._need(eng, ev, waits)
        for b in writes:
            if b.w is not None and (b.w[2] != eng or is_dma or b.w[0].startswith("d_") or eng != "pe"):
                self._need(eng, b.w, waits)
            for e2, ev in b.r.items():
                if e2 != eng or is_dma or ev[0].startswith("d_") or eng != "pe":
                    self._need(eng, ev, waits)
        for k, v in waits.items():
            self.seen[eng][k] = v
        if is_dma:
            self.cnt[dsem] += 16
            ev = (dsem, self.cnt[dsem], eng)
            inc = (dsem, 16)
        else:
            k = "e_" + eng
            self.cnt[k] += 1
            ev = (k, self.cnt[k], eng)
            inc = (k, 1)
        for b in reads:
            key = eng if not is_dma else "dma_" + dsem
            b.r[key] = (ev[0], ev[1])
        for b in writes:
            b.w = ev
            b.r = {}
        self.ops[eng].append((list(waits.items()), _freeze(fn), inc))
        return ev

    def wait_all(self, eng, exclude=()):
        waits = []
        for k, v in self.cnt.items():
            if k in exclude:
                continue
            if v > 0 and k != "e_" + eng and self.seen[eng].get(k, 0) < v:
                waits.append((k, v))
                self.seen[eng][k] = v
        self.ops[eng].append((waits, None, None))

    def emit(self):
        nc = self.nc
        with contextlib.ExitStack() as st:
            for k in self.cnt:
                self.sems[k] = st.enter_context(nc.semaphore(k))
            block = st.enter_context(nc.Block())
            handles = {"pe": block.tensor, "act": block.scalar, "dve": block.vector,
                       "pool": block.gpsimd, "sp": block.sync}
            sems = self.sems
            for e in ENGS:
                ops = self.ops[e]

                def body(eh, ops=ops):
                    for waits, fn, inc in ops:
                        for k, v in waits:
                            eh.wait_ge(sems[k], v)
                        if fn is not None:
                            ins = fn(eh)
                            ins.then_inc(sems[inc[0]], inc[1])
                handles[e](body)


class T:
    def __init__(self, t, name, excl=False):
        self.t = t
        self.b = Buf(name, excl)

    def __getitem__(self, k):
        return self.t[k]


def bc_ap(ap, dims):
    return bass.AP(ap.tensor, ap.offset, [list(ap.ap[0])] + [list(d) for d in dims])


class _Stop(Exception):
    pass


def build_nc(n_pre=NPRE, n_main=NMAIN, do_sample=True, stage=None, debug=False):
    def chk(n):
        if stage == n:
            raise _Stop()
    nc = bass.Bass("TRN2", target_bir_lowering=False)
    S = Sched(nc)
    es = contextlib.ExitStack()

    def din(name, shape):
        return nc.dram_tensor(name, list(shape), F32, kind="ExternalInput").ap()

    def dout(name, shape):
        return nc.dram_tensor(name, list(shape), F32, kind="ExternalOutput").ap()

    xm = din("xm", [NMAIN, 128, D]); xp = din("xp", [NPRE, 128, D]); xs_d = din("xs", [128, D])
    cvec = din("cvec", [17, D]); flags_d = din("flags", [128, 64]); rope_d = din("rope", [18, 128, 64])
    ck_d = din("ck", [16, 128, 128]); cv_d = din("cv", [16, 128, 128]); st0_d = din("st0", [16, 4, 64, 128])
    normg_d = din("normg", [128, 8]); normrow_d = din("normrow", [1, D])
    wada_d = din("w_ada", [D, 3072]); bada_d = din("b_ada", [1, 3072])
    win_d = din("w_in", [D, INC]); gqk_d = din("gqk", [128, 640]); sinks_d = din("sinks", [1, 1024])
    wgate_d = din("w_gla_gate", [16, 256]); bgate_d = din("b_gla_gate", [1, 256]); gdv_d = din("gdv", [128, 1])
    wba_d = din("w_branch_att", [512, D]); wbg_d = din("w_branch_gla", [512, D]); wout_d = din("w_out", [D, D])
    cU = din("cU", [128, 128]); cL = din("cL", [128, 128]); cUb = din("cUb", [128, 128])
    cUm1 = din("cUm1", [128, 128]); cUbm1 = din("cUbm1", [128, 128]); cI = din("cI", [128, 128])
    cSelS = din("cSelS", [17, 128]); cSelP = din("cSelP", [17, 128]); cOH = din("cOH", [128, 16])
    cC = din("cC", [128, 8]); cOnes = din("cOnes", [128, 128])

    ym = dout("ym", [NMAIN, 128, D]); ys = dout("ys", [128, D])
    pk_o = dout("pk", [128, 128]); pv_o = dout("pv", [128, 128]); pst_o = dout("pst", [4, 64, 128])
    sk_o = dout("sk", [16, 128, 128]); sv_o = dout("sv", [16, 128, 128]); sst_o = dout("sst", [16, 4, 64, 128])

    ptr = [16640]
    LIMIT = nc.SBUF_PARTITION_SIZE_BYTES

    def sb(name, shape, dt=F32, at=None):
        size = int(np.prod(shape[1:])) * (4 if dt == F32 else 2)
        size = (size + 31) // 32 * 32
        if at is None:
            off = ptr[0]
            ptr[0] += size
            assert ptr[0] <= LIMIT, (name, ptr[0], LIMIT)
        else:
            off = at[0]
            at[0] += size
        t = nc.alloc_sbuf_tensor_at(name, list(shape), dt, offset=off)
        return T(t, name)

    pf = [T(es.enter_context(nc.psum_tensor("pf%d" % i, [128, 512], F32)), "pf%d" % i, True) for i in range(6)]
    pbk = [T(es.enter_context(nc.psum_tensor("pb%d" % i, [128, 1024], BF16)), "pb%d" % i, True) for i in range(2)]
    dq = {"rec": None, "queue": [], "every": 5, "cnt": 0, "banks": []}

    rot = {"f": 0, "b": 0}

    pinned = set()

    pinned_b = set()

    def nf():
        while True:
            rot["f"] = (rot["f"] + 1) % 6
            if rot["f"] not in pinned:
                if dq["rec"] is not None:
                    pinned.add(rot["f"]); dq["banks"].append(pf[rot["f"]])
                return pf[rot["f"]]

    def pin(*banks):
        for b_ in banks:
            pinned.add(pf.index(b_))

    def unpin(*banks):
        for b_ in banks:
            pinned.discard(pf.index(b_))

    def nb():
        while True:
            rot["b"] = (rot["b"] + 1) % 2
            if rot["b"] not in pinned_b:
                if dq["rec"] is not None:
                    pinned_b.add(rot["b"]); dq["banks"].append(pbk[rot["b"]])
                return pbk[rot["b"]]

    def _reg(eng, fn, reads, writes, dsem):
        return S.op(eng, fn, reads=reads, writes=writes, dsem=dsem)

    def _emit(eng, fn, reads, writes, dsem):
        if dq["rec"] is not None:
            dq["rec"].append((eng, _freeze(fn), reads, writes, dsem))
            return None
        ev = _reg(eng, fn, reads, writes, dsem)
        if dq["queue"]:
            dq["cnt"] += 1
            if dq["cnt"] % dq["every"] == 0:
                _reg(*dq["queue"].pop(0))
                if not dq["queue"]:
                    _release_deferred()
        return ev

    def _release_deferred():
        for b_ in dq["banks"]:
            if b_ in pf:
                pinned.discard(pf.index(b_))
            else:
                pinned_b.discard(pbk.index(b_))
        dq["banks"] = []

    def defer_begin():
        assert dq["rec"] is None and not dq["queue"]
        dq["rec"] = []

    def defer_end():
        dq["queue"] = dq["rec"]
        dq["rec"] = None
        dq["cnt"] = 0
        if not dq["queue"]:
            _release_deferred()

    def defer_flush():
        while dq["queue"]:
            _reg(*dq["queue"].pop(0))
        _release_deferred()

    def OP(eng, fn, r=(), w=()):
        return _emit(eng, fn, [x.b for x in r], [x.b for x in w], None)

    def DMA(eng, out_ap, in_ap, sem, r=(), w=()):
        return _emit(eng, lambda e: e.dma_start(out=out_ap, in_=in_ap), [x.b for x in r], [x.b for x in w], sem)

    dram_sink = T(None, "dram_out")
    dbg_sem = S.dma_sem("dbg")
    dbg_seen = set()

    def DBG(name, tobj, ap):
        if not debug or name in dbg_seen:
            return
        dbg_seen.add(name)
        shp = list(ap.shape)
        d = nc.dram_tensor("dbg_" + name, shp, F32, kind="ExternalOutput").ap()
        if tobj.b.excl:
            raise ValueError("dump SBUF only")
        DMA("pool", d, ap, dbg_sem, r=[tobj])


    Win = sb("Win", [128, KC, INC], BF16)
    WB = [T(Win.t, "WB%d" % i) for i in range(3)]
    Wba = sb("Wba", [128, 4, D], BF16); Wbg = sb("Wbg", [128, 4, D], BF16); Wout = sb("Wout", [128, KC, D], BF16)

    def wbuf(c):
        if 512 <= c < 768 or 1536 <= c < 2320:
            return WB[0]
        if c < 1536:
            return WB[1]
        return WB[2]

    U4b = sb("U4b", [128, 4, 128], BF16); L4b = sb("L4b", [128, 4, 128], BF16); L4fb = sb("L4fb", [128, 4, 128], BF16)
    Ub4b = sb("Ub4b", [128, 4, 128], BF16); C4b = sb("C4b", [128, 8, 8], BF16)
    Uf = sb("Uf", [128, 128]); Um1f = sb("Um1f", [128, 128]); Ubf = sb("Ubf", [128, 128]); Ubm1f = sb("Ubm1f", [128, 128])
    onesf = sb("onesf", [128, 128]); onesb = sb("onesb", [128, 128], BF16); identb = sb("identb", [128, 128], BF16)
    OH = sb("OH", [128, 16]); selS = sb("selS", [17, 128]); selP = sb("selP", [17, 128])
    Gqk = sb("Gqk", [128, 640]); esink = sb("esink", [1, 1024]); gdv = sb("gdv", [128, 1])
    acs = sb("acs", [128, 16]); Gbc = sb("Gbc", [128, D], BF16)
    flg = sb("flg", [128, 64]); WgF = sb("WgF", [17, 256]); WgA = sb("WgA", [17, 256], BF16)
    Sst = sb("Sst", [128, 2, 128])
    VgH = None
    SstQ = [[T(Sst.t, "Sst_%d_%d" % (_j, _hl)) for _hl in range(2)] for _j in range(2)]
    SstAll = [SstQ[0][0], SstQ[0][1], SstQ[1][0], SstQ[1][1]]
    LRa = sb("LRa", [17, 128], BF16)
    ov0 = ptr[0]
    ov = [ov0]
    WadaC = [sb("WadaC%d" % i, [128, KC, 512], BF16, at=ov) for i in range(2)]
    badaC = [sb("badaC%d" % i, [1, 512], F32, at=ov) for i in range(2)]
    modtm = sb("modtm", [17, 3072], F32, at=ov)
    CV = sb("CV", [17, D], F32, at=ov); CE = sb("CE", [17, D], F32, at=ov); CSb = sb("CSb", [17, D], BF16, at=ov)
    sTc = sb("sTc", [128, KC, 32], BF16, at=ov); Grow = sb("Grow", [17, D], F32, at=ov)
    sinkrow = sb("sinkrow", [1, 1024], F32, at=ov)
    idf_s = sb("idf_s", [128, 128], F32, at=ov); cLf_s = sb("cLf_s", [128, 128], F32, at=ov); cCf_s = sb("cCf_s", [128, 8], F32, at=ov)
    Atm = sb("Atm", [128, D], BF16, at=ov); Stm = sb("Stm", [128, D], BF16, at=ov); Gtm = sb("Gtm", [128, D], BF16, at=ov)
    modx = nc.dram_tensor("modx", [3, 128, D], BF16).ap()
    assert ov[0] <= LIMIT
    X = [sb("X%d" % i, [128, D]) for i in range(2)]
    XN = sb("XN", [128, D], BF16)
    junk = XN; XH = XN; ss = sb("ss", [128, 1]); rstd = sb("rstd", [128, 1])
    hT = [sb("hT%d" % i, [128, KC, 128], BF16) for i in range(2)]
    for _h in hT:
        _h.k = [T(_h.t, _h.b.name + "_k%d" % _k) for _k in range(KC)]
    ropeT = [sb("rope%d" % i, [128, 64]) for i in range(2)]
    t1 = sb("t1", [128, 640]); tu = sb("tu", [128, 640]); tw = sb("tw", [128, 640]); QKb = sb("QKb", [128, 640], BF16)
    sq640 = sb("sq640", [128, 640], BF16); ssq = sb("ssq", [128, 10]); rq = sb("rq", [128, 10])
    qT = sb("qT", [128, 4, 128], BF16); kT = [sb("kT%d" % i, [128, 128], BF16) for i in range(2)]
    Vd = [sb("Vd%d" % i, [128, 2, 2, 64], BF16) for i in range(2)]; Vf = sb("Vf", [128, 128])
    PT = [sb("PT%d" % i, [128, 512], BF16) for i in range(2)]
    Zs = sb("Zs", [128, 4, 128], BF16); Rr = sb("Rr", [128, 512]); tt = sb("tt", [128, 512], BF16)
    OZ = sb("OZ", [128, 4, 128], BF16)
    Lg = sb("Lg", [128, 256]); Dk = sb("Dk", [128, 256]); kd = sb("kd", [128, 256], BF16)
    Eq = sb("Eq", [128, 2, 128]); Ek = sb("Ek", [128, 2, 128]); Elast = sb("Elast", [128, 2])
    qe = sb("qe", [128, 2, 128], BF16); ke = sb("ke", [128, 2, 2, 128], BF16); Sbm = sb("Sbm", [128, 2, 2, 128], BF16)
    Vg = sb("Vg", [128, 512], BF16); Am = sb("Am", [128, 4, 128], BF16)
    sqg = sb("sqg", [128, 512], BF16); rsg = Rr
    Zg = sb("Zg", [128, 4, 128], BF16); OGZ = sb("OGZ", [128, 4, 128], BF16)
    Et = sb("Et", [128, 512]); Et2 = sb("Et2", [128, 512])
    Mg = sb("Mg", [128, 16, 128], BF16); mT = sb("mT", [128, KC, 128], BF16); brt = sb("brt", [128, 512]); og = brt
    Y = sb("Y", [128, D])
    YH = [T(Y.t, "Y_h%d" % _i) for _i in range(2)]
    t1P = [T(t1.t, "t1_p%d" % _i) for _i in range(2)]
    sqP = [T(sq640.t, "sq_p%d" % _i) for _i in range(2)]
    QKbP = [T(QKb.t, "QKb_p%d" % _i) for _i in range(2)]
    tuH = [T(tu.t, "tu_h%d" % _i) for _i in range(2)]
    twH = [T(tw.t, "tw_h%d" % _i) for _i in range(2)]
    keH = [T(ke.t, "ke_h%d" % _i) for _i in range(2)]
    SbmH = [T(Sbm.t, "Sbm_h%d" % _i) for _i in range(2)]
    VgH = [T(Vg.t, "Vg_h%d" % _i) for _i in range(2)]
    OZc = [T(OZ.t, "OZ_c%d" % _i) for _i in range(4)]
    MgQ = [T(Mg.t, "Mg_q%d" % _i) for _i in range(4)]
    _xn_off = XN.t.manual_sbuf_range[0]
    MTM = [T(nc.alloc_sbuf_tensor_at("MTM%d" % _i, [128, 512], BF16, offset=_xn_off + 1024 * _i), "MTM%d" % _i) for _i in range(2)]
    _mt_off = mT.t.manual_sbuf_range[0]
    for _i in range(2):
        _t = T(nc.alloc_sbuf_tensor_at("PTx%d" % _i, [128, 512], BF16, offset=_mt_off + 1024 * _i), "PTx%d" % _i)
        _t.b = mT.b
        PT.append(_t)
    CKs = [sb("CKs%d" % i, [128, 128]) for i in range(4)]; CVs = [sb("CVs%d" % i, [128, 128]) for i in range(4)]
    CKb = [sb("CKb%d" % i, [128, 128], BF16) for i in range(2)]
    KcT = [sb("KcT%d" % i, [128, 128], BF16) for i in range(2)]; Vcd = [sb("Vcd%d" % i, [128, 2, 2, 64], BF16) for i in range(2)]
    PTc = [sb("PTc%d" % i, [128, 2, 4, 8], BF16) for i in range(2)]
    S0s = [sb("S0s%d" % i, [128, 2, 128]) for i in range(4)]; S0b = [sb("S0b%d" % i, [128, 2, 2, 128], BF16) for i in range(4)]
    kdm = [sb("kdm%d" % i, [128, 256], BF16) for i in range(2)]; SN = S0s
    ElS = sb("ElS", [128, 2, 16]); qTs = sb("qTs", [128, 16, 4, 8], BF16)

    sem_c = S.dma_sem("c"); sem_mx = S.dma_sem("mx"); sem_c0 = S.dma_sem("c0"); sem_cp = S.dma_sem("cp"); sem_c0p = S.dma_sem("c0p"); sem_w = [S.dma_sem("w%d" % i) for i in range(3)]; sem_wb = S.dma_sem("wb")
    sem_ada = [S.dma_sem("ada%d" % i) for i in range(2)]; sem_bada = [S.dma_sem("bada%d" % i) for i in range(2)]
    sem_x = [S.dma_sem("x%d" % i) for i in range(2)]; sem_y = S.dma_sem("y"); sem_r = [S.dma_sem("r%d" % i) for i in range(2)]
    sem_o = S.dma_sem("o"); sem_ck = [S.dma_sem("ck%d" % i) for i in range(4)]; sem_s0 = [S.dma_sem("s0%d" % i) for i in range(4)]
    sem_sn = [S.dma_sem("sn%d" % i) for i in range(4)]

    try:
        DMA("sp", CV.t[:], cvec, sem_c0, w=[CV])
        DMA("sp", onesf.t[:], cOnes, sem_c0, w=[onesf])
        DMA("sp", idf_s.t[:], cI, sem_c0, w=[idf_s])
        OP("dve", lambda e: e.tensor_copy(identb.t[:], idf_s.t[:]), r=[idf_s], w=[identb])
        DMA("sp", cLf_s.t[:], cL, sem_c, w=[cLf_s])
        DMA("sp", cCf_s.t[:], cC, sem_c, w=[cCf_s])
        for (dst, src) in [(Uf, cU), (Um1f, cUm1), (Ubf, cUb), (Ubm1f, cUbm1), (OH, cOH),
                           (selS, cSelS), (selP, cSelP), (Gqk, gqk_d), (gdv, gdv_d), (flg, flags_d),
                           (sinkrow, sinks_d)]:
            DMA("sp", dst.t[:], src, sem_c, w=[dst])
        chk(10)
        DMA("sp", Grow.t[:], bass.AP(normrow_d.tensor, normrow_d.offset, [[0, 17], [1, D]]), sem_c, w=[Grow])
        chk(11)
        chk(14)
        DMA("sp", WgF.t[0:16, :], wgate_d, sem_c, w=[WgF])
        DMA("sp", WgF.t[16:17, :], bgate_d, sem_c, w=[WgF])
        OP("dve", lambda e: e.tensor_copy(WgA.t[:], WgF.t[:]), r=[WgF], w=[WgA])

        wada_v = wada_d.rearrange("(kc p) c -> p kc c", p=128)
        win_v = win_d.rearrange("(kc p) c -> p kc c", p=128)

        def load_ada(n):
            DMA("pool", WadaC[n % 2].t[:], wada_v[:, :, n * 512:(n + 1) * 512], sem_ada[n % 2], w=[WadaC[n % 2]])
            DMA("sp", badaC[n % 2].t[:], bada_d[:, n * 512:(n + 1) * 512], sem_bada[n % 2], w=[badaC[n % 2]])

        chk(1)
        load_ada(0); load_ada(1)
        DMA("pool", Win.t[:, :, 512:768], win_v[:, :, 512:768], sem_w[0], w=[WB[0]])
        DMA("pool", Win.t[:, :, 1536:2320], win_v[:, :, 1536:2320], sem_w[0], w=[WB[0]])

        OP("dve", lambda e: e.tensor_copy(onesb.t[:], onesf.t[:]), r=[onesf], w=[onesb])
        OP("dve", lambda e: e.tensor_copy(U4b.t[:], bc_ap(Uf.t[:, 0:1], [(0, 4), (1, 128)])), r=[Uf], w=[U4b])
        OP("dve", lambda e: e.tensor_copy(L4b.t[:], bc_ap(cLf_s.t[:, 0:1], [(0, 4), (1, 128)])), r=[cLf_s], w=[L4b])
        OP("dve", lambda e: e.tensor_copy(Ub4b.t[:], bc_ap(Ubf.t[:, 0:1], [(0, 4), (1, 128)])), r=[Ubf], w=[Ub4b])
        OP("dve", lambda e: e.tensor_copy(C4b.t[:], bc_ap(cCf_s.t[:, 0:1], [(0, 8), (1, 8)])), r=[cCf_s], w=[C4b])
        OP("pool", lambda e: e.memset(LRa.t[:], 1.0), w=[LRa])
        OP("pool", lambda e: e.memset(Sst.t[:], 0.0), w=SstAll)

        OP("dve", lambda e: e.tensor_scalar_mul(L4fb.t[:], L4b.t[:], flg.t[:, 48:49]), r=[L4b, flg], w=[L4fb])
        OP("dve", lambda e: e.tensor_scalar_mul(Gqk.t[:, 0:512], Gqk.t[:, 0:512], 0.125), r=[Gqk], w=[Gqk])
        OP("act", lambda e: e.activation(out=esink.t[:], in_=sinkrow.t[:], func=AF.Exp), r=[sinkrow], w=[esink])

        chk(2)
        OP("act", lambda e: e.activation(out=CE.t[:], in_=CV.t[:], func=AF.Exp, scale=-1.0), r=[CV], w=[CE])
        OP("dve", lambda e: e.tensor_scalar_add(CE.t[:], CE.t[:], 1.0), r=[CE], w=[CE])
        OP("dve", lambda e: e.reciprocal(CE.t[:], CE.t[:]), r=[CE], w=[CE])
        OP("dve", lambda e: e.tensor_tensor(CSb.t[:], CV.t[:], CE.t[:], ALU.mult), r=[CV, CE], w=[CSb])
        pb0 = nb()
        for kc in range(KC):
            OP("pe", lambda e, kc=kc: e.transpose(pb0.t[:, kc * 32:kc * 32 + 17], CSb.t[0:17, kc * 128:(kc + 1) * 128],
                                                   identb.t[0:17, 0:17]), r=[CSb, identb], w=[pb0])
        OP("dve", lambda e: e.tensor_copy(sTc.t[:, :, 0:17], pb0.t[:, 0:256].rearrange("p (k c) -> p k c", c=32)[:, :, 0:17]),
           r=[pb0], w=[sTc])
        for n in range(6):
            bk = nf()
            for kc in range(KC):
                OP("pe", lambda e, kc=kc, n=n, bk=bk: e.matmul(bk.t[0:17, :], sTc.t[:, kc, 0:17], WadaC[n % 2].t[:, kc, :],
                                                               start=(kc == 0), stop=False), r=[sTc, WadaC[n % 2]], w=[bk])
            OP("pe", lambda e, n=n, bk=bk: e.matmul(bk.t[0:17, :], onesf.t[0:1, 0:17], badaC[n % 2].t[0:1, :],
                                                    start=False, stop=True), r=[onesf, badaC[n % 2]], w=[bk])
            OP("act", lambda e, n=n, bk=bk: e.copy(modtm.t[0:17, n * 512:(n + 1) * 512], bk.t[0:17, :]), r=[bk], w=[modtm])
            if n + 2 < 6:
                load_ada(n + 2)
        chk(3)
        DMA("pool", Win.t[:, :, 0:512], win_v[:, :, 0:512], sem_w[1], w=[WB[1]])
        DMA("pool", Win.t[:, :, 768:1536], win_v[:, :, 768:1536], sem_w[1], w=[WB[1]])
        for c0 in range(2320, INC, 640):
            DMA("pool", Win.t[:, :, c0:c0 + 640], win_v[:, :, c0:c0 + 640], sem_w[2], w=[WB[2]])
        DMA("pool", Wba.t[:], wba_d.rearrange("(kc p) c -> p kc c", p=128), sem_wb, w=[Wba])
        DMA("pool", Wbg.t[:], wbg_d.rearrange("(kc p) c -> p kc c", p=128), sem_wb, w=[Wbg])
        DMA("pool", Wout.t[:], wout_d.rearrange("(kc p) c -> p kc c", p=128), sem_wb, w=[Wout])

        chk(4)
        OP("dve", lambda e: e.tensor_scalar_add(modtm.t[:, 1024:2048], modtm.t[:, 1024:2048], 1.0), r=[modtm], w=[modtm])
        OP("dve", lambda e: e.tensor_tensor(modtm.t[:, 1024:2048], modtm.t[:, 1024:2048], Grow.t[:], ALU.mult),
           r=[modtm, Grow], w=[modtm])
        for (dst, sel, c0) in [(Atm, selS, 1024), (Stm, selS, 0), (Gtm, selS, 2048), (Gbc, selP, 2048)]:
            for n in range(2):
                bk = nf()
                OP("pe", lambda e, bk=bk, sel=sel, c0=c0, n=n: e.matmul(bk.t[:, :], sel.t[0:17, :],
                                                                        modtm.t[0:17, c0 + n * 512:c0 + (n + 1) * 512],
                                                                        start=True, stop=True), r=[sel, modtm], w=[bk])
                OP("act", lambda e, bk=bk, dst=dst, n=n: e.copy(dst.t[:, n * 512:(n + 1) * 512], bk.t[:, :]), r=[bk], w=[dst])
        for k_, t_ in enumerate((Atm, Stm, Gtm)):
            DMA("sp", modx[k_], t_.t[:], sem_mx, r=[t_])
        bk = nf()
        for kc in range(KC):
            OP("pe", lambda e, kc=kc, bk=bk: e.matmul(bk.t[:, kc:kc + 1], modtm.t[0:1, 1024 + kc * 128:1024 + (kc + 1) * 128],
                                                      onesf.t[0:1, 0:1], start=True, stop=True), r=[modtm, onesf], w=[bk])
            OP("pe", lambda e, kc=kc, bk=bk: e.matmul(bk.t[:, 8 + kc:9 + kc], modtm.t[0:1, kc * 128:(kc + 1) * 128],
                                                      onesf.t[0:1, 0:1], start=True, stop=True), r=[modtm, onesf], w=[bk])
        OP("dve", lambda e, bk=bk: e.tensor_copy(acs.t[:], bk.t[:, 0:16]), r=[bk], w=[acs])

        chk(5)
        for e_ in ENGS:
            S.wait_all(e_, exclude=set(sem_w) | {sem_wb})
        OP("pool", lambda e: e.memset(ke.t[:], 0.0), w=keH)
        OP("pool", lambda e: e.memset(Sbm.t[:], 0.0), w=SbmH)
        for i_ in range(4):
            OP("pool", lambda e, i_=i_: e.memset(S0b[i_].t[:], 0.0), w=[S0b[i_]])

        st = {"x": 0, "h": 0, "r": 0, "kv": 0}
        smod = {"A": None, "S": None}

        def front_dma(x_ap):
            xi = st["x"]; st["x"] ^= 1
            xs = X[xi]
            DMA("sp", xs.t[:], x_ap, sem_x[xi], w=[xs])
            return xs

        def front_a(x_ap, sample, xs=None):
            if xs is None:
                xs = front_dma(x_ap)
            OP("act", lambda e: e.activation(out=junk.t[:], in_=xs.t[:], func=AF.Square, accum_out=ss.t[:, 0:1]),
               r=[xs], w=[junk, ss])
            OP("act", lambda e: e.activation(out=rstd.t[:], in_=ss.t[:], func=AF.Ln, scale=1.0 / D, bias=EPS), r=[ss], w=[rstd])
            OP("act", lambda e: e.activation(out=rstd.t[:], in_=rstd.t[:], func=AF.Exp, scale=-0.5), r=[rstd], w=[rstd])
            chk(16)
            OP("pool", lambda e: e.tensor_tensor(XN.t[:], xs.t[:], bc_ap(rstd.t[:, 0:1], [(0, D)]), ALU.mult), r=[xs, rstd], w=[XN])
            DBG("rstd", rstd, rstd.t[:]); DBG("XN", XN, XN.t[:]); DBG("acs", acs, acs.t[:]); DBG("Gbc", Gbc, Gbc.t[:])
            chk(17)
            src = XN
            if sample:
                OP("dve", lambda e: e.tensor_tensor(XH.t[:], XN.t[:], smod["A"].t[:], ALU.mult), r=[XN, smod["A"]], w=[XH])
                OP("pool", lambda e: e.tensor_tensor(XH.t[:], XH.t[:], smod["S"].t[:], ALU.add), r=[XH, smod["S"]], w=[XH])
                src = XH
            return xs, src

        def front_b(src, sample):
            EV = "dve"
            bk = nb()
            bk2 = nb() if EV == "twobank" else bk
            bks = [bk, bk2]
            for kc in range(KC):
                OP("pe", lambda e, kc=kc: e.transpose(bks[kc % 2].t[:, kc * 128:(kc + 1) * 128], src.t[:, kc * 128:(kc + 1) * 128],
                                                       identb.t[:]), r=[src, identb], w=[bks[kc % 2]])
            chk(18)
            hi = st["h"]; st["h"] ^= 1
            h = hT[hi]
            if sample:
                assert bk2 is bk
                OP("dve", lambda e: e.tensor_copy(h.t[:], bk.t[:].rearrange("p (k c) -> p k c", c=128)), r=[bk], w=h.k)
            else:
                for kc in range(KC):
                    use_dve = {"mix": kc % 2 == 0, "twobank": kc % 2 == 0, "half": kc >= 4, "half2": kc < 4, "dve": True, "act": False}[EV]
                    bkc = bks[kc % 2]
                    OP("dve" if use_dve else "act",
                       (lambda e, kc=kc, bkc=bkc: e.tensor_scalar(h.t[:, kc, :], bkc.t[:, kc * 128:(kc + 1) * 128], acs.t[:, kc:kc + 1],
                                                         acs.t[:, 8 + kc:9 + kc], ALU.mult, ALU.add)) if use_dve else
                       (lambda e, kc=kc, bkc=bkc: e.activation(out=h.t[:, kc, :], in_=bkc.t[:, kc * 128:(kc + 1) * 128], func=AF.Identity,
                                                      scale=acs.t[:, kc:kc + 1], bias=acs.t[:, 8 + kc:9 + kc])),
                       r=[bkc, acs], w=[h.k[kc]])
            pass
            return h

        def proj_tm(h, c0, n, bk, off=0):
            for kc in range(KC):
                OP("pe", lambda e, kc=kc: e.matmul(bk.t[:, off:off + n], h.t[:, kc, :], Win.t[:, kc, c0:c0 + n],
                                                   start=(kc == 0), stop=(kc == KC - 1)), r=[h.k[kc], wbuf(c0)], w=[bk])

        def proj_fm(h, c0, m, bk, off):
            for kc in range(KC):
                OP("pe", lambda e, kc=kc: e.matmul(bk.t[0:m, off:off + 128], Win.t[:, kc, c0:c0 + m], h.t[:, kc, :],
                                                   start=(kc == 0), stop=(kc == KC - 1)), r=[h.k[kc], wbuf(c0)], w=[bk])

        def attn_kv(h, rope_idx, need_q, bq=None):
            ri = st["r"]; st["r"] ^= 1
            rp = ropeT[ri]
            DMA("sp", rp.t[:], rope_d[rope_idx], sem_r[ri], w=[rp])
            bkv = nf()
            proj_tm(h, KA, 256, bkv)
            nh = 10 if need_q else 2
            c0 = 0 if need_q else 512
            w = nh * 64
            if need_q:
                OP("act", lambda e: e.activation(out=sq640.t[:, 0:512], in_=bq.t[:, :], func=AF.Square), r=[bq], w=[sqP[0]])
            OP("act", lambda e: e.activation(out=sq640.t[:, 512:640], in_=bkv.t[:, 0:128], func=AF.Square), r=[bkv], w=[sqP[1]])
            OP("dve", lambda e: e.tensor_reduce(ssq.t[:, 10 - nh:10], sq640.t[:, c0:640].rearrange("p (h d) -> p h d", d=64),
                                                AX.X, ALU.add), r=(sqP if need_q else sqP[1:]), w=[ssq])
            OP("act", lambda e: e.activation(out=rq.t[:, 10 - nh:10], in_=ssq.t[:, 10 - nh:10], func=AF.Ln, scale=1.0 / 64, bias=EPS),
               r=[ssq], w=[rq])
            OP("act", lambda e: e.activation(out=rq.t[:, 10 - nh:10], in_=rq.t[:, 10 - nh:10], func=AF.Exp, scale=-0.5), r=[rq], w=[rq])
            if need_q:
                OP("dve", lambda e: e.tensor_tensor(t1.t[:, 0:512].rearrange("p (h d) -> p h d", d=64),
                                                    bq.t[:, :].rearrange("p (h d) -> p h d", d=64),
                                                    bc_ap(rq.t[:, 0:8], [(1, 8), (0, 64)]), ALU.mult), r=[bq, rq], w=[t1P[0]])
            OP("dve", lambda e: e.tensor_tensor(t1.t[:, 512:640].rearrange("p (h d) -> p h d", d=64),
                                                bkv.t[:, 0:128].rearrange("p (h d) -> p h d", d=64),
                                                bc_ap(rq.t[:, 8:10], [(1, 2), (0, 64)]), ALU.mult), r=[bkv, rq], w=[t1P[1]])
            OP("pool", lambda e: e.tensor_tensor(t1.t[:, c0:640], t1.t[:, c0:640], Gqk.t[:, c0:640], ALU.mult), r=t1P + [Gqk], w=t1P)
            t1v = t1.t[:, c0:640].rearrange("p (h t d) -> p h t d", t=2, d=32)
            tuv = tu.t[:, c0:640].rearrange("p (h t d) -> p h t d", t=2, d=32)
            twv = tw.t[:, c0:640].rearrange("p (h t d) -> p h t d", t=2, d=32)
            cosb = bc_ap(rp.t[:, 0:32], [(0, nh), (0, 2), (1, 32)])
            sinb = bc_ap(rp.t[:, 32:64], [(0, nh), (1, 32)])
            OP("dve", lambda e: e.tensor_tensor(tuv, t1v, cosb, ALU.mult), r=t1P + [rp], w=tuH)
            OP("pool", lambda e: e.tensor_tensor(twv[:, :, 0, :], t1v[:, :, 1, :], sinb, ALU.mult), r=t1P + [rp], w=[twH[0]])
            OP("pool", lambda e: e.tensor_tensor(twv[:, :, 1, :], t1v[:, :, 0, :], sinb, ALU.mult), r=t1P + [rp], w=[twH[1]])
            OP("dve", lambda e: e.tensor_tensor(tuv[:, :, 0, :], tuv[:, :, 0, :], twv[:, :, 0, :], ALU.subtract), r=[tuH[0], twH[0]], w=[tuH[0]])
            OP("dve", lambda e: e.tensor_tensor(tuv[:, :, 1, :], tuv[:, :, 1, :], twv[:, :, 1, :], ALU.add), r=[tuH[1], twH[1]], w=[tuH[1]])
            OP("pool", lambda e: e.tensor_copy(QKb.t[:, 512:640], tu.t[:, 512:640]), r=tuH, w=[QKbP[1]])
            if need_q:
                OP("pool", lambda e: e.tensor_copy(QKb.t[:, 0:512].rearrange("p (a two d) -> p two a d", two=2, a=4),
                                                   tu.t[:, 0:512].rearrange("p (two a d) -> p two a d", two=2, a=4)), r=tuH, w=[QKbP[0]])
            kvi = st["kv"]; st["kv"] ^= 1
            bt = nb()
            OP("pe", lambda e: e.transpose(bt.t[:, 512:640], QKb.t[:, 512:640], identb.t[:]), r=[QKbP[1], identb], w=[bt])
            if need_q:
                for a in range(4):
                    OP("pe", lambda e, a=a: e.transpose(bt.t[:, a * 128:(a + 1) * 128], QKb.t[:, a * 128:(a + 1) * 128], identb.t[:]),
                       r=[QKbP[0], identb], w=[bt])
                OP("act", lambda e: e.copy(qT.t[:], bt.t[:, 0:512].rearrange("p (a q) -> p a q", a=4)), r=[bt], w=[qT])
            OP("act", lambda e: e.copy(kT[kvi].t[:], bt.t[:, 512:640]), r=[bt], w=[kT[kvi]])
            if need_q:
                pass
            vsrc = bass.AP(bkv.t[:, 128:256].tensor, bkv.t[:, 128:256].offset,
                           [list(bkv.t[:, 128:256].ap[0]), [64, 2], [0, 2], [1, 64]])
            OP("dve", lambda e: e.tensor_copy(Vd[kvi].t[:], vsrc), r=[bkv], w=[Vd[kvi]])
            OP("act", lambda e: e.copy(Vf.t[:], bkv.t[:, 128:256]), r=[bkv], w=[Vf])
            if need_q:
                DBG("Vf", Vf, Vf.t[:]); DBG("Vd", Vd[kvi], Vd[kvi].t[:])
            return kvi

        def gla_gates(h, bkg, bvg, flag_col, sample):
            bl = nf()
            proj_fm(h, LR, 16, bl, 0)
            OP("act", lambda e: e.copy(LRa.t[0:16, :], bl.t[0:16, 0:128]), r=[bl], w=[LRa])
            bg_ = nf()
            OP("pe", lambda e: e.matmul(bg_.t[:, 0:256], LRa.t[0:17, :], WgA.t[0:17, :], start=True, stop=True),
               r=[LRa, WgA], w=[bg_])
            OP("act", lambda e: e.activation(out=Lg.t[:], in_=bg_.t[:, 0:256], func=AF.Exp, scale=-1.0), r=[bg_], w=[Lg])
            OP("act", lambda e: e.activation(out=Lg.t[:], in_=Lg.t[:], func=AF.Ln, scale=1.0, bias=1.0), r=[Lg], w=[Lg])
            um1 = Ubm1f if sample else Um1f
            be = nf()
            OP("pe", lambda e: e.matmul(be.t[:, 0:256], um1.t[:], Lg.t[:], start=True, stop=True), r=[um1, Lg], w=[be])
            OP("act", lambda e: e.activation(out=Dk.t[:], in_=be.t[:, 0:256], func=AF.Exp, scale=1.0 / 16), r=[be], w=[Dk])
            OP("dve", lambda e: e.tensor_tensor(kd.t[:], bkg.t[:, 0:256], Dk.t[:], ALU.mult), r=[bkg, Dk], w=[kd])
            if flag_col is None:
                OP("act", lambda e: e.copy(Vg.t[:, 0:256], bkg.t[:, 256:512]), r=[bkg], w=[VgH[0]])
                OP("act", lambda e: e.copy(Vg.t[:, 256:512], bvg.t[:, 0:256]), r=[bvg], w=[VgH[1]])
            else:
                OP("act", lambda e: e.activation(out=Vg.t[:, 0:256], in_=bkg.t[:, 256:512], func=AF.Identity,
                                                 scale=flg.t[:, flag_col:flag_col + 1]), r=[bkg, flg], w=[VgH[0]])
                OP("act", lambda e: e.activation(out=Vg.t[:, 256:512], in_=bvg.t[:, 0:256], func=AF.Identity,
                                                 scale=flg.t[:, flag_col:flag_col + 1]), r=[bvg, flg], w=[VgH[1]])

        def gla_state_update():
            bt_ = nf()
            for j in range(2):
                OP("pe", lambda e, j=j: e.matmul(bt_.t[:, j:j + 1], Lg.t[:, 128 * j:128 * (j + 1)], onesf.t[:, 0:1],
                                                 start=True, stop=True), r=[Lg, onesf], w=[bt_])
            OP("act", lambda e: e.activation(out=Elast.t[:], in_=bt_.t[:, 0:2], func=AF.Exp, scale=-1.0 / 16), r=[bt_], w=[Elast])
            bs = nf()
            for j in range(2):
                OP("pe", lambda e, j=j: e.matmul(bs.t[:, 256 * j:256 * (j + 1)], kd.t[:, 128 * j:128 * (j + 1)],
                                                 Vg.t[:, 256 * j:256 * (j + 1)], start=True, stop=True), r=[kd, VgH[j]], w=[bs])
            for j in range(2):
                for hl in range(2):
                    OP("dve", lambda e, j=j, hl=hl: e.scalar_tensor_tensor(
                        out=Sst.t[64 * hl:64 * (hl + 1), j, :], in0=Sst.t[64 * hl:64 * (hl + 1), j, :],
                        scalar=Elast.t[64 * hl:64 * (hl + 1), j:j + 1],
                        in1=bs.t[64 * hl:64 * (hl + 1), 256 * j + 128 * hl:256 * j + 128 * (hl + 1)],
                        op0=ALU.mult, op1=ALU.add), r=[SstQ[j][hl], Elast, bs], w=[SstQ[j][hl]])
            for hl in range(2):
                OP("pool", lambda e, hl=hl: e.tensor_copy(Sbm.t[64 * hl:64 * (hl + 1), :, hl, :], Sst.t[64 * hl:64 * (hl + 1), :, :]),
                   r=[SstQ[0][hl], SstQ[1][hl]], w=[SbmH[hl]])

        def sigmoid_from_psum(dst_ap, src_ap, src_t, dst_t, tmp):
            OP("act", lambda e: e.activation(out=tmp.t[:], in_=src_ap, func=AF.Exp, scale=-1.0), r=[src_t], w=[tmp])
            OP("act", lambda e: e.activation(out=tmp.t[:], in_=tmp.t[:], func=AF.Ln, scale=1.0, bias=1.0), r=[tmp], w=[tmp])
            OP("act", lambda e: e.activation(out=dst_ap, in_=tmp.t[:], func=AF.Exp, scale=-1.0), r=[tmp], w=[dst_t])

        def attn_prep(h, rope_idx, sample, last):
            bq = nf()
            proj_tm(h, QA, 512, bq)
            cur = attn_kv(h, rope_idx, True, bq)
            if last:
                DMA("sp", pk_o, tu.t[:, 512:640], sem_o, r=tuH)
                DMA("sp", pv_o, Vf.t[:], sem_o, r=[Vf])
            if sample:
                for i in range(16):
                    DMA("sp", sk_o[i, 120:128, :], tu.t[8 * i:8 * (i + 1), 512:640], sem_o, r=tuH)
                    DMA("sp", sv_o[i, 120:128, :], Vf.t[8 * i:8 * (i + 1), :], sem_o, r=[Vf])
            return cur

        def full_tile(xs, h, y_ap, rope_idx, sample, first, last, hook_a=None, hook_b=None, cur=None):
            chk(20)
            if cur is None:
                cur = attn_prep(h, rope_idx, sample, last)
            prev = cur ^ 1
            chk(21)
            def z_a_part():
                bz = nf()
                for c in range(4):
                    proj_fm(h, ZA + 128 * c, 128, bz, 128 * c)
                sigmoid_from_psum(Et2.t[:], bz.t[:, :], bz, Et2, Et)
                OP("dve", lambda e: e.tensor_tensor(Zs.t[:].rearrange("p a q -> p (a q)"), bz.t[:, :], Et2.t[:], ALU.mult),
                   r=[bz, Et2], w=[Zs])
                DBG("Zs", Zs, Zs.t[:])
            if sample:
                z_a_part()
            chk(22)
            mcur = Ub4b if sample else U4b
            mprev = L4fb if first else L4b
            def scores_part(kvh):
                rows = slice(64 * kvh, 64 * (kvh + 1))
                blocks = [(cur, mcur)] if sample else [(prev, mprev), (cur, mcur)]
                pts = []
                for bi, (slot, msk) in enumerate(blocks):
                    bsc = nf()
                    OP("pe", lambda e, slot=slot, bsc=bsc: e.matmul(bsc.t[:, :], kT[slot].t[rows, :],
                                                                    qT.t[rows, :, :].rearrange("p a q -> p (a q)"),
                                                                    start=True, stop=True), r=[kT[slot], qT], w=[bsc])
                    pt = PT[bi] if sample else PT[2 * kvh + bi]
                    OP("act", lambda e, pt=pt, bsc=bsc: e.activation(out=pt.t[:], in_=bsc.t[:, :], func=AF.Exp), r=[bsc], w=[pt])
                    OP("pool", lambda e, pt=pt, msk=msk: e.tensor_tensor(pt.t[:], pt.t[:], msk.t[:].rearrange("p a q -> p (a q)"),
                                                                         ALU.mult), r=[pt, msk], w=[pt])
                    pts.append((pt, slot))
                return pts

            all_pts = {}
            if not sample:
                for kvh in range(2):
                    all_pts[kvh] = scores_part(kvh)
                if hook_a is not None:
                    hook_a()
                    hook_a = None
                z_a_part()
            for kvh in range(2):
                rows = slice(64 * kvh, 64 * (kvh + 1))
                pts = scores_part(kvh) if sample else all_pts[kvh]
                bo = nf(); bd = nf()
                for bi, (pt, slot) in enumerate(pts):
                    OP("pe", lambda e, pt=pt, slot=slot, bi=bi: e.matmul(bo.t[:, :], Vd[slot].t[:, kvh, :, :].rearrange("p u d -> p (u d)"),
                                                                         pt.t[:], start=(bi == 0), stop=(bi == len(pts) - 1)),
                       r=[Vd[slot], pt], w=[bo])
                    OP("pe", lambda e, pt=pt, bi=bi: e.matmul(bd.t[:, :], onesb.t[:], pt.t[:], start=(bi == 0), stop=False),
                       r=[onesb, pt], w=[bd])
                OP("pe", lambda e: e.matmul(bd.t[:, :], onesf.t[0:1, :], esink.t[0:1, 512 * kvh:512 * (kvh + 1)],
                                            start=False, stop=True), r=[onesf, esink], w=[bd])
                if sample:
                    pin(bo, bd)
                    boc, bdc = sample_cache_attn(kvh)
                    unpin(bo, bd, boc, bdc)
                    OP("act", lambda e, bdc=bdc: e.copy(Et.t[:], bdc.t[:, :]), r=[bdc], w=[Et])
                    OP("dve", lambda e: e.tensor_tensor(Et2.t[:].rearrange("p (a i q) -> p i a q", a=4, q=8),
                                                        bd.t[:, :].rearrange("p (a i q) -> p i a q", a=4, q=8),
                                                        Et.t[:].rearrange("p (i a q) -> p i a q", a=4, q=8), ALU.add),
                       r=[bd, Et], w=[Et2])
                    OP("act", lambda e: e.activation(out=Rr.t[:], in_=Et2.t[:], func=AF.Ln), r=[Et2], w=[Rr])
                    OP("act", lambda e: e.activation(out=Rr.t[:], in_=Rr.t[:], func=AF.Exp, scale=-1.0), r=[Rr], w=[Rr])
                    OP("act", lambda e, boc=boc: e.copy(Et.t[:], boc.t[:, :]), r=[boc], w=[Et])
                    OP("dve", lambda e: e.tensor_tensor(Et2.t[:].rearrange("p (a i q) -> p i a q", a=4, q=8),
                                                        bo.t[:, :].rearrange("p (a i q) -> p i a q", a=4, q=8),
                                                        Et.t[:].rearrange("p (i a q) -> p i a q", a=4, q=8), ALU.add),
                       r=[bo, Et], w=[Et2])
                    OP("dve", lambda e: e.tensor_tensor(tt.t[:], Et2.t[:], Rr.t[:], ALU.mult), r=[Et2, Rr], w=[tt])
                else:
                    Rk = Rr if kvh == 0 else Et2
                    tk = tt if kvh == 0 else sqg
                    OP("act", lambda e: e.activation(out=Rk.t[:], in_=bd.t[:, :], func=AF.Ln), r=[bd], w=[Rk])
                    OP("act", lambda e: e.activation(out=Rk.t[:], in_=Rk.t[:], func=AF.Exp, scale=-1.0), r=[Rk], w=[Rk])
                    OP("dve", lambda e: e.tensor_tensor(tk.t[:], bo.t[:, :], Rk.t[:], ALU.mult), r=[bo, Rk], w=[tk])
                tk_ = tt if (sample or kvh == 0) else sqg
                for a2 in range(2):
                    c = 2 * kvh + a2
                    for par in range(2):
                        pr = slice(64 * par, 64 * (par + 1))
                        a = 2 * a2 + par
                        OP("pool", lambda e, c=c, pr=pr, a=a: e.tensor_tensor(OZ.t[pr, c, :], tk_.t[pr, 128 * a:128 * (a + 1)],
                                                                              Zs.t[pr, c, :], ALU.mult), r=[tk_, Zs], w=[OZc[c]])
            chk(23)
            if hook_a is not None:
                hook_a()
            if hook_b is not None:
                hook_b()
            DBG("OZ", OZ, OZ.t[:]); DBG("tt", tt, tt.t[:]); DBG("Rr", Rr, Rr.t[:]); DBG("PT1", PT[1], PT[1].t[:])
            bkg = nf(); pin(bkg)
            bvg = nf(); pin(bvg)
            proj_tm(h, KG, 512, bkg)
            proj_tm(h, VG + 256, 256, bvg)
            gla_gates(h, bkg, bvg, None, sample)
            unpin(bkg, bvg)
            chk(24)
            DBG("Lg", Lg, Lg.t[:]); DBG("Dk", Dk, Dk.t[:]); DBG("kd", kd, kd.t[:]); DBG("Vg", Vg, Vg.t[:])
            um = Ubf if sample else Uf
            bb = nf()
            for j in range(2):
                OP("pe", lambda e, j=j: e.matmul(bb.t[:, 128 * j:128 * (j + 1)], Lg.t[:, 128 * j:128 * (j + 1)], um.t[:],
                                                 start=True, stop=True), r=[Lg, um], w=[bb])
            OP("act", lambda e: e.activation(out=Eq.t[:].rearrange("p j t -> p (j t)"), in_=bb.t[:, 0:256], func=AF.Exp,
                                             scale=-1.0 / 16), r=[bb], w=[Eq])
            OP("act", lambda e: e.activation(out=Ek.t[:].rearrange("p j t -> p (j t)"), in_=bb.t[:, 0:256], func=AF.Exp,
                                             scale=1.0 / 16), r=[bb], w=[Ek])
            bqk = nf()
            for j in range(2):
                proj_fm(h, QG + 128 * j, 128, bqk, 128 * j)
                proj_fm(h, KG + 128 * j, 128, bqk, 256 + 128 * j)
            OP("dve", lambda e: e.scalar_tensor_tensor(out=qe.t[:].rearrange("p j t -> p (j t)"), in0=bqk.t[:, 0:256], scalar=0.125,
                                                       in1=Eq.t[:].rearrange("p j t -> p (j t)"), op0=ALU.mult, op1=ALU.mult),
               r=[bqk, Eq], w=[qe])
            for hl in range(2):
                pr = slice(64 * hl, 64 * (hl + 1))
                OP("dve", lambda e, hl=hl, pr=pr: e.tensor_tensor(ke.t[pr, hl, :, :], bqk.t[pr, 256:512].rearrange("p (j t) -> p j t", j=2),
                                                                  Ek.t[pr, :, :], ALU.mult), r=[bqk, Ek], w=[keH[hl]])
            ba = nf()
            for hh in range(4):
                j, hl = hh // 2, hh % 2
                pr = slice(64 * hl, 64 * (hl + 1))
                OP("pe", lambda e, hh=hh, j=j, hl=hl: e.matmul(ba.t[:, 128 * hh:128 * (hh + 1)], ke.t[:, hl, j, :], qe.t[:, j, :],
                                                               start=True, stop=True), r=[keH[hl], qe], w=[ba])
            OP("dve", lambda e: e.tensor_tensor(Am.t[:].rearrange("p a q -> p (a q)"), ba.t[:, :],
                                                mcur.t[:].rearrange("p a q -> p (a q)"), ALU.mult), r=[ba, mcur], w=[Am])
            bog = nf()
            for hh in range(4):
                j, hl = hh // 2, hh % 2
                pr = slice(64 * hl, 64 * (hl + 1))
                OP("pe", lambda e, hh=hh: e.matmul(bog.t[:, 128 * hh:128 * (hh + 1)], Vg.t[:, 128 * hh:128 * (hh + 1)],
                                                   Am.t[:, hh, :], start=True, stop=sample), r=[VgH[hh // 2], Am], w=[bog])
                if not sample:
                    OP("pe", lambda e, hh=hh, j=j, hl=hl: e.matmul(bog.t[:, 128 * hh:128 * (hh + 1)], Sbm.t[:, j, hl, :], qe.t[:, j, :],
                                                                   start=False, stop=True), r=[SbmH[hl], qe], w=[bog])
            chk(25)
            pass
            if sample:
                pin(bog)
                bint = sample_gla_state()
                unpin(bog, bint)
                OP("act", lambda e: e.copy(Et.t[:], bint.t[:, :]), r=[bint], w=[Et])
                OP("dve", lambda e: e.tensor_tensor(Et2.t[:], bog.t[:, :], Et.t[:], ALU.add), r=[bog, Et], w=[Et2])
                osrc, osrc_t = Et2.t[:], Et2
            else:
                gla_state_update()
                osrc, osrc_t = bog.t[:, :], bog
            chk(26)
            OP("act", lambda e: e.activation(out=sqg.t[:], in_=osrc, func=AF.Square), r=[osrc_t], w=[sqg])
            bn = nf()
            OP("pe", lambda e: e.matmul(bn.t[:, :], onesb.t[:], sqg.t[:], start=True, stop=True), r=[onesb, sqg], w=[bn])
            OP("act", lambda e: e.activation(out=rsg.t[:], in_=bn.t[:, :], func=AF.Ln, scale=1.0 / 128, bias=EPS), r=[bn], w=[rsg])
            OP("act", lambda e: e.activation(out=rsg.t[:], in_=rsg.t[:], func=AF.Exp, scale=-0.5), r=[rsg], w=[rsg])
            OP("dve", lambda e: e.tensor_tensor(og.t[:], osrc, rsg.t[:], ALU.mult), r=[osrc_t, rsg], w=[og])
            bzg = nf()
            for c in range(4):
                proj_fm(h, ZG + 128 * c, 128, bzg, 128 * c)
            sigmoid_from_psum(Et2.t[:], bzg.t[:, :], bzg, Et2, Et)
            OP("dve", lambda e: e.tensor_tensor(Zg.t[:].rearrange("p a q -> p (a q)"), bzg.t[:, :], Et2.t[:], ALU.mult),
               r=[bzg, Et2], w=[Zg])
            OP("dve", lambda e: e.scalar_tensor_tensor(out=OGZ.t[:].rearrange("p a q -> p (a q)"), in0=og.t[:], scalar=gdv.t[:, 0:1],
                                                       in1=Zg.t[:].rearrange("p a q -> p (a q)"), op0=ALU.mult, op1=ALU.mult),
               r=[og, gdv, Zg], w=[OGZ])
            chk(27)
            pass
            Mgt = Mg.t[:].rearrange("p a q -> p (a q)")
            for g4 in range(4):
                bm = nf()
                proj_tm(h, MA + 512 * g4, 512, bm)
                sigmoid_from_psum(Mgt[:, 512 * g4:512 * (g4 + 1)], bm.t[:, :], bm, MgQ[g4], Et if g4 % 2 == 0 else Et2)
            chk(28)
            for n in range(2):
                bra = nf(); brg = nf()
                for kc in range(4):
                    OP("pe", lambda e, kc=kc, n=n: e.matmul(bra.t[:, :], OZ.t[:, kc, :], Wba.t[:, kc, 512 * n:512 * (n + 1)],
                                                            start=(kc == 0), stop=(kc == 3)), r=[Wba, OZc[kc]], w=[bra])
                for kc in range(4):
                    OP("pe", lambda e, kc=kc, n=n: e.matmul(brg.t[:, :], OGZ.t[:, kc, :], Wbg.t[:, kc, 512 * n:512 * (n + 1)],
                                                            start=(kc == 0), stop=(kc == 3)), r=[Wbg, OGZ], w=[brg])
                ba_, bg2_ = (brt, Et) if n == 0 else (Rr, Et2)
                OP("dve", lambda e, n=n: e.tensor_tensor(ba_.t[:], bra.t[:, :], Mgt[:, 512 * n:512 * (n + 1)], ALU.mult),
                   r=[bra, MgQ[n]], w=[ba_])
                OP("dve", lambda e, n=n: e.tensor_tensor(bg2_.t[:], brg.t[:, :], Mgt[:, 1024 + 512 * n:1024 + 512 * (n + 1)], ALU.mult),
                   r=[brg, MgQ[2 + n]], w=[bg2_])
                OP("pool", lambda e, n=n: e.tensor_tensor(MTM[n].t[:], ba_.t[:], bg2_.t[:], ALU.add),
                   r=[ba_, bg2_, XN], w=[MTM[n]])
            btm = nb()
            for kc in range(KC):
                OP("pe", lambda e, kc=kc: e.transpose(btm.t[:, kc * 128:(kc + 1) * 128],
                                                       MTM[kc // 4].t[:, (kc % 4) * 128:(kc % 4 + 1) * 128], identb.t[:]),
                   r=[MTM[kc // 4], identb], w=[btm])
            OP("act", lambda e: e.copy(mT.t[:].rearrange("p a q -> p (a q)"), btm.t[:, :]), r=[btm, XN], w=[mT])
            DBG("mT", mT, mT.t[:])
            chk(29)
            gt = Gbc
            for n in range(2):
                by = nf()
                for kc in range(KC):
                    OP("pe", lambda e, kc=kc, n=n: e.matmul(by.t[:, :], mT.t[:, kc, :], Wout.t[:, kc, 512 * n:512 * (n + 1)],
                                                            start=(kc == 0), stop=(kc == KC - 1)), r=[mT, Wout], w=[by])
                chk(33)
                OP("dve", lambda e, n=n: e.tensor_tensor(Y.t[:, 512 * n:512 * (n + 1)], by.t[:, :], gt.t[:, 512 * n:512 * (n + 1)],
                                                         ALU.mult), r=[by, gt], w=[YH[n]])
                chk(32)
                OP("dve", lambda e, n=n: e.tensor_tensor(Y.t[:, 512 * n:512 * (n + 1)], Y.t[:, 512 * n:512 * (n + 1)],
                                                          xs.t[:, 512 * n:512 * (n + 1)], ALU.add), r=[YH[n], xs], w=[YH[n]])
            chk(30)
            DMA("sp", y_ap, Y.t[:], sem_y, r=YH)
            chk(31)

        smp = {"i": 0}

        def sample_cache_attn(kvh):
            rows = slice(64 * kvh, 64 * (kvh + 1))
            if kvh == 0:
                OP("dve", lambda e: e.tensor_copy(qTs.t[:], qT.t[:].rearrange("p a (i q) -> p i a q", q=8)), r=[qT], w=[qTs])
            boc = nf(); bdc = nf()
            pin(boc, bdc)
            base = smp["i"]; smp["i"] += 16

            def stage1(i):
                sl = (base + i) % 2; s4 = (base + i) % 4
                DMA("sp", CKs[s4].t[:], ck_d[i], sem_ck[s4], w=[CKs[s4]])
                DMA("sp", CVs[s4].t[:], cv_d[i], sem_ck[s4], w=[CVs[s4]])
                OP("dve", lambda e: e.tensor_copy(CKb[sl].t[:], CKs[s4].t[:]), r=[CKs[s4]], w=[CKb[sl]])
                bt = nb()
                OP("pe", lambda e: e.transpose(bt.t[:, 0:128], CKb[sl].t[:], identb.t[:]), r=[CKb[sl], identb], w=[bt])
                OP("act", lambda e: e.copy(KcT[sl].t[:], bt.t[:, 0:128]), r=[bt], w=[KcT[sl]])
                vsrc = bass.AP(CVs[s4].t[:].tensor, CVs[s4].t[:].offset, [list(CVs[s4].t[:].ap[0]), [64, 2], [0, 2], [1, 64]])
                OP("dve", lambda e: e.tensor_copy(Vcd[sl].t[:], vsrc), r=[CVs[s4]], w=[Vcd[sl]])

            def stage2(i):
                sl = (base + i) % 2
                bsc = nf()
                OP("pe", lambda e: e.matmul(bsc.t[:, 0:32], KcT[sl].t[rows, :], qTs.t[rows, i, :, :].rearrange("p a q -> p (a q)"),
                                            start=True, stop=True), r=[KcT[sl], qTs], w=[bsc])
                OP("act", lambda e: e.activation(out=PTc[sl].t[:, 0, :, :].rearrange("p a q -> p (a q)"),
                                                 in_=bsc.t[:, 0:32], func=AF.Exp), r=[bsc], w=[PTc[sl]])
                OP("pool", lambda e: e.tensor_tensor(PTc[sl].t[:, 0, :, :], PTc[sl].t[:, 0, :, :], C4b.t[:, 0:4, :], ALU.mult),
                   r=[PTc[sl], C4b], w=[PTc[sl]])

            def stage3(i):
                sl = (base + i) % 2
                OP("pe", lambda e: e.matmul(boc.t[:, 32 * i:32 * (i + 1)], Vcd[sl].t[:, kvh, :, :].rearrange("p u d -> p (u d)"),
                                            PTc[sl].t[:, 0, :, :].rearrange("p a q -> p (a q)"), start=True, stop=True),
                   r=[Vcd[sl], PTc[sl]], w=[boc])
                OP("pe", lambda e: e.matmul(bdc.t[:, 32 * i:32 * (i + 1)], onesb.t[:],
                                            PTc[sl].t[:, 0, :, :].rearrange("p a q -> p (a q)"), start=True, stop=True),
                   r=[onesb, PTc[sl]], w=[bdc])

            for t_ in range(18):
                if 0 <= t_ - 2 < 16:
                    stage3(t_ - 2)
                if 0 <= t_ - 1 < 16:
                    stage2(t_ - 1)
                if t_ < 16:
                    stage1(t_)
            return boc, bdc

        def sample_gla_state():
            bog = nf()
            pin(bog)
            OP("dve", lambda e: e.tensor_copy(ElS.t[:], Eq.t[:].rearrange("p j (i r) -> p j i r", r=8)[:, :, :, 7]), r=[Eq], w=[ElS])

            def stage1(i):
                sl = i % 2; s4 = i % 4
                DMA("sp", S0s[s4].t[:], st0_d[i].rearrange("(j hl) k v -> (hl k) j v", hl=2), sem_s0[s4], w=[S0s[s4]])
                OP("dve", lambda e: e.tensor_copy(S0b[s4].t[0:64, :, 0, :], S0s[s4].t[0:64, :, :]), r=[S0s[s4]], w=[S0b[s4]])
                OP("act", lambda e: e.copy(S0b[s4].t[64:128, :, 1, :], S0s[s4].t[64:128, :, :]), r=[S0s[s4]], w=[S0b[s4]])
                OP("dve", lambda e: e.tensor_scalar_mul(kdm[sl].t[:], kd.t[:], OH.t[:, i:i + 1]), r=[kd, OH], w=[kdm[sl]])

            def stage2(i):
                sl = i % 2; s4 = i % 4
                for hh in range(4):
                    j, hl = hh // 2, hh % 2
                    OP("pe", lambda e, hh=hh, j=j, hl=hl: e.matmul(bog.t[:, 128 * hh + 8 * i:128 * hh + 8 * (i + 1)],
                                                                   S0b[s4].t[:, j, hl, :], qe.t[:, j, 8 * i:8 * (i + 1)],
                                                                   start=True, stop=True), r=[S0b[s4], qe], w=[bog])
                bs = nf()
                for j in range(2):
                    OP("pe", lambda e, j=j: e.matmul(bs.t[:, 256 * j:256 * (j + 1)], kdm[sl].t[:, 128 * j:128 * (j + 1)],
                                                     Vg.t[:, 256 * j:256 * (j + 1)], start=True, stop=True),
                       r=[kdm[sl], VgH[j]], w=[bs])
                for j in range(2):
                    for hl in range(2):
                        OP("dve", lambda e, j=j, hl=hl: e.scalar_tensor_tensor(
                            out=SN[s4].t[64 * hl:64 * (hl + 1), j, :], in0=S0s[s4].t[64 * hl:64 * (hl + 1), j, :],
                            scalar=ElS.t[64 * hl:64 * (hl + 1), j, i:i + 1],
                            in1=bs.t[64 * hl:64 * (hl + 1), 256 * j + 128 * hl:256 * j + 128 * (hl + 1)],
                            op0=ALU.mult, op1=ALU.add), r=[S0s[s4], ElS, bs], w=[SN[s4]])
                DMA("act", sst_o[i].rearrange("(j hl) k v -> (hl k) j v", hl=2), SN[s4].t[:], sem_sn[s4], r=[SN[s4]])

            for t_ in range(17):
                if 0 <= t_ - 1 < 16:
                    stage2(t_ - 1)
                if t_ < 16:
                    stage1(t_)
            return bog

        def pre_tile(i, xs, h, hook_a=None, hook_b=None):
            if hook_a is not None:
                hook_a()
            bkg = nf(); bvg = nf()
            proj_tm(h, KG, 512, bkg)
            proj_tm(h, VG + 256, 256, bvg)
            gla_gates(h, bkg, bvg, i, False)
            if hook_b is not None:
                hook_b()
            gla_state_update()
            if i == NPRE - 1:
                attn_kv(h, 16, False)

        chk(15)
        DMA("act", sk_o[:, 0:120, :], ck_d[:, 8:128, :], sem_o)
        DMA("act", sv_o[:, 0:120, :], cv_d[:, 8:128, :], sem_o)
        tiles = [("main", i) for i in range(n_main)]
        if do_sample:
            tiles.append(("smp", 0))

        def pre_P1(h):
            bkg = nf(); pin(bkg)
            proj_tm(h, KG, 512, bkg)
            return bkg

        def pre_P2(h):
            bvg = nf(); pin(bvg)
            proj_tm(h, VG + 256, 256, bvg)
            proj_fm(h, LR, 16, bvg, 256)
            return bvg

        def pre_G1(bvg):
            OP("act", lambda e: e.copy(LRa.t[0:16, :], bvg.t[0:16, 256:384]), r=[bvg], w=[LRa])
            bg_ = nf()
            OP("pe", lambda e: e.matmul(bg_.t[:, 0:256], LRa.t[0:17, :], WgA.t[0:17, :], start=True, stop=True),
               r=[LRa, WgA], w=[bg_])
            return bg_

        def pre_G2a(bg_):
            OP("act", lambda e: e.activation(out=Lg.t[:], in_=bg_.t[:, 0:256], func=AF.Exp, scale=-1.0), r=[bg_], w=[Lg])
            OP("act", lambda e: e.activation(out=Lg.t[:], in_=Lg.t[:], func=AF.Ln, scale=1.0, bias=1.0), r=[Lg], w=[Lg])
            OP("pe", lambda e: e.matmul(bg_.t[:, 256:512], Um1f.t[:], Lg.t[:], start=True, stop=True), r=[Um1f, Lg], w=[bg_])
            for j in range(2):
                OP("pe", lambda e, j=j: e.matmul(bg_.t[:, j:j + 1], Lg.t[:, 128 * j:128 * (j + 1)], onesf.t[:, 0:1],
                                                 start=True, stop=True), r=[Lg, onesf], w=[bg_])

        def pre_G2b(bkg, bvg, bg_, flag_col):
            OP("act", lambda e: e.activation(out=Dk.t[:], in_=bg_.t[:, 256:512], func=AF.Exp, scale=1.0 / 16), r=[bg_], w=[Dk])
            OP("act", lambda e: e.activation(out=Elast.t[:], in_=bg_.t[:, 0:2], func=AF.Exp, scale=-1.0 / 16), r=[bg_], w=[Elast])
            OP("dve", lambda e: e.tensor_scalar_mul(Vg.t[:, 0:256], bkg.t[:, 256:512], flg.t[:, flag_col:flag_col + 1]),
               r=[bkg, flg], w=[VgH[0]])
            OP("dve", lambda e: e.tensor_scalar_mul(Vg.t[:, 256:512], bvg.t[:, 0:256], flg.t[:, flag_col:flag_col + 1]),
               r=[bvg, flg], w=[VgH[1]])
            OP("dve", lambda e: e.tensor_tensor(kd.t[:], bkg.t[:, 0:256], Dk.t[:], ALU.mult), r=[bkg, Dk], w=[kd])
            unpin(bkg, bvg)

        def pre_U():
            bs = nf()
            for j in range(2):
                OP("pe", lambda e, j=j: e.matmul(bs.t[:, 256 * j:256 * (j + 1)], kd.t[:, 128 * j:128 * (j + 1)],
                                                 Vg.t[:, 256 * j:256 * (j + 1)], start=True, stop=True), r=[kd, VgH[j]], w=[bs])
            for j in range(2):
                for hl in range(2):
                    OP("dve", lambda e, j=j, hl=hl: e.scalar_tensor_tensor(
                        out=Sst.t[64 * hl:64 * (hl + 1), j, :], in0=Sst.t[64 * hl:64 * (hl + 1), j, :],
                        scalar=Elast.t[64 * hl:64 * (hl + 1), j:j + 1],
                        in1=bs.t[64 * hl:64 * (hl + 1), 256 * j + 128 * hl:256 * j + 128 * (hl + 1)],
                        op0=ALU.mult, op1=ALU.add), r=[SstQ[j][hl], Elast, bs], w=[SstQ[j][hl]])

        pre_idx = list(range(NPRE - n_pre, NPRE))
        items = [("pre", i) for i in pre_idx] + tiles[:1]
        fr = {}

        def fa(k):
            it = items[k]
            xs_, src_ = front_a(x_of(it), it[0] == "smp")
            fr[k] = {"xs": xs_, "src": src_, "smp": it[0] == "smp"}

        def fb(k):
            fr[k]["h"] = front_b(fr[k]["src"], fr[k]["smp"])

        def x_of(t):
            return {"pre": lambda: xp[t[1]], "main": lambda: xm[t[1]], "smp": lambda: xs_d}[t[0]]()

        cur_fr = {}
        if n_pre > 0:
            fa(0); fb(0)
            if len(items) > 1:
                fa(1); fb(1)
            PB = {0: (pre_P1(fr[0]["h"]), pre_P2(fr[0]["h"]))}
            for k in range(n_pre):
                bkg, bvg = PB[k]
                bg_ = pre_G1(bvg)
                if k + 2 < len(items):
                    fa(k + 2)
                nb1 = pre_P1(fr[k + 1]["h"]) if k + 1 < n_pre else None
                pre_G2a(bg_)
                nb2 = pre_P2(fr[k + 1]["h"]) if k + 1 < n_pre else None
                PB[k + 1] = (nb1, nb2)
                pre_G2b(bkg, bvg, bg_, pre_idx[k])
                if k + 2 < len(items):
                    fb(k + 2)
                pre_U()
                if k == n_pre - 1:
                    for hl in range(2):
                        OP("pool", lambda e, hl=hl: e.tensor_copy(Sbm.t[64 * hl:64 * (hl + 1), :, hl, :], Sst.t[64 * hl:64 * (hl + 1), :, :]),
                           r=[SstQ[0][hl], SstQ[1][hl]], w=[SbmH[hl]])
                    attn_kv(fr[k]["h"], 16, False)
            if len(items) > n_pre:
                cur_fr = {"xs": fr[n_pre]["xs"], "h": fr[n_pre]["h"]}

        if tiles and not cur_fr:
            xs0, src0 = front_a(x_of(tiles[0]), tiles[0][0] == "smp")
            cur_fr = {"xs": xs0, "h": front_b(src0, tiles[0][0] == "smp")}
        for ti, t in enumerate(tiles):
            nxt = tiles[ti + 1] if ti + 1 < len(tiles) else None
            if nxt is not None and nxt[0] == "smp":
                nxt = None
            if t[0] == "smp":
                o_ = st["x"] ^ 1
                xo_off = X[o_].t.manual_sbuf_range[0]
                for k_, nm in enumerate(("A", "S")):
                    tt_ = T(nc.alloc_sbuf_tensor_at("smod" + nm, [128, D], BF16, offset=xo_off + 2048 * k_), "smod" + nm)
                    tt_.b = X[o_].b
                    smod[nm] = tt_
                    DMA("sp", tt_.t[:], modx[k_], sem_mx, w=[tt_])
                DMA("sp", Gbc.t[:], modx[2], sem_mx, w=[Gbc])
                xs0, src0 = front_a(xs_d, True)
                cur_fr = {"xs": xs0, "h": front_b(src0, True)}
            nx = {}

            if nxt is not None:
                nx["xs_pre"] = front_dma(x_of(nxt))

            def hook_a(nxt=nxt, nx=nx):
                if nxt is not None:
                    nx["xs"], nx["src"] = front_a(x_of(nxt), nxt[0] == "smp", nx["xs_pre"])

            def tinfo(tt_):
                if tt_[0] == "main":
                    return tt_[1], False, tt_[1] == NMAIN - 1
                return 17, True, False

            def hook_b(nxt=nxt, nx=nx):
                if nxt is not None:
                    nx["h"] = front_b(nx["src"], nxt[0] == "smp")
                    ri, sm, la = tinfo(nxt)
                    defer_begin()
                    nx["cur"] = attn_prep(nx["h"], ri, sm, la)
                    defer_end()

            if t[0] == "main":
                i = t[1]
                full_tile(cur_fr["xs"], cur_fr["h"], ym[i], i, False, i == 0, i == NMAIN - 1, hook_a, hook_b, cur_fr.get("cur"))
            else:
                full_tile(cur_fr["xs"], cur_fr["h"], ys, 17, True, False, False, hook_a, hook_b, cur_fr.get("cur"))
            defer_flush()
            cur_fr = nx
        for hl in range(2):
            dst = pst_o.rearrange("(j hl) k v -> hl k j v", hl=2)[hl]
            DMA("sp", dst, Sst.t[64 * hl:64 * (hl + 1), :, :], sem_o, r=[SstQ[0][hl], SstQ[1][hl]])

    except _Stop:
        pass
    S.wait_all("sp")
    with nc.allow_low_precision("bf16 matmul operands, fp32 accumulation"):
        S.emit()
    es.close()
    return nc


_NC_CACHE = {}


def _consts():
    idx = np.arange(128)
    U = (idx[:, None] <= idx[None, :]).astype(np.float32)
    L = (idx[:, None] >= idx[None, :]).astype(np.float32)
    same = (idx[:, None] // 8 == idx[None, :] // 8).astype(np.float32)
    Ub = U * same
    Um1 = U - 1.0
    Ubm1 = (Ub - same).astype(np.float32)
    I = np.eye(128, dtype=np.float32)
    SelS = np.zeros((17, 128), np.float32)
    for t in range(128):
        SelS[1 + t // 8, t] = 1.0
    SelP = np.zeros((17, 128), np.float32)
    SelP[0, :] = 1.0
    OH = (idx[:, None] // 8 == np.arange(16)[None, :]).astype(np.float32)
    C = (idx[:, None] >= np.arange(8)[None, :]).astype(np.float32)
    ones = np.ones((128, 128), np.float32)
    return dict(cU=U, cL=L, cUb=Ub, cUm1=Um1, cUbm1=Ubm1, cI=I, cSelS=SelS, cSelP=SelP, cOH=OH, cC=C, cOnes=ones)


def _rope_table(pos):
    half = 32
    inv = (1.0 / (np.float32(10000.0) ** (np.arange(half, dtype=np.float32) / np.float32(half)))).astype(np.float32)
    ang = pos.astype(np.float32)[:, None] * inv[None, :]
    return np.concatenate([np.cos(ang), np.sin(ang)], axis=-1).astype(np.float32)


def kernel(x_prompt, x_sample, cache_win_k, cache_win_v, state_gla, c_prompt, c_sample,
           norm_g, w_ada, b_ada, w_in, q_norm_g, k_norm_g, attn_sinks, w_gla_gate, b_gla_gate,
           gla_norm_g, w_branch_att, w_branch_gla, w_out):
    f = lambda a: np.ascontiguousarray(np.asarray(a, dtype=np.float32))
    x_prompt, x_sample = f(x_prompt), f(x_sample)
    ck, cv, st0 = f(cache_win_k)[0], f(cache_win_v)[0], f(state_gla)[0]
    c_prompt, c_sample = f(c_prompt), f(c_sample)
    if "nc" not in _NC_CACHE:
        _NC_CACHE["nc"] = build_nc()
    nc = _NC_CACHE["nc"]
    consts = _consts()
    shared = dict(
        normg=f(np.asarray(norm_g)[0].reshape(8, 128).T), normrow=f(np.asarray(norm_g)[0].reshape(1, D)),
        w_ada=f(w_ada)[0], b_ada=f(b_ada)[0].reshape(1, 3072), w_in=f(w_in)[0],
        gqk=f(np.broadcast_to(np.concatenate([np.tile(np.asarray(q_norm_g)[0], 8), np.tile(np.asarray(k_norm_g)[0], 2)])[None, :], (128, 640))),
        sinks=f(np.repeat(np.asarray(attn_sinks)[0], 128).reshape(1, 1024)),
        w_gla_gate=f(w_gla_gate)[0], b_gla_gate=f(b_gla_gate)[0].reshape(1, 256),
        gdv=f(np.asarray(gla_norm_g)[0].reshape(128, 1)),
        w_branch_att=f(w_branch_att)[0], w_branch_gla=f(w_branch_gla)[0], w_out=f(w_out)[0], **consts)
    in_maps = []
    for c in range(NCORES):
        b, p = c // 4, c % 4
        seq = x_prompt[b].reshape(64, 128, D)
        xm = seq[16 * p:16 * (p + 1)]
        xp = np.zeros((NPRE, 128, D), np.float32)
        npv = 16 * p
        if npv:
            xp[NPRE - npv:] = seq[0:npv]
        flags = np.zeros((128, 64), np.float32)
        flags[:, NPRE - npv:NPRE] = 1.0
        flags[:, 48] = 1.0 if p > 0 else 0.0
        rope = np.zeros((18, 128, 64), np.float32)
        for i in range(16):
            rope[i] = _rope_table(2048 * p + 128 * i + np.arange(128))
        rope[16] = _rope_table(np.maximum(2048 * p - 128 + np.arange(128), 0))
        rope[17] = _rope_table(16384 + (np.arange(128) % 8))
        sl = slice(16 * c, 16 * (c + 1))
        m = dict(xm=np.ascontiguousarray(xm), xp=xp, xs=np.ascontiguousarray(x_sample[sl].reshape(128, D)),
                 cvec=np.ascontiguousarray(np.concatenate([c_prompt[b:b + 1], c_sample[sl]], 0)),
                 flags=flags, rope=rope,
                 ck=np.ascontiguousarray(ck[sl].reshape(16, 128, 128)), cv=np.ascontiguousarray(cv[sl].reshape(16, 128, 128)),
                 st0=np.ascontiguousarray(st0[sl]))
        m.update(shared)
        in_maps.append(m)
    res = run_bass_kernel_spmd(nc, in_maps, core_ids=list(range(NCORES)))
    R = res.results
    _NC_CACHE['last'] = R
    y_prompt = np.zeros((2, 8192, D), np.float32)
    for c in range(NCORES):
        b, p = c // 4, c % 4
        y_prompt[b, 2048 * p:2048 * (p + 1)] = R[c]["ym"].reshape(2048, D)
    y_sample = np.concatenate([R[c]["ys"].reshape(16, 8, D) for c in range(NCORES)], 0)
    pk = np.stack([R[3]["pk"], R[7]["pk"]], 0).reshape(1, 2, 128, 2, 64)
    pv = np.stack([R[3]["pv"], R[7]["pv"]], 0).reshape(1, 2, 128, 2, 64)
    pst = np.stack([R[3]["pst"], R[7]["pst"]], 0).reshape(1, 2, 4, 64, 128)
    sk = np.concatenate([R[c]["sk"] for c in range(NCORES)], 0).reshape(1, 128, 128, 2, 64)
    sv = np.concatenate([R[c]["sv"] for c in range(NCORES)], 0).reshape(1, 128, 128, 2, 64)
    sst = np.concatenate([R[c]["sst"] for c in range(NCORES)], 0).reshape(1, 128, 4, 64, 128)
    return (y_prompt, y_sample, pk.astype(np.float32), pv.astype(np.float32), pst.astype(np.float32),
            sk.astype(np.float32), sv.astype(np.float32), sst.astype(np.float32))
```
